# Optimizing a Trainium2 kernel written in Bass

```python
import math
import jax
import jax.numpy as jnp
from jax import lax
import numpy as np

D_MODEL = 2048
BATCH = 4
SEQ = 2048
DEPTH = 2
DEC_BATCH = 128
DEC_SEQ = 8
PAST_LEN = 16384
PAGE_SIZE = 128

N_AB = (DEPTH + 1) // 2
N_C = DEPTH // 2

GDN_HEADS = 8
GDN_DK = D_MODEL // 16
GDN_DV = D_MODEL // 16
GDN_CONV = 4
GDN_CHUNK = 64
RET_HEADS = 4
RET_DK = D_MODEL // 16
RET_DV = D_MODEL // 8
RET_CHUNK = 64
ROPE_BASE = 10000.0
S5_GROUP = 16
S5_GROUPS = D_MODEL // S5_GROUP
S5_P = 64
D_FF = ((8 * D_MODEL // 3 + 255) // 256) * 256
FFN_CONV = 3
EPS = 1e-6

GDN_QK_W = GDN_HEADS * GDN_DK
GDN_V_W = GDN_HEADS * GDN_DV
GDN_CONV_W = 2 * GDN_QK_W + GDN_V_W
RET_QK_W = RET_HEADS * RET_DK
RET_V_W = RET_HEADS * RET_DV
IN_SIZES = (GDN_CONV_W, GDN_V_W, GDN_HEADS, GDN_HEADS, RET_QK_W, RET_QK_W, RET_V_W, RET_V_W)
IN_SPLITS = tuple(int(s) for s in np.cumsum(IN_SIZES)[:-1])
D_IN = sum(IN_SIZES)
MIX_W = GDN_V_W + RET_V_W

kernel_name = 'hybrid_gdn_retention_s5_convffn_step'


def rmsnorm(x, w):
    xf = x.astype(jnp.float32)
    y = xf * lax.rsqrt(jnp.mean(xf * xf, axis=-1, keepdims=True) + EPS)
    return (y * w.astype(jnp.float32)).astype(x.dtype)


def l2norm(x):
    return x * lax.rsqrt(jnp.sum(x * x, axis=-1, keepdims=True) + EPS)


def causal_dwconv(x, buf, w):
    k_taps = w.shape[0]
    L = x.shape[1]
    xp = jnp.concatenate([buf.astype(x.dtype), x], axis=1)
    out = xp[:, 0:L] * w[0]
    for j in range(1, k_taps):
        out = out + xp[:, j:j + L] * w[j]
    return out, xp[:, L:]


def rotary(t, pos):
    half = t.shape[-1] // 2
    inv = ROPE_BASE ** (-jnp.arange(half, dtype=jnp.float32) / half)
    ang = pos[:, None] * inv[None, :]
    cos = jnp.cos(ang)[None, :, None, :]
    sin = jnp.sin(ang)[None, :, None, :]
    t1, t2 = t[..., :half], t[..., half:]
    return jnp.concatenate([t1 * cos - t2 * sin, t1 * sin + t2 * cos], axis=-1)


def to_chunks(t, c):
    b, l = t.shape[:2]
    t = t.reshape((b, l // c, c) + t.shape[2:])
    return t.transpose((1, 0, 3, 2) + tuple(range(4, t.ndim)))


def from_chunks(t):
    n, b, h, c = t.shape[:4]
    t = t.transpose((1, 0, 3, 2) + tuple(range(4, t.ndim)))
    return t.reshape((b, n * c, h) + t.shape[4:])


def gated_delta_chunked(q, k, v, log_a, beta, s0):
    L = q.shape[1]
    dv = v.shape[-1]
    c = math.gcd(L, GDN_CHUNK)
    qc, kc, vc = to_chunks(q, c), to_chunks(k, c), to_chunks(v, c)
    g = jnp.cumsum(to_chunks(log_a, c), axis=-1)
    bc = to_chunks(beta, c)
    incl = jnp.tril(jnp.ones((c, c), dtype=bool))
    strict = jnp.tril(jnp.ones((c, c), dtype=bool), -1)
    diff = g[..., :, None] - g[..., None, :]
    decay = jnp.where(incl, jnp.exp(jnp.where(incl, diff, 0.0)), 0.0)
    kk = jnp.einsum('nbhid,nbhjd->nbhij', kc, kc)
    a_mat = jnp.where(strict, bc[..., :, None] * kk * decay, 0.0) + jnp.eye(c, dtype=q.dtype)
    rhs = jnp.concatenate([vc * bc[..., None], kc * (bc * jnp.exp(g))[..., None]], axis=-1)
    sol = lax.linalg.triangular_solve(a_mat, rhs, left_side=True, lower=True, unit_diagonal=True)
    u, w = sol[..., :dv], sol[..., dv:]
    qk = jnp.einsum('nbhid,nbhjd->nbhij', qc, kc) * decay
    q_dec = qc * jnp.exp(g)[..., None]
    k_dec = kc * jnp.exp(g[..., -1:] - g)[..., None]
    g_last = jnp.exp(g[..., -1])

    def step(s, inp):
        u_n, w_n, qk_n, qd_n, kd_n, gl_n = inp
        v_new = u_n - jnp.einsum('bhcd,bhde->bhce', w_n, s)
        o = jnp.einsum('bhcd,bhde->bhce', qd_n, s) + jnp.einsum('bhij,bhje->bhie', qk_n, v_new)
        s = s * gl_n[..., None, None] + jnp.einsum('bhcd,bhce->bhde', kd_n, v_new)
        return s, o

    s_fin, o = lax.scan(step, s0, (u, w, qk, q_dec, k_dec, g_last))
    return from_chunks(o), s_fin


def retention_chunked(q, k, v, s0):
    L = q.shape[1]
    c = math.gcd(L, RET_CHUNK)
    log_g = jnp.log1p(-jnp.exp2(-5.0 - jnp.arange(RET_HEADS, dtype=jnp.float32)))
    qc, kc, vc = to_chunks(q, c), to_chunks(k, c), to_chunks(v, c)
    idx = jnp.arange(c, dtype=jnp.float32)
    diff = idx[:, None] - idx[None, :]
    causal = diff >= 0
    decay = jnp.where(causal, jnp.exp(log_g[:, None, None] * jnp.where(causal, diff, 0.0)), 0.0)
    q_dec = jnp.exp(log_g[:, None] * (idx + 1.0))[..., None]
    k_dec = jnp.exp(log_g[:, None] * (c - 1.0 - idx))[..., None]
    g_chunk = jnp.exp(log_g * c)[:, None, None]
    o = jnp.einsum('nbhij,nbhje->nbhie', jnp.einsum('nbhid,nbhjd->nbhij', qc, kc) * decay, vc)
    kv = jnp.einsum('nbhjd,nbhje->nbhde', kc * k_dec, vc)

    def step(s, kv_n):
        return s * g_chunk + kv_n, s

    s_fin, s_prev = lax.scan(step, s0, kv)
    o = o + jnp.einsum('nbhid,nbhde->nbhie', qc * q_dec, s_prev)
    return from_chunks(o), s_fin


def ab_mixer(h, pos, gdn_s, gdn_cb, ret_s, w_in, gdn_conv_w, gdn_a_log, gdn_dt_bias,
             gdn_norm_w, ret_gn_w, ret_gn_b, w_out):
    f32 = jnp.float32
    bn, L, _ = h.shape
    proj = h @ w_in
    qkv_a, z_a, a_a, b_a, q_b, k_b, v_b, g_b = jnp.split(proj, IN_SPLITS, axis=-1)
    qkv_a, new_cb = causal_dwconv(qkv_a, gdn_cb, gdn_conv_w)
    qkv_a = jax.nn.silu(qkv_a.astype(f32))
    q_a, k_a, v_a = jnp.split(qkv_a, [GDN_QK_W, 2 * GDN_QK_W], axis=-1)
    q_a = l2norm(q_a.reshape(bn, L, GDN_HEADS, GDN_DK)) * (GDN_DK ** -0.5)
    k_a = l2norm(k_a.reshape(bn, L, GDN_HEADS, GDN_DK))
    v_a = v_a.reshape(bn, L, GDN_HEADS, GDN_DV)
    log_alpha = -jnp.exp(gdn_a_log.astype(f32)) * jax.nn.softplus(a_a.astype(f32) + gdn_dt_bias.astype(f32))
    beta = jax.nn.sigmoid(b_a.astype(f32))
    o_a, s_a = gated_delta_chunked(q_a, k_a, v_a, log_alpha, beta, gdn_s.astype(f32))
    o_a = (o_a * lax.rsqrt(jnp.mean(o_a * o_a, axis=-1, keepdims=True) + EPS) * gdn_norm_w.astype(f32)
           * jax.nn.silu(z_a.astype(f32).reshape(bn, L, GDN_HEADS, GDN_DV)))
    qb = rotary(q_b.astype(f32).reshape(bn, L, RET_HEADS, RET_DK), pos)
    kb = rotary(k_b.astype(f32).reshape(bn, L, RET_HEADS, RET_DK), pos) * (RET_DK ** -0.5)
    vb = v_b.astype(f32).reshape(bn, L, RET_HEADS, RET_DV)
    o_b, s_b = retention_chunked(qb, kb, vb, ret_s.astype(f32))
    mu = jnp.mean(o_b, axis=-1, keepdims=True)
    var = jnp.mean(jnp.square(o_b - mu), axis=-1, keepdims=True)
    o_b = ((o_b - mu) * lax.rsqrt(var + EPS)).reshape(bn, L, RET_V_W)
    o_b = (o_b * ret_gn_w.astype(f32) + ret_gn_b.astype(f32)) * jax.nn.silu(g_b.astype(f32))
    mixed = jnp.concatenate([o_a.reshape(bn, L, GDN_V_W), o_b], axis=-1).astype(h.dtype)
    return mixed @ w_out, s_a, new_cb, s_b


def s5_combine(e1, e2):
    a1r, a1i, b1r, b1i = e1
    a2r, a2i, b2r, b2i = e2
    return (a2r * a1r - a2i * a1i, a2r * a1i + a2i * a1r,
            a2r * b1r - a2i * b1i + b2r, a2r * b1i + a2i * b1r + b2i)


def s5_mixer(h, h0_re, h0_im, lam_re, lam_im, log_dt, b_re, b_im, c_re, c_im, d, w_glu):
    f32 = jnp.float32
    u = h.astype(f32)
    bn, L, _ = u.shape
    ug = u.reshape(bn, L, S5_GROUPS, S5_GROUP)
    lam_re = lam_re.astype(f32)
    lam_im = lam_im.astype(f32)
    dt = jnp.exp(log_dt.astype(f32))[:, None]
    mag = jnp.exp(lam_re * dt)
    ph = lam_im * dt
    ab_re, ab_im = mag * jnp.cos(ph), mag * jnp.sin(ph)
    den = lam_re * lam_re + lam_im * lam_im
    cf_re = ((ab_re - 1.0) * lam_re + ab_im * lam_im) / den
    cf_im = (ab_im * lam_re - (ab_re - 1.0) * lam_im) / den
    b_re = b_re.astype(f32)
    b_im = b_im.astype(f32)
    bb_re = cf_re[..., None] * b_re - cf_im[..., None] * b_im
    bb_im = cf_re[..., None] * b_im + cf_im[..., None] * b_re
    bu_re = jnp.einsum('blgc,gpc->blgp', ug, bb_re)
    bu_im = jnp.einsum('blgc,gpc->blgp', ug, bb_im)
    h0_re = h0_re.astype(f32)
    h0_im = h0_im.astype(f32)
    bu_re = bu_re.at[:, 0].add(ab_re * h0_re - ab_im * h0_im)
    bu_im = bu_im.at[:, 0].add(ab_re * h0_im + ab_im * h0_re)
    a_re = jnp.broadcast_to(ab_re, bu_re.shape)
    a_im = jnp.broadcast_to(ab_im, bu_im.shape)
    _, _, x_re, x_im = lax.associative_scan(s5_combine, (a_re, a_im, bu_re, bu_im), axis=1)
    y = (jnp.einsum('blgp,gcp->blgc', x_re, c_re.astype(f32))
         - jnp.einsum('blgp,gcp->blgc', x_im, c_im.astype(f32)))
    y = y.reshape(bn, L, D_MODEL) + d.astype(f32) * u
    g = jax.nn.gelu(y).astype(h.dtype)
    val, gate = jnp.split(g @ w_glu, 2, axis=-1)
    return val * jax.nn.sigmoid(gate), x_re[:, -1], x_im[:, -1]


def conv_ffn(h, buf, w_up, conv_w, conv_b, w_down):
    up = h @ w_up
    up, new_buf = causal_dwconv(up, buf, conv_w)
    up = up + conv_b
    val, gate = jnp.split(up, 2, axis=-1)
    return (jax.nn.silu(gate) * val) @ w_down, new_buf


def trunk(x, pos, gdn_s, gdn_cb, ret_s, s5_re, s5_im, ffn_cb, p):
    gdn_new, gcb_new, ret_new, s5r_new, s5i_new, fcb_new = [], [], [], [], [], []
    for layer in range(DEPTH):
        i = layer // 2
        h = rmsnorm(x, p['norm_mix_w'][layer])
        if layer % 2 == 0:
            mix, s_a, cb_a, s_b = ab_mixer(h, pos, gdn_s[i], gdn_cb[i], ret_s[i], p['w_in'][i],
                                           p['gdn_conv_w'][i], p['gdn_a_log'][i], p['gdn_dt_bias'][i],
                                           p['gdn_norm_w'][i], p['ret_gn_w'][i], p['ret_gn_b'][i],
                                           p['w_out'][i])
            gdn_new.append(s_a)
            gcb_new.append(cb_a)
            ret_new.append(s_b)
        else:
            mix, hr, hi = s5_mixer(h, s5_re[i], s5_im[i], p['s5_lam_re'][i], p['s5_lam_im'][i],
                                   p['s5_log_dt'][i], p['s5_b_re'][i], p['s5_b_im'][i],
                                   p['s5_c_re'][i], p['s5_c_im'][i], p['s5_d'][i], p['w_glu'][i])
            s5r_new.append(hr)
            s5i_new.append(hi)
        x = x + mix.astype(x.dtype)
        h = rmsnorm(x, p['norm_ffn_w'][layer])
        f, cb = conv_ffn(h, ffn_cb[layer], p['w_up'][layer], p['ffn_conv_w'][layer],
                         p['ffn_conv_b'][layer], p['w_down'][layer])
        fcb_new.append(cb)
        x = x + f.astype(x.dtype)
    y = rmsnorm(x, p['norm_final_w'])
    return (y, jnp.stack(gdn_new), jnp.stack(gcb_new), jnp.stack(ret_new),
            jnp.stack(s5r_new), jnp.stack(s5i_new), jnp.stack(fcb_new))


def setup_inputs(seed: int = 0) -> dict:
    key = jax.random.key(seed)
    ks = iter(jax.random.split(key, 40))
    f32 = jnp.float32

    def nrm(shape, scale):
        return scale * jax.random.normal(next(ks), shape, f32)

    x_prompt = nrm((BATCH, SEQ, D_MODEL), 1.0)
    x_sample = nrm((DEC_BATCH, DEC_SEQ, D_MODEL), 1.0)
    state_gdn = nrm((N_AB, DEC_BATCH, GDN_HEADS, GDN_DK, GDN_DV), 0.1)
    state_gdn_conv = nrm((N_AB, DEC_BATCH, GDN_CONV - 1, GDN_CONV_W), 1.0)
    state_ret = nrm((N_AB, DEC_BATCH, RET_HEADS, RET_DK, RET_DV), 1.0)
    state_s5_re = nrm((N_C, DEC_BATCH, S5_GROUPS, S5_P), 0.3)
    state_s5_im = nrm((N_C, DEC_BATCH, S5_GROUPS, S5_P), 0.3)
    state_ffn_conv = nrm((DEPTH, DEC_BATCH, FFN_CONV - 1, 2 * D_FF), 1.0)
    norm_mix_w = 1.0 + nrm((DEPTH, D_MODEL), 0.02)
    norm_ffn_w = 1.0 + nrm((DEPTH, D_MODEL), 0.02)
    norm_final_w = 1.0 + nrm((D_MODEL,), 0.02)
    w_in = nrm((N_AB, D_MODEL, D_IN), D_MODEL ** -0.5)
    gdn_conv_w = nrm((N_AB, GDN_CONV, GDN_CONV_W), GDN_CONV ** -0.5)
    gdn_a_log = jnp.log(jax.random.uniform(next(ks), (N_AB, GDN_HEADS), f32, 1.0, 16.0))
    dt = jnp.exp(jax.random.uniform(next(ks), (N_AB, GDN_HEADS), f32, math.log(1e-3), math.log(1e-1)))
    gdn_dt_bias = dt + jnp.log(-jnp.expm1(-dt))
    gdn_norm_w = 1.0 + nrm((N_AB, GDN_DV), 0.02)
    ret_gn_w = 1.0 + nrm((N_AB, RET_V_W), 0.02)
    ret_gn_b = nrm((N_AB, RET_V_W), 0.02)
    w_out = nrm((N_AB, MIX_W, D_MODEL), MIX_W ** -0.5)
    s5_lam_re = -0.5 + nrm((N_C, S5_GROUPS, S5_P), 0.01)
    s5_lam_im = math.pi * jnp.arange(S5_P, dtype=f32) + nrm((N_C, S5_GROUPS, S5_P), 0.01)
    s5_log_dt = jax.random.uniform(next(ks), (N_C, S5_GROUPS), f32, math.log(1e-3), math.log(1e-1))
    s5_b_re = nrm((N_C, S5_GROUPS, S5_P, S5_GROUP), (2 * S5_GROUP) ** -0.5)
    s5_b_im = nrm((N_C, S5_GROUPS, S5_P, S5_GROUP), (2 * S5_GROUP) ** -0.5)
    s5_c_re = nrm((N_C, S5_GROUPS, S5_GROUP, S5_P), S5_P ** -0.5)
    s5_c_im = nrm((N_C, S5_GROUPS, S5_GROUP, S5_P), S5_P ** -0.5)
    s5_d = nrm((N_C, D_MODEL), 1.0)
    w_glu = nrm((N_C, D_MODEL, 2 * D_MODEL), D_MODEL ** -0.5)
    w_up = nrm((DEPTH, D_MODEL, 2 * D_FF), D_MODEL ** -0.5)
    ffn_conv_w = nrm((DEPTH, FFN_CONV, 2 * D_FF), FFN_CONV ** -0.5)
    ffn_conv_b = nrm((DEPTH, 2 * D_FF), 0.01)
    w_down = nrm((DEPTH, D_FF, D_MODEL), D_FF ** -0.5)
    return {'x_prompt': x_prompt, 'x_sample': x_sample,
            'state_gdn': state_gdn, 'state_gdn_conv': state_gdn_conv, 'state_ret': state_ret,
            'state_s5_re': state_s5_re, 'state_s5_im': state_s5_im, 'state_ffn_conv': state_ffn_conv,
            'norm_mix_w': norm_mix_w, 'norm_ffn_w': norm_ffn_w, 'norm_final_w': norm_final_w,
            'w_in': w_in, 'gdn_conv_w': gdn_conv_w, 'gdn_a_log': gdn_a_log, 'gdn_dt_bias': gdn_dt_bias,
            'gdn_norm_w': gdn_norm_w, 'ret_gn_w': ret_gn_w, 'ret_gn_b': ret_gn_b, 'w_out': w_out,
            's5_lam_re': s5_lam_re, 's5_lam_im': s5_lam_im, 's5_log_dt': s5_log_dt,
            's5_b_re': s5_b_re, 's5_b_im': s5_b_im, 's5_c_re': s5_c_re, 's5_c_im': s5_c_im,
            's5_d': s5_d, 'w_glu': w_glu,
            'w_up': w_up, 'ffn_conv_w': ffn_conv_w, 'ffn_conv_b': ffn_conv_b, 'w_down': w_down}


def reference(x_prompt, x_sample, state_gdn, state_gdn_conv, state_ret, state_s5_re, state_s5_im,
              state_ffn_conv, norm_mix_w, norm_ffn_w, norm_final_w,
              w_in, gdn_conv_w, gdn_a_log, gdn_dt_bias, gdn_norm_w, ret_gn_w, ret_gn_b, w_out,
              s5_lam_re, s5_lam_im, s5_log_dt, s5_b_re, s5_b_im, s5_c_re, s5_c_im, s5_d, w_glu,
              w_up, ffn_conv_w, ffn_conv_b, w_down):
    f32 = jnp.float32
    p = dict(norm_mix_w=norm_mix_w, norm_ffn_w=norm_ffn_w, norm_final_w=norm_final_w,
             w_in=w_in, gdn_conv_w=gdn_conv_w, gdn_a_log=gdn_a_log, gdn_dt_bias=gdn_dt_bias,
             gdn_norm_w=gdn_norm_w, ret_gn_w=ret_gn_w, ret_gn_b=ret_gn_b, w_out=w_out,
             s5_lam_re=s5_lam_re, s5_lam_im=s5_lam_im, s5_log_dt=s5_log_dt,
             s5_b_re=s5_b_re, s5_b_im=s5_b_im, s5_c_re=s5_c_re, s5_c_im=s5_c_im,
             s5_d=s5_d, w_glu=w_glu, w_up=w_up, ffn_conv_w=ffn_conv_w,
             ffn_conv_b=ffn_conv_b, w_down=w_down)
    bp = x_prompt.shape[0]
    z_gdn = jnp.zeros((N_AB, bp, GDN_HEADS, GDN_DK, GDN_DV), f32)
    z_gcb = jnp.zeros((N_AB, bp, GDN_CONV - 1, GDN_CONV_W), x_prompt.dtype)
    z_ret = jnp.zeros((N_AB, bp, RET_HEADS, RET_DK, RET_DV), f32)
    z_s5 = jnp.zeros((N_C, bp, S5_GROUPS, S5_P), f32)
    z_fcb = jnp.zeros((DEPTH, bp, FFN_CONV - 1, 2 * D_FF), x_prompt.dtype)
    pos_p = jnp.arange(x_prompt.shape[1], dtype=f32)
    pos_s = PAST_LEN + jnp.arange(x_sample.shape[1], dtype=f32)
    y_prompt, gdn_p, gcb_p, ret_p, s5r_p, s5i_p, fcb_p = trunk(
        x_prompt, pos_p, z_gdn, z_gcb, z_ret, z_s5, z_s5, z_fcb, p)
    y_sample, gdn_s, gcb_s, ret_s, s5r_s, s5i_s, fcb_s = trunk(
        x_sample, pos_s, state_gdn, state_gdn_conv, state_ret, state_s5_re, state_s5_im,
        state_ffn_conv, p)
    return (y_prompt, y_sample, gdn_p, gdn_s, gcb_p, gcb_s, ret_p, ret_s,
            s5r_p, s5r_s, s5i_p, s5i_s, fcb_p, fcb_s)
```

```python
import numpy as np
import ml_dtypes
from concourse.bass_utils import run_bass_kernel_spmd
import contextlib
import concourse.bass as bass
import concourse.mybir as mybir

F32 = mybir.dt.float32
BF16 = mybir.dt.bfloat16
AF = mybir.ActivationFunctionType
ALU = mybir.AluOpType
AX = mybir.AxisListType


class DSem:
    def __init__(self, sem):
        self.sem = sem
        self.count = 0


class KB:
    def __init__(self, nc, es):
        self.nc = nc
        self.es = es
        self.E = {'pe': nc.tensor, 'act': nc.scalar, 'dve': nc.vector, 'pool': nc.gpsimd, 'sp': nc.sync}
        self.sem = {e: es.enter_context(nc.semaphore('s_' + e)) for e in self.E}
        self.cnt = {e: 0 for e in self.E}
        self.waited = {e: {} for e in self.E}
        self.res = {}
        self.nds = 0
        self.nt = 0

    def sb(self, shape, dt, name=None):
        self.nt += 1
        return self.es.enter_context(self.nc.sbuf_tensor(name or f"t{self.nt}", list(shape), dt))

    def ps(self, shape, dt, name=None):
        self.nt += 1
        return self.es.enter_context(self.nc.psum_tensor(name or f"p{self.nt}", list(shape), dt))

    def dsem(self):
        self.nds += 1
        return DSem(self.es.enter_context(self.nc.semaphore(f"d{self.nds}")))

    @staticmethod
    def _ex(keys):
        out = []
        for k in keys:
            if isinstance(k, tuple) and k[0] == 'ps' and len(k) == 2:
                out.append(('ps', k[1], 0))
                out.append(('ps', k[1], 64))
            else:
                out.append(k)
        return out

    def _deps(self, reads, writes):
        reads = self._ex(reads)
        writes = self._ex(writes)
        deps = []
        for k in reads:
            st = self.res.get(k)
            if st and st['w']:
                deps.append(st['w'])
        for k in writes:
            st = self.res.get(k)
            if st:
                if st['w']:
                    deps.append(st['w'])
                deps.extend(st['r'].values())
        return deps

    def _wait(self, eng, deps):
        for (sem, val, deng) in deps:
            if deng == eng and eng == 'pe':
                continue
            key = id(sem)
            if self.waited[eng].get(key, 0) >= val:
                continue
            self.E[eng].wait_ge(sem, val)
            self.waited[eng][key] = val

    def _update(self, tok, reads, writes):
        reads = self._ex(reads)
        writes = self._ex(writes)
        for k in reads:
            st = self.res.setdefault(k, {'w': None, 'r': {}})
            st['r'][id(tok[0])] = tok
        for k in writes:
            self.res[k] = {'w': tok, 'r': {}}

    def op(self, eng, fn, reads=(), writes=()):
        self._wait(eng, self._deps(reads, writes))
        inst = fn(self.E[eng])
        self.cnt[eng] += 1
        inst.then_inc(self.sem[eng], 1)
        self._update((self.sem[eng], self.cnt[eng], eng), reads, writes)

    def dma(self, eng, out, in_, reads, writes, ds, **kw):
        self._wait(eng, self._deps(reads, writes))
        inst = self.E[eng].dma_start(out=out, in_=in_, **kw)
        ds.count += 16
        inst.then_inc(ds.sem, 16)
        self._update((ds.sem, ds.count, 'dma'), reads, writes)

    def wait_all(self, eng, keys):
        deps = []
        for k in keys:
            st = self.res.get(k)
            if st:
                if st['w']:
                    deps.append(st['w'])
                deps.extend(st['r'].values())
        self._wait(eng, deps)


def _kb_barrier(self):
    toks = []
    for e in self.E:
        if self.cnt[e] > 0:
            toks.append((self.sem[e], self.cnt[e], e))
    for ds in self.all_ds:
        if ds.count > 0:
            toks.append((ds.sem, ds.count, 'dma'))
    for e in self.E:
        self._wait(e, [t for t in toks if t[2] != e])
    self.res = {}


def _kb_dsem(self):
    self.nds += 1
    d = DSem(self.es.enter_context(self.nc.semaphore(f"d{self.nds}")))
    if not hasattr(self, 'all_ds'):
        self.all_ds = []
    self.all_ds.append(d)
    return d


KB.barrier = _kb_barrier
KB.dsem = _kb_dsem


class Arena:
    def __init__(self, kb, nbytes, name="arena"):
        self.kb = kb
        self.n32 = nbytes // 4
        self.t = kb.sb([128, self.n32], F32, name=name)
        self.off = 0

    def reset(self):
        self.off = 0

    def alloc(self, shape, dt):
        n = 1
        for s in shape[1:]:
            n *= s
        nb = n * (2 if dt == BF16 else 4)
        n32 = (nb + 3) // 4
        assert self.off + n32 <= self.n32, f"arena overflow {self.off + n32} > {self.n32}"
        ap = self.t[:, self.off:self.off + n32]
        self.off += n32
        self.peak = max(getattr(self, 'peak', 0), self.off)
        if dt == BF16:
            ap = ap.bitcast(dt)[:, :n]
        elif dt != F32:
            ap = ap.bitcast(dt)
        if len(shape) > 2:
            names = "abcdefg"[:len(shape) - 1]
            kw = {names[i]: shape[1 + i] for i in range(len(shape) - 1)}
            ap = ap.rearrange("p (" + " ".join(names) + ") -> p " + " ".join(names), **kw)
        return ap


D = 2048
DFF = 5632
KT = D // 128
FT = DFF // 128
EPS = 1e-6


class Core:
    def __init__(self, kb, ident_bf_dram, ident_f_dram):
        self.kb = kb
        nc = kb.nc
        self.banks = [kb.ps([128, 512], F32, name=f"bank{i}") for i in range(8)]
        self.R = 12
        self.wbf = [kb.sb([128, 512], BF16, name=f"wbf{i}") for i in range(self.R)]
        self.wds = [kb.dsem() for _ in range(self.R)]
        self.wi = 0
        self.ident_bf = kb.sb([128, 128], BF16, name="ident_bf")
        self.ident_f = kb.sb([128, 128], F32, name="ident_f")
        self.cds = kb.dsem()
        kb.dma('pool', self.ident_bf[:], ident_bf_dram, [], ['ident_bf'], self.cds)
        kb.dma('sp', self.ident_f[:], ident_f_dram, [], ['ident_f'], kb.dsem())

    def bank(self, i):
        return self.banks[i]

    def wload(self, src, w):
        kb = self.kb
        s = self.wi % self.R
        self.wi += 1
        kb.dma('sp', self.wbf[s][:, :w], src, [], [('wbf', s)], self.wds[s])
        return ('wbf', s), self.wbf[s]


def weight_specs(P):
    specs = [('w_in', P['w_in'], D, DIN), ('w_out', P['w_out'], D, D), ('w_glu', P['w_glu'], D, 2 * D)]
    for l in range(2):
        specs.append((f'w_up{l}', P['w_up'][l], D, 2 * DFF))
        specs.append((f'w_down{l}', P['w_down'][l], DFF, D))
    return specs


def convert_weights(core, A, nc, specs):
    kb = core.kb
    NBUF = 3
    st = [A.alloc([128, 2048], F32) for _ in range(NBUF)]
    ob = [A.alloc([128, 2048], BF16) for _ in range(NBUF)]
    ids = [kb.dsem() for _ in range(NBUF)]
    ods = [kb.dsem() for _ in range(NBUF)]
    out = {}
    i = 0
    for (nm, src, rows, cols) in specs:
        dst = nc.dram_tensor(nm + "_bf16", [rows, cols], BF16, kind="Internal").ap()
        out[nm] = dst
        for r0 in range(0, rows, 128):
            for c0 in range(0, cols, 2048):
                w = min(2048, cols - c0)
                b = i % NBUF
                i += 1
                kb.dma('sp', st[b][:, :w], src[r0:r0 + 128, c0:c0 + w], [], [('cst', b)], ids[b])
                kb.op('dve', lambda e: e.tensor_copy(out=ob[b][:, :w], in_=st[b][:, :w]), [('cst', b)], [('cob', b)])
                kb.dma('act', dst[r0:r0 + 128, c0:c0 + w], ob[b][:, :w], [('cob', b)], [], ods[b])
    return out


class LateConv:
    def __init__(self, core, nc, specs):
        self.core = core
        kb = core.kb
        self.out = {}
        self.chunks = []
        for (nm, src, rows, cols) in specs:
            dst = nc.dram_tensor(nm + "_bf16", [rows, cols], BF16, kind="Internal").ap()
            self.out[nm] = dst
            for r0 in range(0, rows, 128):
                for c0 in range(0, cols, 1024):
                    w = min(1024, cols - c0)
                    self.chunks.append((src[r0:r0 + 128, c0:c0 + w], dst[r0:r0 + 128, c0:c0 + w], w))
        self.pos = 0
        self.ids = [kb.dsem() for _ in range(2)]
        self.ods = [kb.dsem() for _ in range(2)]
        self.st = None

    def alloc(self, A):
        self.st = [A.alloc([128, 1024], F32) for _ in range(2)]
        self.ob = [A.alloc([128, 1024], BF16) for _ in range(2)]

    def emit(self, n):
        kb = self.core.kb
        for _ in range(n):
            if self.pos >= len(self.chunks):
                return
            src, dst, w = self.chunks[self.pos]
            b = self.pos % 2
            self.pos += 1
            kb.dma('sp', self.st[b][:, :w], src, [], [('lst', b)], self.ids[b])
            kb.op('act', lambda e: e.activation(out=self.ob[b][:, :w], in_=self.st[b][:, :w], func=AF.Copy), [('lst', b)], [('lob', b)])
            kb.dma('act', dst, self.ob[b][:, :w], [('lob', b)], [], self.ods[b])


def rmsnorm_T(core, X, TB, N, wnorm_dram, HT, tagx='X', tagh='HT', bank=6):
    kb = core.kb
    nc = kb.nc
    if not hasattr(core, 'nrm'):
        core.nrm = dict(
            wbc=[kb.sb([128, D], F32, name=f"wbc{i}") for i in range(2)],
            wds=[kb.dsem() for _ in range(2)],
            xs=[kb.sb([128, D], BF16, name=f"xs{i}") for i in range(2)],
            ssq=kb.sb([128, 8], F32, name="ssq"),
            rstd=kb.sb([128, 8], F32, name="rstd"),
            i=0, j=0)
    n = core.nrm
    wi = n['i'] % 2
    n['i'] += 1
    wbc = n['wbc'][wi]
    kb.dma('sp', wbc[:], wnorm_dram.partition_broadcast(128), [], [('wbc', wi)], n['wds'][wi])
    ssq, rstd = n['ssq'], n['rstd']
    for tb in range(TB):
        xs = n['xs'][n['j'] % 2]
        kxs = ('xs', n['j'] % 2)
        n['j'] += 1
        kb.op('act', lambda e: e.activation(out=xs[:], in_=X[:, tb, :], func=AF.Square, accum_out=ssq[:, tb:tb + 1]),
              reads=[(tagx, tb)], writes=[kxs, ('ssq', tb)])
        kb.op('act', lambda e: e.activation(out=rstd[:, tb:tb + 1], in_=ssq[:, tb:tb + 1], func=AF.Sqrt, scale=1.0 / D, bias=core.eps_t[:, 0:1]),
              reads=[('ssq', tb)], writes=[('rstd', tb)])
        kb.op('dve', lambda e: e.reciprocal(out=rstd[:, tb:tb + 1], in_=rstd[:, tb:tb + 1]),
              reads=[('rstd', tb)], writes=[('rstd', tb)])
        kb.op('dve', lambda e: e.scalar_tensor_tensor(out=xs[:], in0=X[:, tb, :], scalar=rstd[:, tb:tb + 1], in1=wbc[:], op0=ALU.mult, op1=ALU.mult),
              reads=[(tagx, tb), ('rstd', tb), ('wbc', wi)], writes=[kxs])
        for half in range(2):
            bk = bank + half
            pt = core.bank(bk)[:].bitcast(BF16)
            for kk in range(8):
                k = half * 8 + kk
                kb.op('pe', lambda e: e.transpose(out=pt[:, kk * 128:(kk + 1) * 128], in_=xs[:, k * 128:(k + 1) * 128], identity=core.ident_bf[:]),
                      reads=[kxs, 'ident_bf'], writes=[('ps', bk)])
            kb.op('act', lambda e: e.activation(out=HT[:, half * 8:half * 8 + 8, tb * 128:(tb + 1) * 128],
                                                in_=pt.rearrange("p (k t) -> p k t", k=8), func=AF.Copy),
                  reads=[('ps', bk)], writes=[(tagh, tb)])


def ffn_phase(core, X, TB, N, nseq, L, HT, HM, w_up, conv_w_sb, conv_b_sb, w_down, CB, U, CV, SG):
    kb = core.kb
    G = 3
    groups = []
    t = 0
    while t < FT:
        g = min(G, FT - t)
        groups.append((t, g))
        t += g
    par = 0
    for (t0, g) in groups:
        for half in range(2):
            b0 = par * 3
            Ub = U[par]
            kU = ('U', par)
            par ^= 1
            c0 = half * DFF + t0 * 128
            for k in range(KT):
                wk, wt = core.wload(w_up[k * 128:(k + 1) * 128, c0:c0 + g * 128], g * 128)
                for i in range(g):
                    kb.op('pe', lambda e: e.matmul(core.bank(b0 + i)[:, :N], lhsT=wt[:, i * 128:(i + 1) * 128], rhs=HT[:, k, :], start=(k == 0), stop=(k == KT - 1)),
                          reads=[wk] + [('HT', tb) for tb in range(TB)], writes=[('ps', b0 + i)])
            for i in range(g):
                ft = half * FT + t0 + i
                kb.op('pool', lambda e: e.tensor_copy(out=Ub[:, i, :, 0:2], in_=CB[:, ft, :, :]),
                      reads=[('CB', ft)], writes=[kU + (i,)])
                kb.op('act', lambda e: e.activation(out=Ub[:, i, :, 2:2 + L], in_=core.bank(b0 + i)[:, :N].rearrange("p (s l) -> p s l", s=nseq), func=AF.Copy),
                      reads=[('ps', b0 + i)], writes=[kU + (i,)])
                kb.op('pool', lambda e: e.tensor_copy(out=CB[:, ft, :, :], in_=Ub[:, i, :, L:L + 2]),
                      reads=[kU + (i,)], writes=[('CB', ft)])
                dst = CV if half == 0 else SG
                kd = ('CV', i) if half == 0 else ('SG', i)
                d3 = dst[:, i, :].rearrange("p (s l) -> p s l", s=nseq)
                kb.op('dve', lambda e: e.tensor_scalar(out=d3, in0=Ub[:, i, :, 0:L], scalar1=conv_w_sb[:, 0, ft:ft + 1], scalar2=conv_b_sb[:, ft:ft + 1], op0=ALU.mult, op1=ALU.add),
                      reads=[kU + (i,), 'convw'], writes=[kd])
                for j in (1, 2):
                    kb.op('dve', lambda e: e.scalar_tensor_tensor(out=d3, in0=Ub[:, i, :, j:j + L], scalar=conv_w_sb[:, j, ft:ft + 1], in1=d3, op0=ALU.mult, op1=ALU.add),
                          reads=[kU + (i,), kd, 'convw'], writes=[kd])
                if half == 1:
                    kb.op('act', lambda e: e.activation(out=SG[:, i, :], in_=SG[:, i, :], func=AF.Silu),
                          reads=[kd], writes=[kd])
                    kb.op('dve', lambda e: e.tensor_tensor(out=HM[:, t0 + i, :], in0=CV[:, i, :], in1=SG[:, i, :], op=ALU.mult),
                          reads=[('CV', i), ('SG', i)], writes=[('HM', t0 + i)])
    for cb in range(D // 512):
        b0 = (cb % 2) * 4
        for k in range(FT):
            wk, wt = core.wload(w_down[k * 128:(k + 1) * 128, cb * 512:(cb + 1) * 512], 512)
            for tb in range(TB):
                kb.op('pe', lambda e: e.matmul(core.bank(b0 + tb)[:, :512], lhsT=HM[:, k, tb * 128:(tb + 1) * 128], rhs=wt[:, :512], start=(k == 0), stop=(k == FT - 1)),
                      reads=[wk, ('HM', k)], writes=[('ps', b0 + tb)])
        for tb in range(TB):
            kb.op('dve', lambda e: e.tensor_tensor(out=X[:, tb, cb * 512:(cb + 1) * 512], in0=core.bank(b0 + tb)[:, :512], in1=X[:, tb, cb * 512:(cb + 1) * 512], op=ALU.add),
                  reads=[('ps', b0 + tb), ('X', tb)], writes=[('X', tb)])


def load_cols(core, src2d, R, dst, key, bank=7):
    kb = core.kb
    if not hasattr(core, 'lc'):
        core.lc = dict(tmp=[kb.sb([128, 128], F32, name=f"lctmp{i}") for i in range(2)], ds=[kb.dsem() for _ in range(2)], i=0)
    r0 = 0
    while r0 < R:
        rows = min(128, R - r0)
        i = core.lc['i'] % 2
        core.lc['i'] += 1
        tmp = core.lc['tmp'][i]
        kb.dma('sp', tmp[0:rows, :], src2d[r0:r0 + rows, :], [], [('lctmp', i)], core.lc['ds'][i])
        kb.op('pe', lambda e: e.transpose(out=core.bank(bank)[:, 0:rows], in_=tmp[0:rows, :], identity=core.ident_f[0:rows, 0:rows]),
              reads=[('lctmp', i), 'ident_f'], writes=[('ps', bank)])
        kb.op('dve', lambda e: e.tensor_copy(out=dst[:, r0:r0 + rows], in_=core.bank(bank)[:, 0:rows]),
              reads=[('ps', bank)], writes=[key])
        r0 += rows


GH = 8
RH = 4
DIN = 7184
C_Z = 3072
C_AB = 4096
C_QB = 4112
C_KB = 4624
C_VB = 5136
C_GB = 6160
MIXW = 2048


def _act(kb, out, in_, func, reads, writes, **kw):
    kb.op('act', lambda e: e.activation(out=out, in_=in_, func=func, **kw), reads, writes)


def _tt(kb, out, in0, in1, op, reads, writes, eng='dve'):
    kb.op(eng, lambda e: e.tensor_tensor(out=out, in0=in0, in1=in1, op=op), reads, writes)


def _ts(kb, out, in0, s1, s2, op0, op1, reads, writes, eng='dve'):
    if op1 is None:
        kb.op(eng, lambda e: e.tensor_scalar(out=out, in0=in0, scalar1=s1, scalar2=None, op0=op0), reads, writes)
    else:
        kb.op(eng, lambda e: e.tensor_scalar(out=out, in0=in0, scalar1=s1, scalar2=s2, op0=op0, op1=op1), reads, writes)


def _stt(kb, out, in0, scalar, in1, op0, op1, reads, writes):
    kb.op('dve', lambda e: e.scalar_tensor_tensor(out=out, in0=in0, scalar=scalar, in1=in1, op0=op0, op1=op1), reads, writes)


def _mm(kb, out, lhsT, rhs, reads, writes, start=True, stop=True, tp=None):
    if tp is None:
        kb.op('pe', lambda e: e.matmul(out, lhsT=lhsT, rhs=rhs, start=start, stop=stop), reads, writes)
    else:
        kb.op('pe', lambda e: e.matmul(out, lhsT=lhsT, rhs=rhs, start=start, stop=stop, tile_position=tp), reads, writes)


def _tr(kb, out, in_, ident, reads, writes):
    kb.op('pe', lambda e: e.transpose(out=out, in_=in_, identity=ident), reads, writes)


def _cp(kb, eng, out, in_, reads, writes):
    if eng == 'act':
        kb.op(eng, lambda e: e.activation(out=out, in_=in_, func=AF.Copy), reads, writes)
    else:
        kb.op(eng, lambda e: e.tensor_copy(out=out, in_=in_), reads, writes)


def bc(ap, axis, n):
    a = ap.unsqueeze(axis)
    shp = list(a.shape)
    shp[axis] = n
    return a.broadcast_to(shp)


def mixer0_setup(core, P):
    kb = core.kb
    m = {}
    m['gcw'] = kb.sb([128, 4, 24], F32, name="gcw")
    load_cols(core, P['gdn_conv_w'].rearrange("j (t p) -> (j t) p", p=128), 96, m['gcw'][:].rearrange("p j t -> p (j t)"), 'gcw')
    m['gnw'] = kb.sb([128, 1], F32, name="gnw")
    load_cols(core, P['gdn_norm_w'].rearrange("(o p) -> o p", o=1), 1, m['gnw'][:], 'gnw')
    m['rgw'] = kb.sb([128, 8], F32, name="rgw")
    m['rgb'] = kb.sb([128, 8], F32, name="rgb")
    load_cols(core, P['ret_gn_w'].rearrange("(t p) -> t p", p=128), 8, m['rgw'][:], 'rgw')
    load_cols(core, P['ret_gn_b'].rearrange("(t p) -> t p", p=128), 8, m['rgb'][:], 'rgb')
    m['negA'] = kb.sb([128, 8], F32, name="negA")
    m['dtb'] = kb.sb([128, 8], F32, name="dtb")
    kb.dma('sp', m['negA'][:], P['gdn_a_log'].partition_broadcast(128), [], ['negA'], kb.dsem())
    kb.dma('sp', m['dtb'][:], P['gdn_dt_bias'].partition_broadcast(128), [], ['dtb'], kb.dsem())
    _act(kb, m['negA'][:], m['negA'][:], AF.Exp, ['negA'], ['negA'])
    _ts(kb, m['negA'][:], m['negA'][:], -1.0, None, ALU.mult, None, ['negA'], ['negA'])
    wabf = kb.sb([128, 16, 16], F32, name="wabf")
    m['wab'] = kb.sb([128, 16, 16], BF16, name="wab")
    kb.dma('sp', wabf[:], P['w_in'][:, C_AB:C_AB + 16].rearrange("(k p) c -> p k c", p=128), [], ['wabf'], kb.dsem())
    _cp(kb, 'dve', m['wab'][:], wabf[:], ['wabf'], ['wab'])
    m['mask'] = kb.sb([128, 2, 3, 64], F32, name="mask")
    kb.dma('sp', m['mask'][:], P['c_mask'], [], ['mask'], kb.dsem())
    m['segmask'] = kb.sb([128, 8, 64], BF16, name="segmask")
    kb.dma('sp', m['segmask'][:], P['c_segmask'], [], ['segmask'], kb.dsem())
    m['rowmask'] = kb.sb([128, 8], BF16, name="rowmask")
    kb.dma('sp', m['rowmask'][:], P['c_rowmask'], [], ['rowmask'], kb.dsem())
    m['dect'] = kb.sb([128, 2, 4, 64], F32, name="dect")
    kb.dma('sp', m['dect'][:], P['c_dect'], [], ['dect'], kb.dsem())
    m['qdec'] = kb.sb([128, 2, 4, 64], F32, name="qdec")
    kb.dma('sp', m['qdec'][:], P['c_qdec'], [], ['qdec'], kb.dsem())
    m['kdec'] = kb.sb([128, 2, 4], F32, name="kdec")
    kb.dma('sp', m['kdec'][:], P['c_kdec'], [], ['kdec'], kb.dsem())
    m['onesb'] = kb.sb([128, 128], BF16, name="onesb")
    kb.op('dve', lambda e: e.memset(m['onesb'][:], 1.0), writes=['onesb'])
    m['onesf'] = kb.sb([128, 128], F32, name="onesf")
    kb.op('dve', lambda e: e.memset(m['onesf'][:], 1.0 / 256.0), writes=['onesf'])
    m['ones1f'] = kb.sb([128, 128], F32, name="ones1f")
    kb.op('dve', lambda e: e.memset(m['ones1f'][:], 1.0), writes=['ones1f'])
    m['one_t'] = kb.sb([128, 1], F32, name="one_t")
    kb.op('dve', lambda e: e.memset(m['one_t'][:], 1.0), writes=['one_t'])
    m['SG'] = kb.sb([128, 8, 128], F32, name="SGp")
    m['SR'] = kb.sb([128, 4, 256], F32, name="SRp")
    m['CBq'] = kb.sb([128, 24, 3], F32, name="CBqp")
    kb.op('dve', lambda e: e.memset(m['SG'][:], 0.0), writes=['SG'])
    kb.op('dve', lambda e: e.memset(m['SR'][:], 0.0), writes=['SR'])
    kb.op('dve', lambda e: e.memset(m['CBq'][:], 0.0), writes=['CBq'])
    m['rot_ds'] = kb.dsem()
    m['st_ds'] = [kb.dsem() for _ in range(4)]
    m['sto_ds'] = [kb.dsem() for _ in range(4)]
    m['sti'] = 0
    return m


def fm_groups():
    out = []
    for g in range(8):
        out.append((g * 384, 3, 'qkv', g * 3))
    for (c, kind) in ((C_Z, 'z'), (C_GB, 'g')):
        for (i0, n) in ((0, 3), (3, 3), (6, 2)):
            out.append((c + i0 * 128, n, kind, i0))
    return out


def mixer0_sub(core, m, A, P, HTs, cfg, mixedT, mcol0, CBq):
    kb = core.kb
    NS, nseq, L, nseg, v = cfg['NS'], cfg['nseq'], cfg['L'], cfg['nseg'], cfg['v']
    TBs = NS // 128
    w_in = P['w_in_b']
    HTk = [('HT', t) for t in range(4)]
    qT = A.alloc([128, 8, NS], BF16)
    kT = A.alloc([128, 8, NS], BF16)
    vT = A.alloc([128, 8, NS], BF16)
    zT = A.alloc([128, 8, NS], BF16)
    gT = A.alloc([128, 8, NS], BF16)
    mark1 = A.off
    QU = [A.alloc([128, 3, nseq, L + 3], F32) for _ in range(2)]
    CVQs = [A.alloc([128, 3, NS], F32) for _ in range(3)]
    SQs = [A.alloc([128, NS], BF16) for _ in range(9)]
    RSs = [A.alloc([128, NS], F32) for _ in range(9)]
    par = 0
    pendA = {}
    pendBC = {}

    def flush(d, key):
        for fn in d.pop(key, []):
            fn()
    for gidx, (c0, g, kind, i0) in enumerate(fm_groups()):
        b0 = par * 3
        Ub = QU[par]
        kU = ('QU', par)
        CVQ = CVQs[gidx % 3]
        cpar = gidx % 3
        par ^= 1
        for k in range(KT):
            wk, wt = core.wload(w_in[k * 128:(k + 1) * 128, c0:c0 + g * 128], g * 128)
            for i in range(g):
                _mm(kb, core.bank(b0 + i)[:, :NS], wt[:, i * 128:(i + 1) * 128], HTs[:, k, :], [wk] + HTk, [('ps', b0 + i)], start=(k == 0), stop=(k == KT - 1))
        tA, tBC = [], []
        for i in range(g):
            ps = core.bank(b0 + i)[:, :NS]
            pk = ('ps', b0 + i)
            if kind == 'z':
                _act(kb, zT[:, i0 + i, :], ps, AF.Silu, [pk], [('zT', i0 + i)])
                continue
            if kind == 'g':
                _act(kb, gT[:, i0 + i, :], ps, AF.Silu, [pk], [('gT', i0 + i)])
                continue
            ct = i0 + i
            h = ct % 8
            _cp(kb, 'pool', Ub[:, i, :, 0:3], CBq[:, ct, :, :], [('CBq', ct)], [kU + (i,)])
            _act(kb, Ub[:, i, :, 3:3 + L], ps.rearrange("p (s l) -> p s l", s=nseq), AF.Copy, [pk], [kU + (i,)])
            _cp(kb, 'pool', CBq[:, ct, :, :], Ub[:, i, :, L:L + 3], [kU + (i,)], [('CBq', ct)])
            d3 = CVQ[:, i, :].rearrange("p (s l) -> p s l", s=nseq)
            kd = ('CVQ', cpar, i)
            slot = cpar * 3 + i
            _ts(kb, d3, Ub[:, i, :, 0:L], m['gcw'][:, 0, ct:ct + 1], None, ALU.mult, None, [kU + (i,), 'gcw'], [kd])
            for j in (1, 2, 3):
                _stt(kb, d3, Ub[:, i, :, j:j + L], m['gcw'][:, j, ct:ct + 1], d3, ALU.mult, ALU.add, [kU + (i,), kd, 'gcw'], [kd])

            def mkA(CVQ=CVQ, i=i, kd=kd, slot=slot, ct=ct, h=h):
                SQ = SQs[slot]
                if ct >= 16:
                    _act(kb, vT[:, h, :], CVQ[:, i, :], AF.Silu, [kd], [('vT', h)])
                    return
                _act(kb, CVQ[:, i, :], CVQ[:, i, :], AF.Silu, [kd], [kd])
                _act(kb, SQ[:, :], CVQ[:, i, :], AF.Square, [kd], [('SQ', slot)])

            def mkB(slot=slot, ct=ct):
                if ct >= 16:
                    return
                lb = 6 + (slot % 4) // 2
                lc = ((slot % 4) % 2) * 256
                _mm(kb, core.bank(lb)[:, lc:lc + NS], m['onesb'][:], SQs[slot][:, :], [('SQ', slot), 'onesb'], [('ps', lb)])

            def mkC(CVQ=CVQ, i=i, kd=kd, slot=slot, ct=ct, h=h):
                if ct >= 16:
                    return
                RS = RSs[slot]
                kRS = ('RS', slot)
                lb = 6 + (slot % 4) // 2
                lc = ((slot % 4) % 2) * 256
                _act(kb, RS[:, :], core.bank(lb)[:, lc:lc + NS], AF.Sqrt, [('ps', lb)], [kRS], bias=core.eps_t[:, 0:1])
                kb.op('dve', lambda e: e.reciprocal(out=RS[:, :], in_=RS[:, :]), [kRS], [kRS])
                if ct < 8:
                    _stt(kb, qT[:, h, :], CVQ[:, i, :], 128.0 ** -0.5, RS[:, :], ALU.mult, ALU.mult, [kd, kRS], [('qT', h)])
                else:
                    _tt(kb, kT[:, h, :], CVQ[:, i, :], RS[:, :], ALU.mult, [kd, kRS], [('kT', h)])
            tA.append(mkA)
            tBC.append((mkB, mkC))
        pendA[gidx] = tA
        pendBC[gidx] = [fb for (fb, fc) in tBC] + [fc for (fb, fc) in tBC]
        flush(pendBC, gidx - 2)
        flush(pendA, gidx - 1)
    for gi2 in sorted(set(list(pendA.keys()) + list(pendBC.keys()))):
        flush(pendA, gi2)
        flush(pendBC, gi2)
    kb.barrier()
    A.off = mark1
    rq = A.alloc([128, TBs, 4, 128], BF16)
    rk = A.alloc([128, TBs, 4, 128], BF16)
    rv = A.alloc([128, TBs, 1024], BF16)
    LA = A.alloc([128, TBs, 8], F32)
    BETA = A.alloc([128, TBs, 8], F32)
    mark2 = A.off
    ROT = A.alloc([128, 4, TBs, 64], F32)
    for r in range(4):
        kb.dma('sp', ROT[:, r, :, :], P['c_rot'][r, cfg['rot0']:cfg['rot0'] + NS, :].rearrange("(tb p) f -> p tb f", p=128), [], ['ROT'], m['rot_ds'])
    TA = A.alloc([128, 4, 64], F32)
    TBt = A.alloc([128, 4, 64], F32)
    bpar = 0
    for bi, c0 in enumerate((C_QB, C_KB, C_VB, C_VB + 512)):
        b0 = bpar * 2
        bpar = (bpar + 1) % 3
        for k in range(KT):
            wk, wt = core.wload(w_in[k * 128:(k + 1) * 128, c0:c0 + 512], 512)
            for tb in range(TBs):
                _mm(kb, core.bank(b0 + tb)[:, :512], HTs[:, k, tb * 128:(tb + 1) * 128], wt[:, :512], [wk] + HTk, [('ps', b0 + tb)], start=(k == 0), stop=(k == KT - 1))
        for tb in range(TBs):
            ps = core.bank(b0 + tb)
            pk = ('ps', b0 + tb)
            if bi >= 2:
                _act(kb, rv[:, tb, (bi - 2) * 512:(bi - 1) * 512], ps[:, :512], AF.Copy, [pk], ['rv'])
                continue
            dst = rq if bi == 0 else rk
            kdst = 'rq' if bi == 0 else 'rk'
            p4 = ps[:, :512].rearrange("p (h d) -> p h d", h=4)
            t1, t2 = p4[:, :, 0:64], p4[:, :, 64:128]
            cos = bc(ROT[:, 2 * bi, tb, :], 1, 4)
            sin = bc(ROT[:, 2 * bi + 1, tb, :], 1, 4)
            _tt(kb, TA, t1, cos, ALU.mult, [pk, 'ROT'], ['TA'])
            _tt(kb, TBt, t2, sin, ALU.mult, [pk, 'ROT'], ['TB'])
            _tt(kb, dst[:, tb, :, 0:64], TA, TBt, ALU.subtract, ['TA', 'TB'], [kdst])
            _tt(kb, TA, t1, sin, ALU.mult, [pk, 'ROT'], ['TA'])
            _tt(kb, TBt, t2, cos, ALU.mult, [pk, 'ROT'], ['TB'])
            _tt(kb, dst[:, tb, :, 64:128], TA, TBt, ALU.add, ['TA', 'TB'], [kdst])
    AB = A.alloc([128, TBs, 16], F32)
    for tb in range(TBs):
        for k in range(KT):
            _mm(kb, core.bank(6)[:, tb * 16:(tb + 1) * 16], HTs[:, k, tb * 128:(tb + 1) * 128], m['wab'][:, k, :], ['wab'] + HTk, [('ps', 6)], start=(k == 0), stop=(k == KT - 1))
    _act(kb, AB[:, :, :], core.bank(6)[:, 0:TBs * 16].rearrange("p (t c) -> p t c", t=TBs), AF.Copy, [('ps', 6)], ['AB'])
    _tt(kb, LA[:, :, :], AB[:, :, 0:8], bc(m['dtb'][:, :], 1, TBs), ALU.add, ['AB', 'dtb'], ['LA'])
    _act(kb, LA[:, :, :], LA[:, :, :], AF.Exp, ['LA'], ['LA'])
    _act(kb, LA[:, :, :], LA[:, :, :], AF.Ln, ['LA'], ['LA'], bias=m['one_t'][:, 0:1])
    _tt(kb, LA[:, :, :], LA[:, :, :], bc(m['negA'][:, :], 1, TBs), ALU.mult, ['LA', 'negA'], ['LA'])
    _act(kb, BETA[:, :, :], AB[:, :, 8:16], AF.Sigmoid, ['AB'], ['BETA'])
    kb.barrier()
    A.off = mark2
    gdn_sub(core, m, A, P, cfg, qT, kT, vT, zT, LA, BETA, mixedT, mcol0)
    kb.barrier()
    A.off = mark2
    ret_sub(core, m, A, P, cfg, rq, rk, rv, gT, mixedT, mcol0)
    kb.barrier()
    A.off = mark2


def gdn_sub(core, m, A, P, cfg, qT, kT, vT, zT, LA, BETA, mixedT, mcol0):
    kb = core.kb
    NS, nseg, v = cfg['NS'], cfg['nseg'], cfg['v']
    TBs = NS // 128
    MK = m['mask']
    I_INCL, I_STRICT, I_SEG = 0, 1, 2
    idf = core.ident_f
    OTs = A.alloc([128, 8, 128], F32)
    GG = A.alloc([128, 16], F32)
    EG = A.alloc([128, 8], F32)
    EKD = A.alloc([128, 8], F32)
    BEG = A.alloc([128, 8], F32)
    GQh = A.alloc([128, 8, 64], F32)
    BQh = A.alloc([128, 8, 64], F32)
    EGQ = A.alloc([128, 8, 128], F32)
    qdT = A.alloc([128, 8, 128], BF16)
    kbg = A.alloc([128, 8, 128], BF16)
    kdt = A.alloc([128, 8, 128], BF16)
    vb = A.alloc([128, 8, 128], BF16)
    D1 = A.alloc([128, 8, 64], F32)
    DMs = A.alloc([128, 8, 64], F32)
    RHt, RBt = D1, DMs
    QKm = A.alloc([128, 8, 64], BF16)
    Uc = [A.alloc([128, 8, 64], F32) for _ in range(2)]
    UTc = [A.alloc([128, 8, 64], F32) for _ in range(2)]
    Pm = A.alloc([128, 8, 64], F32)
    Pb = A.alloc([128, 8, 64], BF16)
    U32 = A.alloc([128, 8, 128], F32)
    wT = A.alloc([128, 8, 128], BF16)
    VN = A.alloc([128, 8, 128], BF16)
    def _re(ap):
        return ap.rearrange("p h c -> p (h c)").rearrange("p (a b) -> p a b", a=4)
    SQo = _re(Pb)
    RSo = _re(Uc[0])
    TMP = _re(Uc[1])
    kSQ = [('Pb', 0), ('Pb', 64)]
    kRS = [('U', 0, 0), ('U', 0, 64)]
    kTM = [('U', 1, 0), ('U', 1, 64)]
    if nseg == 1:
        S = m['SG']
        Sb = A.alloc([128, 8, 128], BF16)
        _cp(kb, 'act', Sb, S[:, :, :], ['SG'], ['Sb'])
    else:
        Ss = [A.alloc([128, 8, 128], F32) for _ in range(2)]
        Ssb = [A.alloc([128, 8, 128], BF16) for _ in range(2)]
        wTm = A.alloc([128, 8, 64], BF16)
        qdTm = A.alloc([128, 8, 64], BF16)
        kdm = A.alloc([128, 8, 128], BF16)
        EGL = A.alloc([128, 8], F32)
    H = (0, 64)
    for tb in range(TBs):
        tc0 = tb * 128
        for pb in H:
            sl = slice(pb, pb + 64)
            _mm(kb, core.bank(7)[sl, 0:8], MK[sl, v, I_INCL, :], LA[sl, tb, :], ['mask', 'LA'], [('ps', 7)])
            _mm(kb, core.bank(7)[sl, 8:16], MK[sl, v, I_SEG, :], LA[sl, tb, :], ['mask', 'LA'], [('ps', 7)])
        _act(kb, GG[:, :], core.bank(7)[:, 0:16], AF.Copy, [('ps', 7)], ['GG'])
        _act(kb, EG[:, :], GG[:, 0:8], AF.Exp, ['GG'], ['EG'])
        _tt(kb, EKD[:, :], GG[:, 8:16], GG[:, 0:8], ALU.subtract, ['GG'], ['EKD'])
        _act(kb, EKD[:, :], EKD[:, :], AF.Exp, ['EKD'], ['EKD'])
        _tt(kb, BEG[:, :], EG[:, :], BETA[:, tb, :], ALU.mult, ['EG', 'BETA'], ['BEG'])
        for pb in H:
            sl = slice(pb, pb + 64)
            _tt(kb, RHt[sl], bc(MK[sl, v, I_INCL, :], 1, 8), bc(LA[sl, tb, :], 2, 64), ALU.mult, ['mask', 'LA'], [('D1', pb)])
            _tt(kb, RBt[sl], bc(idf[sl, pb:pb + 64], 1, 8), bc(BETA[sl, tb, :], 2, 64), ALU.mult, ['ident_f', 'BETA'], [('DMs', pb)])
        for hg in range(2):
            for hh in range(4):
                h = hg * 4 + hh
                for pb in H:
                    sl = slice(pb, pb + 64)
                    _mm(kb, core.bank(hg)[:, hh * 128 + pb:hh * 128 + pb + 64], m['ones1f'][sl, :], RHt[sl, h, :], [('D1', pb), 'ones1f'], [('ps', hg)])
                    _mm(kb, core.bank(2 + hg)[:, hh * 128 + pb:hh * 128 + pb + 64], m['ones1f'][sl, :], RBt[sl, h, :], [('DMs', pb), 'ones1f'], [('ps', 2 + hg)])
            g4 = core.bank(hg)[:, :].rearrange("p (h c) -> p h c", h=4)
            b4 = core.bank(2 + hg)[:, :].rearrange("p (h c) -> p h c", h=4)
            _act(kb, EGQ[:, hg * 4:hg * 4 + 4, :], g4, AF.Exp, [('ps', hg)], ['EGQ'])
            for pb in H:
                sl = slice(pb, pb + 64)
                _cp(kb, 'act', GQh[sl, hg * 4:hg * 4 + 4, :], g4[sl, :, pb:pb + 64], [('ps', hg)], [('GQh', pb)])
                _cp(kb, 'act', BQh[sl, hg * 4:hg * 4 + 4, :], b4[sl, :, pb:pb + 64], [('ps', 2 + hg)], [('BQh', pb)])
        _tt(kb, qdT, qT[:, :, tc0:tc0 + 128], EGQ, ALU.mult, ['qT', 'EGQ'], ['qdT'])
        ptk = core.bank(4)[:].bitcast(BF16)
        ptv = core.bank(5)[:].bitcast(BF16)
        for h in range(8):
            _tr(kb, ptk[:, h * 128:(h + 1) * 128], kT[:, h, tc0:tc0 + 128], core.ident_bf[:], ['kT', 'ident_bf'], [('ps', 4)])
            _tr(kb, ptv[:, h * 128:(h + 1) * 128], vT[:, h, tc0:tc0 + 128], core.ident_bf[:], ['vT', 'ident_bf'], [('ps', 5)])
        pk3 = ptk.rearrange("p (h d) -> p h d", h=8)
        pv3 = ptv.rearrange("p (h d) -> p h d", h=8)
        _tt(kb, kbg, pk3, bc(BEG[:, :], 2, 128), ALU.mult, [('ps', 4), 'BEG'], ['kbg'])
        _tt(kb, kdt, pk3, bc(EKD[:, :], 2, 128), ALU.mult, [('ps', 4), 'EKD'], ['kdt'])
        _tt(kb, vb, pv3, bc(BETA[:, tb, :], 2, 128), ALU.mult, [('ps', 5), 'BETA'], ['vb'])
        for pb in H:
            sl = slice(pb, pb + 64)
            _tt(kb, D1[sl], GQh[sl], bc(GG[sl, 0:8], 2, 64), ALU.subtract, [('GQh', pb), 'GG'], [('D1', pb)])
            _ts(kb, D1[sl], D1[sl], 0.0, None, ALU.min, None, [('D1', pb)], [('D1', pb)])
            _act(kb, D1[sl], D1[sl], AF.Exp, [('D1', pb)], [('D1', pb)])
            _tt(kb, DMs[sl], D1[sl], bc(MK[sl, v, I_STRICT, :], 1, 8), ALU.mult, [('D1', pb), 'mask'], [('DMs', pb)])
            _tt(kb, DMs[sl], DMs[sl], BQh[sl], ALU.mult, [('DMs', pb), ('BQh', pb)], [('DMs', pb)])
            _tt(kb, D1[sl], D1[sl], bc(MK[sl, v, I_INCL, :], 1, 8), ALU.mult, [('D1', pb), 'mask'], [('D1', pb)])
            for h in range(8):
                blk = slice(tc0 + pb, tc0 + pb + 64)
                _mm(kb, core.bank(6)[sl, h * 64:(h + 1) * 64], kT[:, h, blk], kT[:, h, blk], ['kT'], [('ps', 6, pb)])
                _mm(kb, core.bank(7)[sl, h * 64:(h + 1) * 64], kT[:, h, blk], qT[:, h, blk], ['kT', 'qT'], [('ps', 7, pb)])
            U0 = Uc[0]
            _tt(kb, U0[sl], core.bank(6)[sl, :].rearrange("p (h c) -> p h c", h=8), DMs[sl], ALU.mult, [('ps', 6, pb), ('DMs', pb)], [('U', 0, pb)])
            _tt(kb, QKm[sl], core.bank(7)[sl, :].rearrange("p (h c) -> p h c", h=8), D1[sl], ALU.mult, [('ps', 7, pb), ('D1', pb)], [('QKm', pb)])
        for pb in H:
            sl = slice(pb, pb + 64)
            for h in range(8):
                _mm(kb, core.bank(0)[sl, h * 64:(h + 1) * 64], Uc[0][sl, h, :], idf[sl, pb:pb + 64], [('U', 0, pb), 'ident_f'], [('ps', 0, pb)])
            _cp(kb, 'act', UTc[0][sl], core.bank(0)[sl, :].rearrange("p (h c) -> p h c", h=8), [('ps', 0, pb)], [('UT', 0, pb)])
            _tt(kb, Pm[sl], bc(idf[sl, pb:pb + 64], 1, 8), Uc[0][sl], ALU.subtract, ['ident_f', ('U', 0, pb)], [('P', pb)])
        cur = 0
        for step in range(5):
            nxt = cur ^ 1
            last = (step == 4)
            for pb in H:
                sl = slice(pb, pb + 64)
                for h in range(8):
                    if not last:
                        _mm(kb, core.bank(1)[sl, h * 64:(h + 1) * 64], UTc[cur][sl, h, :], Uc[cur][sl, h, :], [('UT', cur, pb), ('U', cur, pb)], [('ps', 1, pb)])
                    _mm(kb, core.bank(2)[sl, h * 64:(h + 1) * 64], Uc[cur][sl, h, :], UTc[cur][sl, h, :], [('UT', cur, pb), ('U', cur, pb)], [('ps', 2, pb)])
                if not last:
                    _cp(kb, 'act', Uc[nxt][sl], core.bank(1)[sl, :].rearrange("p (h c) -> p h c", h=8), [('ps', 1, pb)], [('U', nxt, pb)])
                _cp(kb, 'act', UTc[nxt][sl], core.bank(2)[sl, :].rearrange("p (h c) -> p h c", h=8), [('ps', 2, pb)], [('UT', nxt, pb)])
                for h in range(8):
                    _mm(kb, core.bank(3)[sl, h * 64:(h + 1) * 64], UTc[nxt][sl, h, :], Pm[sl, h, :], [('UT', nxt, pb), ('P', pb)], [('ps', 3, pb)])
                _tt(kb, Pm[sl], Pm[sl], core.bank(3)[sl, :].rearrange("p (h c) -> p h c", h=8), ALU.add, [('ps', 3, pb), ('P', pb)], [('P', pb)])
            cur = nxt
        for pb in H:
            sl = slice(pb, pb + 64)
            _cp(kb, 'act', Pb[sl], Pm[sl], [('P', pb)], [('Pb', pb)])
            for h in range(8):
                _mm(kb, core.bank(4 + h // 4)[sl, (h % 4) * 128:(h % 4 + 1) * 128], Pb[sl, h, :], vb[sl, h, :], [('Pb', pb), 'vb'], [('ps', 4 + h // 4, pb)])
            for hg in range(2):
                _cp(kb, 'act', U32[sl, hg * 4:hg * 4 + 4, :], core.bank(4 + hg)[sl, :].rearrange("p (h e) -> p h e", h=4), [('ps', 4 + hg, pb)], [('U32', pb)])
            for h in range(8):
                _mm(kb, core.bank(6 + h // 4)[:, (h % 4) * 128 + pb:(h % 4) * 128 + pb + 64], kbg[sl, h, :], Pb[sl, h, :], [('Pb', pb), 'kbg'], [('ps', 6 + h // 4)])
        for hg in range(2):
            _cp(kb, 'act', wT[:, hg * 4:hg * 4 + 4, :], core.bank(6 + hg)[:, :].rearrange("p (h c) -> p h c", h=4), [('ps', 6 + hg)], ['wT'])
        for pb in H:
            sl = slice(pb, pb + 64)
            hc = slice(pb, pb + 64)
            if nseg == 1:
                for h in range(8):
                    _mm(kb, core.bank(h // 4)[sl, (h % 4) * 128:(h % 4 + 1) * 128], wT[:, h, hc], Sb[:, h, :], ['wT', 'Sb'], [('ps', h // 4, pb)])
                for hg in range(2):
                    _tt(kb, VN[sl, hg * 4:hg * 4 + 4, :], U32[sl, hg * 4:hg * 4 + 4, :], core.bank(hg)[sl, :].rearrange("p (h e) -> p h e", h=4), ALU.subtract, [('U32', pb), ('ps', hg, pb)], [('VN', pb)])
                for h in range(8):
                    o = core.bank(2)[:, h * 64:(h + 1) * 64]
                    _mm(kb, o, Sb[:, h, :], qdT[:, h, hc], ['Sb', 'qdT'], [('ps', 2)], start=True, stop=False)
                    _mm(kb, o, VN[sl, h, :], QKm[sl, h, :], [('VN', pb), ('QKm', pb)], [('ps', 2)], start=False, stop=True)
                    _mm(kb, core.bank(4 + h // 4)[:, (h % 4) * 128:(h % 4 + 1) * 128], kdt[sl, h, :], VN[sl, h, :], ['kdt', ('VN', pb)], [('ps', 4 + h // 4)])
                _cp(kb, 'act', OTs[:, :, hc], core.bank(2)[:, :].rearrange("p (h c) -> p h c", h=8), [('ps', 2)], ['OTs'])
                egl = bc(EGQ[:, :, pb + 63], 2, 128)
                _tt(kb, S[:, :, :], S[:, :, :], egl, ALU.mult, ['SG', 'EGQ'], ['SG'])
                for hg in range(2):
                    _tt(kb, S[:, hg * 4:hg * 4 + 4, :], S[:, hg * 4:hg * 4 + 4, :], core.bank(4 + hg)[:, :].rearrange("p (h e) -> p h e", h=4), ALU.add, ['SG', ('ps', 4 + hg)], ['SG'])
                _cp(kb, 'act', Sb, S[:, :, :], ['SG'], ['Sb'])
            else:
                half = pb // 64
                sq0 = cfg['seq0'] + (tb * 2 + half) * 8
                for h in range(8):
                    si = m['sti'] % 2
                    m['sti'] += 1
                    St, Stb = Ss[si], Ssb[si]
                    kS, kSb = ('Ss', si), ('Ssb', si)
                    kb.dma('sp', St, P['state_gdn'][sq0:sq0 + 8, h, :, :].rearrange("s d e -> d s e"), [], [kS], m['st_ds'][si])
                    _cp(kb, 'act', Stb, St, [kS], [kSb])
                    _tt(kb, wTm, bc(wT[:, h, hc], 1, 8), m['segmask'][:, :, :], ALU.mult, ['wT', 'segmask'], ['wTm'])
                    _tt(kb, qdTm, bc(qdT[:, h, hc], 1, 8), m['segmask'][:, :, :], ALU.mult, ['qdT', 'segmask'], ['qdTm'])
                    _tt(kb, kdm[sl], bc(kdt[sl, h, :], 1, 8), bc(m['rowmask'][sl, :], 2, 128), ALU.mult, ['kdt', 'rowmask'], [('kdm', pb)])
                    for s in range(8):
                        _mm(kb, core.bank(0)[sl, 0:128], wTm[:, s, :], Stb[:, s, :], ['wTm', kSb], [('ps', 0, pb)], start=(s == 0), stop=(s == 7))
                    _tt(kb, VN[sl, h, :], U32[sl, h, :], core.bank(0)[sl, 0:128], ALU.subtract, [('U32', pb), ('ps', 0, pb)], [('VN', pb)])
                    o = core.bank(2)[:, h * 64:(h + 1) * 64]
                    for s in range(8):
                        _mm(kb, o, Stb[:, s, :], qdTm[:, s, :], [kSb, 'qdTm'], [('ps', 2)], start=(s == 0), stop=False)
                    _mm(kb, o, VN[sl, h, :], QKm[sl, h, :], [('VN', pb), ('QKm', pb)], [('ps', 2)], start=False, stop=True)
                    for s in range(8):
                        _mm(kb, core.bank(4 + s // 4)[:, (s % 4) * 128:(s % 4 + 1) * 128], kdm[sl, s, :], VN[sl, h, :], [('kdm', pb), ('VN', pb)], [('ps', 4 + s // 4)])
                    _cp(kb, 'dve', EGL[:, :], EGQ[:, h, pb + 7:pb + 64:8], ['EGQ'], ['EGL'])
                    _tt(kb, St, St, bc(EGL[:, :], 2, 128), ALU.mult, [kS, 'EGL'], [kS])
                    for sg in range(2):
                        _tt(kb, St[:, sg * 4:sg * 4 + 4, :], St[:, sg * 4:sg * 4 + 4, :], core.bank(4 + sg)[:, :].rearrange("p (s e) -> p s e", s=4), ALU.add, [kS, ('ps', 4 + sg)], [kS])
                    kb.dma('sp', P['o_gdn_s'][sq0:sq0 + 8, h, :, :].rearrange("s d e -> d s e"), St, [kS], [], m['sto_ds'][si])
                _cp(kb, 'act', OTs[:, :, hc], core.bank(2)[:, :].rearrange("p (h c) -> p h c", h=8), [('ps', 2)], ['OTs'])
        for hg in range(2):
            hs = slice(hg * 4, hg * 4 + 4)
            _act(kb, SQo, OTs[:, hs, :], AF.Square, ['OTs'], kSQ)
            for hh in range(4):
                _mm(kb, core.bank(3)[:, hh * 128:(hh + 1) * 128], m['onesb'][:], SQo[:, hh, :], kSQ + ['onesb'], [('ps', 3)])
            _act(kb, RSo, core.bank(3)[:, :].rearrange("p (h c) -> p h c", h=4), AF.Sqrt, [('ps', 3)], kRS, scale=1.0 / 128, bias=core.eps_t[:, 0:1])
            kb.op('dve', lambda e: e.reciprocal(out=RSo, in_=RSo), kRS, kRS)
            _stt(kb, TMP, OTs[:, hs, :], m['gnw'][:, 0:1], RSo, ALU.mult, ALU.mult, ['OTs', 'gnw'] + kRS, kTM)
            _tt(kb, mixedT[:, hs, mcol0 + tc0:mcol0 + tc0 + 128], TMP, zT[:, hs, tc0:tc0 + 128], ALU.mult, kTM + ['zT'], ['mixedT'])


def ret_sub(core, m, A, P, cfg, rq, rk, rv, gT, mixedT, mcol0):
    kb = core.kb
    NS, nseg, v = cfg['NS'], cfg['nseg'], cfg['v']
    TBs = NS // 128
    NBs = NS // 64
    lg = [np.log1p(-2.0 ** (-5.0 - h)) for h in range(4)]
    cch = 64 if nseg == 1 else 8
    gch = [float(np.exp(lg[h] * cch)) for h in range(4)]
    rqT = A.alloc([128, 4, NS], BF16)
    rkT = A.alloc([128, 4, NS], BF16)
    rqd = A.alloc([128, 4, NS], BF16)
    rkd = A.alloc([128, TBs, 4, 128], BF16)
    RQK = A.alloc([128, 4, 64], BF16)
    ORs = A.alloc([128, 8, NS], F32)
    for tb in range(TBs):
        ptq = core.bank(0)[:].bitcast(BF16)
        ptk = core.bank(1)[:].bitcast(BF16)
        for h in range(4):
            _tr(kb, ptq[:, h * 128:(h + 1) * 128], rq[:, tb, h, :], core.ident_bf[:], ['rq', 'ident_bf'], [('ps', 0)])
            _tr(kb, ptk[:, h * 128:(h + 1) * 128], rk[:, tb, h, :], core.ident_bf[:], ['rk', 'ident_bf'], [('ps', 1)])
        _cp(kb, 'act', rqT[:, :, tb * 128:(tb + 1) * 128], ptq[:, 0:512].rearrange("p (h c) -> p h c", h=4), [('ps', 0)], ['rqT'])
        _cp(kb, 'act', rkT[:, :, tb * 128:(tb + 1) * 128], ptk[:, 0:512].rearrange("p (h c) -> p h c", h=4), [('ps', 1)], ['rkT'])
    _tt(kb, rqd.rearrange("p h (b c) -> p h b c", c=64), rqT.rearrange("p h (b c) -> p h b c", c=64), bc(m['qdec'][:, v, :, :], 2, NBs), ALU.mult, ['rqT', 'qdec'], ['rqd'])
    _tt(kb, rkd, rk, bc(bc(m['kdec'][:, v, :], 1, TBs), 3, 128), ALU.mult, ['rk', 'kdec'], ['rkd'])
    if nseg == 1:
        S = m['SR']
        Sb = A.alloc([128, 4, 256], BF16)
        _cp(kb, 'act', Sb, S[:, :, :], ['SR'], ['SRb'])
    else:
        Ss = [A.alloc([128, 8, 256], F32) for _ in range(2)]
        Ssb = [A.alloc([128, 8, 256], BF16) for _ in range(2)]
        rqdm = A.alloc([128, 8, 64], BF16)
        rkdm = A.alloc([128, 8, 128], BF16)
    for tb in range(TBs):
        for pb in (0, 64):
            sl = slice(pb, pb + 64)
            blk = slice(tb * 128 + pb, tb * 128 + pb + 64)
            for h in range(4):
                _mm(kb, core.bank(0)[sl, h * 64:(h + 1) * 64], rkT[:, h, blk], rqT[:, h, blk], ['rkT', 'rqT'], [('ps', 0, pb)])
            _tt(kb, RQK[sl], core.bank(0)[sl, 0:256].rearrange("p (h c) -> p h c", h=4), m['dect'][sl, v, :, :], ALU.mult, [('ps', 0, pb), 'dect'], [('RQK', pb)])
            if nseg == 1:
                for h in range(4):
                    for et in range(2):
                        o = core.bank(1)[:, (h * 2 + et) * 64:(h * 2 + et + 1) * 64]
                        _mm(kb, o, rv[sl, tb, h * 256 + et * 128:h * 256 + (et + 1) * 128], RQK[sl, h, :], ['rv', ('RQK', pb)], [('ps', 1)], start=True, stop=False)
                        _mm(kb, o, Sb[:, h, et * 128:(et + 1) * 128], rqd[:, h, blk], ['SRb', 'rqd'], [('ps', 1)], start=False, stop=True)
                    _mm(kb, core.bank(2 + h // 2)[:, (h % 2) * 256:(h % 2 + 1) * 256], rkd[sl, tb, h, :], rv[sl, tb, h * 256:(h + 1) * 256], ['rkd', 'rv'], [('ps', 2 + h // 2)])
                _cp(kb, 'act', ORs[:, :, blk], core.bank(1)[:, :].rearrange("p (h c) -> p h c", h=8), [('ps', 1)], ['ORs'])
                for h in range(4):
                    _stt(kb, S[:, h, :], S[:, h, :], gch[h], core.bank(2 + h // 2)[:, (h % 2) * 256:(h % 2 + 1) * 256], ALU.mult, ALU.add, ['SR', ('ps', 2 + h // 2)], ['SR'])
                _cp(kb, 'act', Sb, S[:, :, :], ['SR'], ['SRb'])
            else:
                half = pb // 64
                sq0 = cfg['seq0'] + (tb * 2 + half) * 8
                for h in range(4):
                    si = m['sti'] % 2
                    m['sti'] += 1
                    St, Stb = Ss[si], Ssb[si]
                    kS, kSb = ('Rs', si), ('Rsb', si)
                    kb.dma('sp', St, P['state_ret'][sq0:sq0 + 8, h, :, :].rearrange("s d e -> d s e"), [], [kS], m['st_ds'][2 + si])
                    _cp(kb, 'act', Stb, St, [kS], [kSb])
                    _tt(kb, rqdm, bc(rqd[:, h, blk], 1, 8), m['segmask'][:, :, :], ALU.mult, ['rqd', 'segmask'], ['rqdm'])
                    _tt(kb, rkdm[sl], bc(rkd[sl, tb, h, :], 1, 8), bc(m['rowmask'][sl, :], 2, 128), ALU.mult, ['rkd', 'rowmask'], [('rkdm', pb)])
                    for et in range(2):
                        o = core.bank(1)[:, (h * 2 + et) * 64:(h * 2 + et + 1) * 64]
                        _mm(kb, o, rv[sl, tb, h * 256 + et * 128:h * 256 + (et + 1) * 128], RQK[sl, h, :], ['rv', ('RQK', pb)], [('ps', 1)], start=True, stop=False)
                        for s in range(8):
                            _mm(kb, o, Stb[:, s, et * 128:(et + 1) * 128], rqdm[:, s, :], [kSb, 'rqdm'], [('ps', 1)], start=False, stop=(s == 7))
                    for sg in range(2):
                        for s4 in range(4):
                            s = sg * 4 + s4
                            _mm(kb, core.bank(2 + s4 // 2)[:, (s4 % 2) * 256:(s4 % 2 + 1) * 256], rkdm[sl, s, :], rv[sl, tb, h * 256:(h + 1) * 256], [('rkdm', pb), 'rv'], [('ps', 2 + s4 // 2)])
                        for b2 in range(2):
                            s0 = sg * 4 + b2 * 2
                            _stt(kb, St[:, s0:s0 + 2, :], St[:, s0:s0 + 2, :], gch[h], core.bank(2 + b2)[:, :].rearrange("p (s e) -> p s e", s=2), ALU.mult, ALU.add, [kS, ('ps', 2 + b2)], [kS])
                    kb.dma('sp', P['o_ret_s'][sq0:sq0 + 8, h, :, :].rearrange("s d e -> d s e"), St, [kS], [], m['sto_ds'][2 + si])
                _cp(kb, 'act', ORs[:, :, blk], core.bank(1)[:, :].rearrange("p (h c) -> p h c", h=8), [('ps', 1)], ['ORs'])
    SQr = A.alloc([128, 2, NS], F32)
    MEAN = A.alloc([128, NS], F32)
    VAR = A.alloc([128, NS], F32)
    T1 = A.alloc([128, NS], F32)
    for h in range(4):
        _act(kb, SQr, ORs[:, 2 * h:2 * h + 2, :], AF.Square, ['ORs'], ['SQr'])
        for et in range(2):
            _mm(kb, core.bank(4)[:, 0:NS], m['onesf'][:], ORs[:, 2 * h + et, :], ['ORs', 'onesf'], [('ps', 4)], start=(et == 0), stop=(et == 1))
        for et in range(2):
            _mm(kb, core.bank(5)[:, 0:NS], m['onesf'][:], SQr[:, et, :], ['SQr', 'onesf'], [('ps', 5)], start=(et == 0), stop=(et == 1))
        _cp(kb, 'act', MEAN, core.bank(4)[:, 0:NS], [('ps', 4)], ['MEAN'])
        _act(kb, VAR, core.bank(4)[:, 0:NS], AF.Square, [('ps', 4)], ['VAR'])
        _tt(kb, VAR, core.bank(5)[:, 0:NS], VAR, ALU.subtract, [('ps', 5), 'VAR'], ['VAR'])
        _act(kb, VAR, VAR, AF.Sqrt, ['VAR'], ['VAR'], bias=core.eps_t[:, 0:1])
        kb.op('dve', lambda e: e.reciprocal(out=VAR, in_=VAR), ['VAR'], ['VAR'])
        for et in range(2):
            c = 2 * h + et
            _tt(kb, T1, ORs[:, c, :], MEAN, ALU.subtract, ['ORs', 'MEAN'], ['T1'])
            _tt(kb, T1, T1, VAR, ALU.mult, ['T1', 'VAR'], ['T1'])
            _ts(kb, T1, T1, m['rgw'][:, c:c + 1], m['rgb'][:, c:c + 1], ALU.mult, ALU.add, ['T1', 'rgw', 'rgb'], ['T1'])
            _tt(kb, mixedT[:, 8 + c, mcol0:mcol0 + NS], T1, gT[:, c, :], ALU.mult, ['T1', 'gT'], ['mixedT'])


def wout_phase(core, X, TB, mixedT, w_out):
    kb = core.kb
    for cb in range(D // 512):
        b0 = (cb % 2) * 4
        for k in range(KT):
            wk, wt = core.wload(w_out[k * 128:(k + 1) * 128, cb * 512:(cb + 1) * 512], 512)
            for tb in range(TB):
                _mm(kb, core.bank(b0 + tb)[:, :512], mixedT[:, k, tb * 128:(tb + 1) * 128], wt[:, :512], [wk, 'mixedT'], [('ps', b0 + tb)], start=(k == 0), stop=(k == KT - 1))
        for tb in range(TB):
            _tt(kb, X[:, tb, cb * 512:(cb + 1) * 512], core.bank(b0 + tb)[:, :512], X[:, tb, cb * 512:(cb + 1) * 512], ALU.add, [('ps', b0 + tb), ('X', tb)], [('X', tb)])


I32 = mybir.dt.int32
TWO_PI = 6.283179


def s5_setup(core, P, A, tbl_dram):
    kb = core.kb
    idf = core.ident_f
    s = {}
    s['sel'] = kb.sb([128, 2], F32, name="s5sel")
    kb.dma('sp', s['sel'][:], P['c_sel'], [], ['sel'], kb.dsem())
    s['iota1'] = A.alloc([128, 512], F32)
    kb.dma('sp', s['iota1'], P['c_iota1'], [], ['iota1'], kb.dsem())
    s['segst'] = kb.sb([128, 128], F32, name="s5segst")
    kb.dma('sp', s['segst'][:], P['c_segst'], [], ['segst'], kb.dsem())
    s['halfpi'] = kb.sb([128, 1], F32, name="s5halfpi")
    kb.op('dve', lambda e: e.memset(s['halfpi'][:], float(np.pi / 2)), writes=['halfpi'])
    s['dcol'] = kb.sb([128, 16], F32, name="s5d")
    load_cols(core, P['s5_d'].rearrange("(t p) -> t p", p=128), 16, s['dcol'][:], 'dcol')
    LR = A.alloc([128, 64], F32)
    LI = A.alloc([128, 64], F32)
    DT = A.alloc([128, 64], F32)
    load_cols(core, P['s5_lam_re'].rearrange("(pr gi) p -> pr (gi p)", gi=2), 64, LR, 'LR')
    load_cols(core, P['s5_lam_im'].rearrange("(pr gi) p -> pr (gi p)", gi=2), 64, LI, 'LI')
    LD2 = A.alloc([128, 2], F32)
    LDrep = A.alloc([128, 2, 64], F32)
    kb.dma('sp', LD2[0:64, :], P['s5_log_dt'].rearrange("(pr gi) -> pr gi", gi=2), [], ['LD2'], kb.dsem())
    _cp(kb, 'dve', LDrep[0:64], bc(LD2[0:64, :], 2, 64), ['LD2'], ['LDrep'])
    _tr(kb, core.bank(6)[:, 0:64], LDrep[0:64].rearrange("r g p -> r (g p)"), idf[0:64, 0:64], ['LDrep', 'ident_f'], [('ps', 6)])
    _act(kb, DT, core.bank(6)[:, 0:64], AF.Exp, [('ps', 6)], ['DT'])
    s['R'] = kb.sb([128, 64], F32, name="s5R")
    s['THK'] = kb.sb([128, 64], F32, name="s5THK")
    R, THK = s['R'], s['THK']
    TH = A.alloc([128, 64], F32)
    _tt(kb, R[:], LR, DT, ALU.mult, ['LR', 'DT'], ['R'])
    _act(kb, R[:], R[:], AF.Exp, ['R'], ['R'])
    _tt(kb, TH, LI, DT, ALU.mult, ['LI', 'DT'], ['TH'])
    _ts(kb, THK[:], TH, float(1.0 / (2 * np.pi)), None, ALU.mult, None, ['TH'], ['THK'])
    Ki = A.alloc([128, 64], I32)
    FR = A.alloc([128, 64], F32)
    AB = A.alloc([128, 64], F32)
    CS = A.alloc([128, 64], F32)
    SN = A.alloc([128, 64], F32)
    _cp(kb, 'dve', Ki, THK[:], ['THK'], ['Ki'])
    _tt(kb, FR, THK[:], Ki, ALU.subtract, ['THK', 'Ki'], ['FR'])
    _stt(kb, AB, FR, -1.0, FR, ALU.mult, ALU.max, ['FR'], ['AB'])
    _act(kb, SN, FR, AF.Sin, ['FR'], ['SN'], scale=TWO_PI)
    _act(kb, CS, AB, AF.Sin, ['AB', 'halfpi'], ['CS'], scale=-TWO_PI, bias=s['halfpi'][:, 0:1])
    ABr = A.alloc([128, 64], F32)
    ABi = A.alloc([128, 64], F32)
    _tt(kb, ABr, R[:], CS, ALU.mult, ['R', 'CS'], ['ABr'])
    _ts(kb, ABr, ABr, -1.0, None, ALU.add, None, ['ABr'], ['ABr'])
    _tt(kb, ABi, R[:], SN, ALU.mult, ['R', 'SN'], ['ABi'])
    DEN = A.alloc([128, 64], F32)
    T1 = A.alloc([128, 64], F32)
    T2 = A.alloc([128, 64], F32)
    CFr = A.alloc([128, 64], F32)
    CFi = A.alloc([128, 64], F32)
    _tt(kb, DEN, LR, LR, ALU.mult, ['LR'], ['DEN'])
    _tt(kb, T1, LI, LI, ALU.mult, ['LI'], ['T1'])
    _tt(kb, DEN, DEN, T1, ALU.add, ['DEN', 'T1'], ['DEN'])
    kb.op('dve', lambda e: e.reciprocal(out=DEN, in_=DEN), ['DEN'], ['DEN'])
    _tt(kb, T1, ABr, LR, ALU.mult, ['ABr', 'LR'], ['T1'])
    _tt(kb, T2, ABi, LI, ALU.mult, ['ABi', 'LI'], ['T2'])
    _tt(kb, T1, T1, T2, ALU.add, ['T1', 'T2'], ['T1'])
    _tt(kb, CFr, T1, DEN, ALU.mult, ['T1', 'DEN'], ['CFr'])
    _tt(kb, T1, ABi, LR, ALU.mult, ['ABi', 'LR'], ['T1'])
    _tt(kb, T2, ABr, LI, ALU.mult, ['ABr', 'LI'], ['T2'])
    _tt(kb, T1, T1, T2, ALU.subtract, ['T1', 'T2'], ['T1'])
    _tt(kb, CFi, T1, DEN, ALU.mult, ['T1', 'DEN'], ['CFi'])
    BN = A.alloc([128, 2048], F32)
    SLs = {}
    for nm, kind in (('s5_b_re', 'b'), ('s5_b_im', 'b'), ('s5_c_re', 'c'), ('s5_c_im', 'c')):
        dst = A.alloc([128, 64, 16], F32)
        SLs[nm] = dst
        if kind == 'b':
            src = P[nm].rearrange("(pr gi) p c -> pr (gi p c)", gi=2)
            v4 = BN[0:64, :].rearrange("r (gi p c) -> r gi p c", gi=2, p=64)
            kb.dma('sp', BN[0:64, :], src, [], ['BN'], kb.dsem())
        else:
            src4 = P[nm].rearrange("(pr gi) c p -> pr gi c p", gi=2)
            d4 = BN[0:64, :].rearrange("r (c gi p) -> r c gi p", c=16, gi=2)
            for gi in range(2):
                kb.dma('sp', d4[:, :, gi, :], src4[:, gi, :, :], [], ['BN'], kb.dsem())
        for c0 in (0, 8):
            for cc in range(8):
                c = c0 + cc
                in_ = v4[:, :, :, c] if kind == 'b' else BN[0:64, c * 128:(c + 1) * 128]
                kb.op('pe', lambda e: e.transpose(out=core.bank(c0 // 8)[:, cc * 64:(cc + 1) * 64], in_=in_, identity=idf[0:64, 0:64]),
                      ['BN', 'ident_f'], [('ps', c0 // 8)])
            _cp(kb, 'act', dst[:, :, c0:c0 + 8].rearrange("p pr c -> p c pr"), core.bank(c0 // 8)[:, :].rearrange("p (c pr) -> p c pr", c=8), [('ps', c0 // 8)], [nm])
    Bre, Bim, Cre, Cim = SLs['s5_b_re'], SLs['s5_b_im'], SLs['s5_c_re'], SLs['s5_c_im']
    BBr = A.alloc([128, 64, 16], F32)
    BBi = A.alloc([128, 64, 16], F32)
    TT = A.alloc([128, 64, 16], F32)
    _tt(kb, BBr, Bre, bc(CFr, 2, 16), ALU.mult, ['s5_b_re', 'CFr'], ['BBr'])
    _tt(kb, TT, Bim, bc(CFi, 2, 16), ALU.mult, ['s5_b_im', 'CFi'], ['TT'])
    _tt(kb, BBr, BBr, TT, ALU.subtract, ['BBr', 'TT'], ['BBr'])
    _tt(kb, BBi, Bim, bc(CFr, 2, 16), ALU.mult, ['s5_b_im', 'CFr'], ['BBi'])
    _tt(kb, TT, Bre, bc(CFi, 2, 16), ALU.mult, ['s5_b_re', 'CFi'], ['TT'])
    _tt(kb, BBi, BBi, TT, ALU.add, ['BBi', 'TT'], ['BBi'])
    s['WB'] = [kb.sb([128, 16, 128], BF16, name=f"s5WB{i}") for i in range(2)]
    s['WC'] = [kb.sb([128, 64, 2, 16], BF16, name=f"s5WC{i}") for i in range(3)]
    LT = A.alloc([128, 64, 2, 16], F32)
    for ri, BB in enumerate((BBr, BBi)):
        kBB = 'BBr' if ri == 0 else 'BBi'
        for gi in range(2):
            _ts(kb, LT[:, :, gi, :], BB, s['sel'][:, gi:gi + 1], None, ALU.mult, None, [kBB, 'sel'], ['LT'])
        for k4 in range(4):
            for kk in range(4):
                k = k4 * 4 + kk
                kb.op('pe', lambda e: e.transpose(out=core.bank(2 + k4 % 2)[:, kk * 128:(kk + 1) * 128], in_=LT[:, 4 * k:4 * k + 4, :, :].rearrange("p q g c -> p (q g c)"), identity=idf[:]),
                      ['LT', 'ident_f'], [('ps', 2 + k4 % 2)])
            _cp(kb, 'act', s['WB'][ri][:, k4 * 4:k4 * 4 + 4, :], core.bank(2 + k4 % 2)[:, :].rearrange("p (k n) -> p k n", k=4), [('ps', 2 + k4 % 2)], ['WB'])
    for gi in range(2):
        _ts(kb, s['WC'][0][:, :, gi, :], Cre, s['sel'][:, gi:gi + 1], None, ALU.mult, None, ['s5_c_re', 'sel'], ['WC'])
        _ts(kb, s['WC'][1][:, :, gi, :], Cim, s['sel'][:, gi:gi + 1], -1.0, ALU.mult, ALU.mult, ['s5_c_im', 'sel'], ['WC'])
        _ts(kb, s['WC'][2][:, :, gi, :], Cre, s['sel'][:, gi:gi + 1], -1.0, ALU.mult, ALU.mult, ['s5_c_re', 'sel'], ['WC'])
    KK = [A.alloc([128, 512], F32) for _ in range(2)]
    KI = [A.alloc([128, 512], I32) for _ in range(2)]
    TB2 = [A.alloc([128, 2, 512], F32) for _ in range(2)]
    tds = [kb.dsem() for _ in range(2)]
    for pr in range(64):
        i = pr % 2
        _ts(kb, KK[i], s['iota1'], THK[:, pr:pr + 1], None, ALU.mult, None, ['iota1', 'THK'], [('KK', i)])
        _cp(kb, 'dve', KI[i], KK[i], [('KK', i)], [('KI', i)])
        _tt(kb, KK[i], KK[i], KI[i], ALU.subtract, [('KK', i), ('KI', i)], [('KK', i)])
        _act(kb, TB2[i][:, 1, :], KK[i], AF.Sin, [('KK', i)], [('TB2', i)], scale=TWO_PI)
        _stt(kb, KK[i], KK[i], -1.0, KK[i], ALU.mult, ALU.max, [('KK', i)], [('KK', i)])
        _act(kb, TB2[i][:, 0, :], KK[i], AF.Sin, [('KK', i), 'halfpi'], [('TB2', i)], scale=-TWO_PI, bias=s['halfpi'][:, 0:1])
        kb.dma('sp', tbl_dram[pr], TB2[i], [('TB2', i)], ['tbl_dram'], tds[i])
    s['XC'] = kb.sb([128, 64, 2], F32, name="s5XC")
    kb.op('dve', lambda e: e.memset(s['XC'][:], 0.0), writes=['XC'])
    s['tds'] = [kb.dsem() for _ in range(2)]
    s['ti'] = 0
    return s


def gelu_tanh(kb, dst, Y, X2, keys_in, key_out):
    _act(kb, X2, Y, AF.Square, keys_in, ['gX2'])
    _ts(kb, X2, X2, 0.044715, 1.0, ALU.mult, ALU.add, ['gX2'], ['gX2'])
    _tt(kb, X2, X2, Y, ALU.mult, ['gX2'] + keys_in, ['gX2'])
    _act(kb, X2, X2, AF.Sigmoid, ['gX2'], ['gX2'], scale=1.5957691216057308)
    _tt(kb, dst, Y, X2, ALU.mult, ['gX2'] + keys_in, [key_out])


def s5_phase(core, s, A, P, X, HT, cfg, tbl_dram, XCs=None, bg=None):
    kb = core.kb
    N, nseq, L = cfg['N'], cfg['nseq'], cfg['L']
    TB = N // 128
    PB = 2 if nseq == 1 else 1
    HTk = [('HT', t) for t in range(4)]
    G = A.alloc([128, 16, N], BF16)
    TBLs = [A.alloc([128, PB, 2, L], F32) for _ in range(2)]
    nm10 = ('Cr', 'Ci', 'Zr', 'Zi', 'T1', 'T2', 'S3', 'S4')
    B = {n: A.alloc([128, PB, N], F32) for n in nm10}
    P1, P2, P3, P4 = (A.alloc([128, PB, N], BF16) for _ in range(4))
    Y = A.alloc([128, N], F32)
    X2 = A.alloc([128, N], F32)
    RSEG = A.alloc([128, PB, N], F32)
    Cr, Ci, Zr, Zi, T1, T2, S3, S4 = (B[n] for n in nm10)
    v4 = lambda ap: ap.rearrange("p a (s l) -> p a s l", s=nseq)
    v3 = lambda ap: ap.rearrange("p (s l) -> p s l", s=nseq)
    ngrp = 64 // PB
    bg_per = 0
    if bg is not None:
        bg.alloc(A)
        bg_per = (len(bg.chunks) - bg.pos + ngrp - 1) // ngrp

    def load_tbl(g):
        t_i = g % 2
        p0 = g * PB
        if L == 512:
            kb.dma('sp', TBLs[t_i].rearrange("p a r l -> p a (r l)"), tbl_dram[p0:p0 + PB].rearrange("a p r l -> p a (r l)"), ['tbl_dram'], [('TBL', t_i)], s['tds'][t_i])
        else:
            for a in range(PB):
                kb.dma('sp', TBLs[t_i][:, a, :, :], tbl_dram[p0 + a][:, :, 0:L], ['tbl_dram'], [('TBL', t_i)], s['tds'][t_i])
    load_tbl(0)
    for gidx in range(ngrp):
        pr0 = gidx * PB
        ti = gidx % 2
        TBL = TBLs[ti]
        kT = ('TBL', ti)
        if gidx + 1 < ngrp:
            load_tbl(gidx + 1)
        if bg is not None:
            bg.emit(bg_per)
        COS = bc(TBL[:, :, 0, :], 2, nseq)
        SIN = bc(TBL[:, :, 1, :], 2, nseq)
        nbk = (PB * N + 511) // 512
        for a in range(PB):
            pr = pr0 + a
            k, q = pr // 4, pr % 4
            qs = slice(32 * q, 32 * q + 32)
            col = a * N
            bre = col // 512
            bim = nbk + col // 512
            _mm(kb, core.bank(bre)[:, col % 512:col % 512 + N], s['WB'][0][qs, k, :], HT[qs, k, :], ['WB'] + HTk, [('ps', bre)], tp=(32 * q, 0))
            _mm(kb, core.bank(bim)[:, col % 512:col % 512 + N], s['WB'][1][qs, k, :], HT[qs, k, :], ['WB'] + HTk, [('ps', bim)], tp=(32 * q, 0))
        for bk in range(nbk):
            w = min(512, PB * N - bk * 512)
            _cp(kb, 'act', S3.rearrange("p a n -> p (a n)")[:, bk * 512:bk * 512 + w], core.bank(bk)[:, :w], [('ps', bk)], ['S3'])
            _cp(kb, 'act', S4.rearrange("p a n -> p (a n)")[:, bk * 512:bk * 512 + w], core.bank(nbk + bk)[:, :w], [('ps', nbk + bk)], ['S4'])
        pre, pim = v4(S3), v4(S4)
        _tt(kb, v4(T1), pre, COS, ALU.mult, ['S3', kT], ['T1'])
        _tt(kb, v4(T2), pim, SIN, ALU.mult, ['S4', kT], ['T2'])
        _tt(kb, Cr, T1, T2, ALU.add, ['T1', 'T2'], ['Cr'])
        _tt(kb, v4(T1), pim, COS, ALU.mult, ['S4', kT], ['T1'])
        _tt(kb, v4(T2), pre, SIN, ALU.mult, ['S3', kT], ['T2'])
        _tt(kb, Ci, T1, T2, ALU.subtract, ['T1', 'T2'], ['Ci'])
        for a in range(PB):
            pr = pr0 + a
            rcol = s['R'][:, pr:pr + 1]
            if nseq == 1:
                XC = s['XC']
                kb.op('dve', lambda e: e.tensor_tensor_scan(out=Zr[:, a, :], data0=rcol.broadcast_to([128, N]), data1=Cr[:, a, :], initial=XC[:, pr, 0:1], op0=ALU.mult, op1=ALU.add), ['Cr', 'R', ('XC', pr)], ['Zr'])
                kb.op('dve', lambda e: e.tensor_tensor_scan(out=Zi[:, a, :], data0=rcol.broadcast_to([128, N]), data1=Ci[:, a, :], initial=XC[:, pr, 1:2], op0=ALU.mult, op1=ALU.add), ['Ci', 'R', ('XC', pr)], ['Zi'])
            else:
                c3r, c3i = v3(Cr[:, a, :]), v3(Ci[:, a, :])
                _stt(kb, c3r[:, :, 0], XCs[:, pr, :, 0], rcol, c3r[:, :, 0], ALU.mult, ALU.add, [('XCs', pr), 'R', 'Cr'], ['Cr'])
                _stt(kb, c3i[:, :, 0], XCs[:, pr, :, 1], rcol, c3i[:, :, 0], ALU.mult, ALU.add, [('XCs', pr), 'R', 'Ci'], ['Ci'])
                _ts(kb, RSEG[:, a, :], s['segst'][:, :N], rcol, None, ALU.mult, None, ['segst', 'R'], ['RSEG'])
                kb.op('dve', lambda e: e.tensor_tensor_scan(out=Zr[:, a, :], data0=RSEG[:, a, :], data1=Cr[:, a, :], initial=0.0, op0=ALU.mult, op1=ALU.add), ['Cr', 'RSEG'], ['Zr'])
                kb.op('dve', lambda e: e.tensor_tensor_scan(out=Zi[:, a, :], data0=RSEG[:, a, :], data1=Ci[:, a, :], initial=0.0, op0=ALU.mult, op1=ALU.add), ['Ci', 'RSEG'], ['Zi'])
        _tt(kb, v4(P1), v4(Zr), COS, ALU.mult, ['Zr', kT], ['P1'])
        _tt(kb, v4(P2), v4(Zi), SIN, ALU.mult, ['Zi', kT], ['P2'])
        _tt(kb, v4(P3), v4(Zr), SIN, ALU.mult, ['Zr', kT], ['P3'])
        _tt(kb, v4(P4), v4(Zi), COS, ALU.mult, ['Zi', kT], ['P4'])
        for a in range(PB):
            pr = pr0 + a
            k, q = pr // 4, pr % 4
            qs = slice(32 * q, 32 * q + 32)
            if nseq == 1:
                xcr, xci = s['XC'][:, pr, 0:1], s['XC'][:, pr, 1:2]
                kXC = ('XC', pr)
            else:
                xcr, xci = XCs[:, pr, :, 0], XCs[:, pr, :, 1]
                kXC = ('XCs', pr)
            cl, sn = TBL[:, a, 0, L - 1:L], TBL[:, a, 1, L - 1:L]
            zrl, zil = v3(Zr[:, a, :])[:, :, L - 1], v3(Zi[:, a, :])[:, :, L - 1]
            t1 = T1[:, a, 0:nseq]
            t2 = T2[:, a, 0:nseq]
            _ts(kb, t1, zil, sn, None, ALU.mult, None, ['Zi', kT], ['T1'])
            _ts(kb, t2, zrl, sn, None, ALU.mult, None, ['Zr', kT], ['T2'])
            _stt(kb, xcr, zrl, cl, t1, ALU.mult, ALU.subtract, ['Zr', kT, 'T1'], [kXC])
            _stt(kb, xci, zil, cl, t2, ALU.mult, ALU.add, ['Zi', kT, 'T2'], [kXC])
            yb = 4 + (k % 2)
            yo = core.bank(yb)[qs, :N]
            wcr = s['WC'][0][:, pr, :, :].rearrange("p g c -> p (g c)")
            wci = s['WC'][1][:, pr, :, :].rearrange("p g c -> p (g c)")
            _mm(kb, yo, wcr, P1[:, a, :], ['WC', 'P1'], [('ps', yb)], start=True, stop=False, tp=(0, 32 * q))
            _mm(kb, yo, s['WC'][2][:, pr, :, :].rearrange("p g c -> p (g c)"), P2[:, a, :], ['WC', 'P2'], [('ps', yb)], start=False, stop=False, tp=(0, 32 * q))
            _mm(kb, yo, wci, P3[:, a, :], ['WC', 'P3'], [('ps', yb)], start=False, stop=False, tp=(0, 32 * q))
            _mm(kb, yo, wci, P4[:, a, :], ['WC', 'P4'], [('ps', yb)], start=False, stop=True, tp=(0, 32 * q))
            if q == 3:
                _stt(kb, Y, HT[:, k, :], s['dcol'][:, k:k + 1], core.bank(yb)[:, :N], ALU.mult, ALU.add, HTk + ['dcol', ('ps', yb)], ['Y'])
                gelu_tanh(kb, G[:, k, :], Y, X2, ['Y'], 'G')
    w_glu = P['w_glu']
    SGt = A.alloc([128, 512], F32)
    for cb in range(4):
        for half in range(2):
            b0 = half * 4
            c0 = half * 2048 + cb * 512
            for k in range(KT):
                wk, wt = core.wload(w_glu[k * 128:(k + 1) * 128, c0:c0 + 512], 512)
                for tb in range(TB):
                    _mm(kb, core.bank(b0 + tb)[:, :512], G[:, k, tb * 128:(tb + 1) * 128], wt[:, :512], [wk, 'G'], [('ps', b0 + tb)], start=(k == 0), stop=(k == KT - 1))
        for tb in range(TB):
            _act(kb, SGt, core.bank(4 + tb)[:, :512], AF.Sigmoid, [('ps', 4 + tb)], ['SGt'])
            _tt(kb, SGt, core.bank(tb)[:, :512], SGt, ALU.mult, [('ps', tb), 'SGt'], ['SGt'])
            _tt(kb, X[:, tb, cb * 512:(cb + 1) * 512], X[:, tb, cb * 512:(cb + 1) * 512], SGt, ALU.add, ['SGt', ('X', tb)], [('X', tb)])


BF = ml_dtypes.bfloat16
BF = ml_dtypes.bfloat16
PAST_LEN = 16384
SEQ = 2048
DEC_SEQ = 8


def make_consts():
    c = {}
    c['idb'] = np.eye(128).astype(BF)
    c['idf'] = np.eye(128, dtype=np.float32)
    mask = np.zeros((128, 2, 3, 64), np.float32)
    j = np.arange(64)[:, None]
    i = np.arange(64)[None, :]
    for v, seg in enumerate((64, 8)):
        same = (j // seg) == (i // seg)
        incl = same & (i >= j)
        strict = same & (i > j)
        for half in range(2):
            mask[half * 64:(half + 1) * 64, v, 0] = incl
            mask[half * 64:(half + 1) * 64, v, 1] = strict
            mask[half * 64:(half + 1) * 64, v, 2] = same
    c['c_mask'] = mask
    segmask = np.zeros((128, 8, 64), np.float32)
    for s in range(8):
        segmask[:, s, s * 8:(s + 1) * 8] = 1
    c['c_segmask'] = segmask.astype(BF)
    rowmask = np.zeros((128, 8), np.float32)
    for p in range(128):
        rowmask[p, (p % 64) // 8] = 1
    c['c_rowmask'] = rowmask.astype(BF)
    lg = np.log1p(-np.exp2(-5.0 - np.arange(4, dtype=np.float64)))
    dect = np.zeros((128, 2, 4, 64), np.float64)
    qdec = np.zeros((128, 2, 4, 64), np.float64)
    kdec = np.zeros((128, 2, 4), np.float64)
    for v, seg in enumerate((64, 8)):
        same = (j // seg) == (i // seg)
        incl = same & (i >= j)
        for h in range(4):
            d = np.where(incl, np.exp(lg[h] * np.where(incl, (i - j), 0)), 0.0)
            dect[0:64, v, h] = d
            dect[64:128, v, h] = d
            qdec[:, v, h, :] = np.exp(lg[h] * ((np.arange(64) % seg) + 1.0))[None, :]
            kdec[:, v, h] = np.exp(lg[h] * (seg - 1.0 - (np.arange(128) % seg)))
    c['c_dect'] = dect.astype(np.float32)
    c['c_qdec'] = qdec.astype(np.float32)
    c['c_kdec'] = kdec.astype(np.float32)
    half = 64
    inv = (np.float32(10000.0) ** (-(np.arange(half, dtype=np.float32)) / np.float32(half))).astype(np.float32)
    pos = np.concatenate([np.arange(SEQ, dtype=np.float32), np.tile(PAST_LEN + np.arange(DEC_SEQ, dtype=np.float32), 16)])
    ang = (pos[:, None] * inv[None, :]).astype(np.float32).astype(np.float64)
    rot = np.stack([np.cos(ang), np.sin(ang), np.cos(ang) * 128.0 ** -0.5, np.sin(ang) * 128.0 ** -0.5]).astype(np.float32)
    c['c_rot'] = rot
    return c


def make_consts_s5(c):
    sel = np.zeros((128, 2), np.float32)
    sel[:64, 0] = 1
    sel[64:, 1] = 1
    c['c_sel'] = sel
    c['c_iota1'] = np.tile(np.arange(1, 513, dtype=np.float32)[None, :], (128, 1))
    segst = np.ones((128, 128), np.float32)
    segst[:, ::8] = 0
    c['c_segst'] = segst
    return c


def rmsnorm_TA(core, A, X, TB, N, wnorm_dram, HT):
    kb = core.kb
    if not hasattr(core, 'nrm2'):
        core.nrm2 = dict(ssq=kb.sb([128, 8], F32, name="ssq"), rstd=kb.sb([128, 8], F32, name="rstd"), wds=kb.dsem())
    n = core.nrm2
    wbc = A.alloc([128, D], F32)
    xsb = [A.alloc([128, D], BF16) for _ in range(2)]
    kb.dma('sp', wbc, wnorm_dram.partition_broadcast(128), [], ['wbc'], n['wds'])
    ssq, rstd = n['ssq'], n['rstd']
    for tb in range(TB):
        xs = xsb[tb % 2]
        kxs = ('xs', tb % 2)
        _act(kb, xs, X[:, tb, :], AF.Square, [('X', tb)], [kxs, ('ssq', tb)], accum_out=ssq[:, tb:tb + 1])
        _act(kb, rstd[:, tb:tb + 1], ssq[:, tb:tb + 1], AF.Sqrt, [('ssq', tb)], [('rstd', tb)], scale=1.0 / D, bias=core.eps_t[:, 0:1])
        kb.op('dve', lambda e: e.reciprocal(out=rstd[:, tb:tb + 1], in_=rstd[:, tb:tb + 1]), [('rstd', tb)], [('rstd', tb)])
        _stt(kb, xs, X[:, tb, :], rstd[:, tb:tb + 1], wbc, ALU.mult, ALU.mult, [('X', tb), ('rstd', tb), 'wbc'], [kxs])
        for half in range(2):
            bk = 6 + half
            pt = core.bank(bk)[:].bitcast(BF16)
            for kk in range(8):
                k = half * 8 + kk
                _tr(kb, pt[:, kk * 128:(kk + 1) * 128], xs[:, k * 128:(k + 1) * 128], core.ident_bf[:], [kxs, 'ident_bf'], [('ps', bk)])
            _act(kb, HT[:, half * 8:half * 8 + 8, tb * 128:(tb + 1) * 128], pt.rearrange("p (k t) -> p k t", k=8), AF.Copy, [('ps', bk)], [('HT', tb)])


def final_norm(core, A, X, TB, w_dram, out_view, ods):
    kb = core.kb
    n = core.nrm2
    wbc = A.alloc([128, D], F32)
    junk = A.alloc([128, D], BF16)
    kb.dma('sp', wbc, w_dram.partition_broadcast(128), [], ['wbc'], n['wds'])
    ssq, rstd = n['ssq'], n['rstd']
    for tb in range(TB):
        _act(kb, junk, X[:, tb, :], AF.Square, [('X', tb)], ['junk', ('ssq', tb)], accum_out=ssq[:, tb:tb + 1])
        _act(kb, rstd[:, tb:tb + 1], ssq[:, tb:tb + 1], AF.Sqrt, [('ssq', tb)], [('rstd', tb)], scale=1.0 / D, bias=core.eps_t[:, 0:1])
        kb.op('dve', lambda e: e.reciprocal(out=rstd[:, tb:tb + 1], in_=rstd[:, tb:tb + 1]), [('rstd', tb)], [('rstd', tb)])
        _stt(kb, X[:, tb, :], X[:, tb, :], rstd[:, tb:tb + 1], wbc, ALU.mult, ALU.mult, [('X', tb), ('rstd', tb), 'wbc'], [('X', tb)])
    kb.dma('sp', out_view, X[:, 0:TB, :], [('X', tb) for tb in range(TB)], [], ods)


def load_cols4(core, src2d, R, ntile, dst3, key):
    kb = core.kb
    if not hasattr(core, 'lc4'):
        core.lc4 = dict(tmp=[kb.sb([128, 512], F32, name=f"lc4tmp{i}") for i in range(2)], ds=[kb.dsem() for _ in range(2)], ods=[kb.dsem() for _ in range(2)], i=0)
    i = core.lc4['i'] % 2
    core.lc4['i'] += 1
    tmp = core.lc4['tmp'][i]
    kb.dma('sp', tmp[0:R, 0:ntile * 128], src2d, [], [('lc4', i)], core.lc4['ds'][i])
    for j in range(ntile):
        _tr(kb, core.bank(7)[:, j * R:(j + 1) * R], tmp[0:R, j * 128:(j + 1) * 128], core.ident_f[0:R, 0:R], [('lc4', i), 'ident_f'], [('ps', 7)])
    _cp(kb, 'dve', dst3, core.bank(7)[:, 0:ntile * R].rearrange("p (t r) -> p t r", t=ntile), [('ps', 7)], [key])


def store_cols4(core, srcs, R, dst2d, keys):
    kb = core.kb
    if not hasattr(core, 'lc4'):
        core.lc4 = dict(tmp=[kb.sb([128, 512], F32, name=f"lc4tmp{i}") for i in range(2)], ds=[kb.dsem() for _ in range(2)], ods=[kb.dsem() for _ in range(2)], i=0)
    i = core.lc4['i'] % 2
    core.lc4['i'] += 1
    tmp = core.lc4['tmp'][i]
    n = len(srcs)
    for j, sap in enumerate(srcs):
        _tr(kb, core.bank(7)[0:R, j * 128:(j + 1) * 128], sap, core.ident_f[:], list(keys) + ['ident_f'], [('ps', 7)])
    _cp(kb, 'dve', tmp[0:R, 0:n * 128], core.bank(7)[0:R, 0:n * 128], [('ps', 7)], [('lc4', i)])
    kb.dma('sp', dst2d, tmp[0:R, 0:n * 128], [('lc4', i)], [], core.lc4['ods'][i])


W_NAMES = ['norm_mix_w', 'norm_ffn_w', 'norm_final_w', 'w_in', 'gdn_conv_w', 'gdn_a_log', 'gdn_dt_bias', 'gdn_norm_w', 'ret_gn_w', 'ret_gn_b', 'w_out',
           's5_lam_re', 's5_lam_im', 's5_log_dt', 's5_b_re', 's5_b_im', 's5_c_re', 's5_c_im', 's5_d', 'w_glu', 'w_up', 'ffn_conv_w', 'ffn_conv_b', 'w_down']
W_SHAPES = dict(norm_mix_w=[2, D], norm_ffn_w=[2, D], norm_final_w=[D], w_in=[D, DIN], gdn_conv_w=[4, 3072], gdn_a_log=[8], gdn_dt_bias=[8], gdn_norm_w=[128],
                ret_gn_w=[1024], ret_gn_b=[1024], w_out=[D, D], s5_lam_re=[128, 64], s5_lam_im=[128, 64], s5_log_dt=[128], s5_b_re=[128, 64, 16], s5_b_im=[128, 64, 16],
                s5_c_re=[128, 16, 64], s5_c_im=[128, 16, 64], s5_d=[D], w_glu=[D, 2 * D], w_up=[2, D, 2 * DFF], ffn_conv_w=[2, 3, 2 * DFF], ffn_conv_b=[2, 2 * DFF], w_down=[2, DFF, D])
IN_SHAPES = dict(x_p=[2048, D], x_s=[128, D], state_gdn=[16, 8, 128, 128], state_gcb=[16, 3, 3072], state_ret=[16, 4, 128, 256], st_re=[16, 128, 64], st_im=[16, 128, 64],
                 state_fcb=[2, 16, 2, 2 * DFF])
OUT_SHAPES = dict(y_p=[2048, D], y_s=[128, D], o_gdn_p=[8, 128, 128], o_gdn_s=[16, 8, 128, 128], o_gcb_p=[3, 3072], o_gcb_s=[16, 3, 3072], o_ret_p=[4, 128, 256],
                  o_ret_s=[16, 4, 128, 256], o_s5r_p=[128, 64], o_s5r_s=[16, 128, 64], o_s5i_p=[128, 64], o_s5i_s=[16, 128, 64], o_fcb_p=[2, 2, 2 * DFF], o_fcb_s=[2, 16, 2, 2 * DFF])


def build_program(consts, n_tiles=5, dbg=False):
    nc = bass.Bass("TRN2", target_bir_lowering=False)

    def din(name, shape, dt=F32):
        return nc.dram_tensor(name, list(shape), dt, kind="ExternalInput").ap()

    def dout(name, shape, dt=F32):
        return nc.dram_tensor(name, list(shape), dt, kind="ExternalOutput").ap()
    P = {}
    for k in W_NAMES:
        P[k] = din(k, W_SHAPES[k])
    for k, shp in IN_SHAPES.items():
        P[k] = din(k, shp)
    for k, v in consts.items():
        P[k] = din(k, v.shape, BF16 if v.dtype == BF else F32)
    for k, shp in OUT_SHAPES.items():
        P[k] = dout(k, shp)
    tbl_dram = nc.dram_tensor("tbl_scratch", [64, 128, 2, 512], F32, kind="Internal").ap()
    if dbg:
        for i in range(4):
            P[f'dbg{i}'] = dout(f'dbg{i}', [512, D])
        P['dbgG'] = dout('dbgG', [128, 16 * 512], BF16)
        P['dbgT0'] = dout('dbgT0', [4, 128, 2, 512])
        P['dbgR'] = dout('dbgR', [128, 128])
        P['dbgT1'] = dout('dbgT1', [4, 128, 2, 512])

    def dump(kb, X, name, dds):
        if dbg:
            kb.barrier()
            kb.dma('sp', P[name].rearrange("(tb p) d -> p tb d", p=128), X[:, 0:4, :], [], [], dds)
            kb.barrier()
    P0 = dict(P)
    for k in ('w_in', 'gdn_conv_w', 'gdn_a_log', 'gdn_dt_bias', 'gdn_norm_w', 'ret_gn_w', 'ret_gn_b', 'w_out', 's5_lam_re', 's5_lam_im', 's5_log_dt',
              's5_b_re', 's5_b_im', 's5_c_re', 's5_c_im', 's5_d', 'w_glu'):
        P0[k] = P[k]
    with contextlib.ExitStack() as es:
        kb = KB(nc, es)
        core = Core(kb, P['idb'], P['idf'])
        core.eps_t = kb.sb([128, 1], F32, name="eps")
        kb.op('dve', lambda e: e.memset(core.eps_t[:], EPS), writes=['eps'])
        m = mixer0_setup(core, P0)
        X = kb.sb([128, 4, D], F32, name="X")
        HT = kb.sb([128, KT, 512], BF16, name="HT")
        cws = [kb.sb([128, 3, 2 * FT], F32, name=f"cws{l}") for l in range(2)]
        cbs = [kb.sb([128, 2 * FT], F32, name=f"cbs{l}") for l in range(2)]
        CBp = [kb.sb([128, 2 * FT, 1, 2], F32, name=f"CBp{l}") for l in range(2)]
        for l in range(2):
            load_cols(core, P['ffn_conv_w'][l].rearrange("j (t p) -> (j t) p", p=128), 3 * 2 * FT, cws[l][:].rearrange("p j t -> p (j t)"), 'convw')
            load_cols(core, P['ffn_conv_b'][l].rearrange("(t p) -> t p", p=128), 2 * FT, cbs[l][:], 'convw')
            kb.op('dve', lambda e: e.memset(CBp[l][:], 0.0), writes=[('CB', ft) for ft in range(2 * FT)])
        rem = nc.sbuf_bytes_remaining
        A = Arena(kb, rem - 28000)
        specs_all = weight_specs(P)
        late_names = ('w_up1', 'w_down1')
        WBF = convert_weights(core, A, nc, [sp for sp in specs_all if sp[0] not in late_names])
        late = LateConv(core, nc, [sp for sp in specs_all if sp[0] in late_names])
        WBF.update(late.out)
        kb.barrier()
        A.reset()
        P0['w_in_b'] = WBF['w_in']
        P0['w_glu'] = WBF['w_glu']
        s = s5_setup(core, P0, A, tbl_dram)
        kb.barrier()
        A.reset()
        if dbg:
            dd = kb.dsem()
            kb.dma('sp', P['dbgT0'], tbl_dram[48:52], [], [], dd)
            kb.dma('sp', P['dbgR'][:, 0:64], s['R'][:], [], [], dd)
            kb.dma('sp', P['dbgR'][:, 64:128], s['THK'][:], [], [], dd)
            kb.barrier()
        xds = kb.dsem()
        ods = kb.dsem()
        sds = kb.dsem()
        for tile in range(n_tiles):
            is_s = (tile == 4)
            N = 128 if is_s else 512
            TB = N // 128
            nseq, L = (16, 8) if is_s else (1, 512)
            last_p = (tile == 3)
            if is_s:
                xin = P['x_s'].rearrange("(tb p) d -> p tb d", p=128)
                yout = P['y_s'].rearrange("(tb p) d -> p tb d", p=128)
            else:
                xin = P['x_p'][tile * 512:(tile + 1) * 512, :].rearrange("(tb p) d -> p tb d", p=128)
                yout = P['y_p'][tile * 512:(tile + 1) * 512, :].rearrange("(tb p) d -> p tb d", p=128)
            kb.dma('sp', X[:, 0:TB, :], xin, [], [('X', tb) for tb in range(TB)], xds)
            HTv = HT[:, :, 0:N]
            rmsnorm_TA(core, A, X, TB, N, P['norm_mix_w'][0], HTv)
            kb.barrier()
            A.reset()
            mixedT = A.alloc([128, 16, N], BF16)
            if is_s:
                CBq = A.alloc([128, 24, 16, 3], F32)
                for c4 in range(6):
                    load_cols4(core, P['state_gcb'].rearrange("s j c -> (s j) c")[:, c4 * 512:(c4 + 1) * 512], 48, 4, CBq[:, c4 * 4:c4 * 4 + 4, :, :].rearrange("p t s j -> p t (s j)"), ('CBq', 0))
                kb.barrier()
            else:
                CBq = m['CBq'][:].rearrange("p c (s j) -> p c s j", s=1)
            base = A.off
            NS = 128 if is_s else 256
            for sub in range(N // NS):
                A.off = base
                cfg = dict(NS=NS, nseq=(16 if is_s else 1), L=(8 if is_s else NS), nseg=(8 if is_s else 1), v=(1 if is_s else 0),
                           rot0=(2048 if is_s else tile * 512 + sub * NS), seq0=0)
                mixer0_sub(core, m, A, P0, HTv[:, :, sub * NS:(sub + 1) * NS], cfg, mixedT, sub * NS, CBq)
                kb.barrier()
            wout_phase(core, X, TB, mixedT, WBF['w_out'])
            if tile == 0:
                dump(kb, X, 'dbg0', sds)
            if is_s or last_p:
                R = nseq * 3
                dst = (P['o_gcb_s'].rearrange("s j c -> (s j) c") if is_s else P['o_gcb_p'])
                for c4 in range(6):
                    store_cols4(core, [CBq[:, c4 * 4 + j, :, :].rearrange("p s j -> p (s j)") for j in range(4)], R, dst[:, c4 * 512:(c4 + 1) * 512], [('CBq', ct) for ct in range(24)])
            kb.barrier()
            A.reset()
            for layer in range(2):
                if layer == 1:
                    rmsnorm_TA(core, A, X, TB, N, P['norm_mix_w'][1], HTv)
                    kb.barrier()
                    A.reset()
                    XCs = None
                    if is_s:
                        XCs = A.alloc([128, 64, 16, 2], F32)
                        for sq in range(16):
                            for ri, nm in enumerate(('st_re', 'st_im')):
                                load_cols(core, P[nm][sq].rearrange("(pr gi) p -> pr (gi p)", gi=2), 64, XCs[:, :, sq, ri], 'XCs')
                    if is_s:
                        kb.barrier()
                    s5_phase(core, s, A, P0, X, HTv, dict(N=N, nseq=nseq, L=L), tbl_dram, XCs, bg=(late if tile == 0 else None))
                    if tile == 0:
                        late.emit(10 ** 9)
                    if is_s or last_p:
                        kb.barrier()
                    if tile == 0:
                        dump(kb, X, 'dbg2', sds)
                        if dbg:
                            kb.dma('sp', P['dbgG'], A.t[:, 0:4096].bitcast(BF16), [], [], sds)
                            kb.barrier()
                    if is_s:
                        for sq in range(16):
                            for ri, nm in enumerate(('o_s5r_s', 'o_s5i_s')):
                                store_cols4(core, [XCs[:, :, sq, ri]], 64, P[nm][sq].rearrange("(pr gi) p -> pr (gi p)", gi=2), ['XCs'])
                    elif last_p:
                        for ri, nm in enumerate(('o_s5r_p', 'o_s5i_p')):
                            store_cols4(core, [s['XC'][:, :, ri]], 64, P[nm].rearrange("(pr gi) p -> pr (gi p)", gi=2), ['XC'])
                    kb.barrier()
                    A.reset()
                rmsnorm_TA(core, A, X, TB, N, P['norm_ffn_w'][layer], HTv)
                kb.barrier()
                A.reset()
                HM = A.alloc([128, FT, N], BF16)
                U = [A.alloc([128, 3, nseq, L + 2], F32) for _ in range(2)]
                CV = A.alloc([128, 3, N], F32)
                SG = A.alloc([128, 3, N], F32)
                if is_s:
                    CB = A.alloc([128, 2 * FT, 16, 2], F32)
                    src = P['state_fcb'][layer].rearrange("s j c -> (s j) c")
                    for c4 in range(22):
                        load_cols4(core, src[:, c4 * 512:(c4 + 1) * 512], 32, 4, CB[:, c4 * 4:c4 * 4 + 4, :, :].rearrange("p t s j -> p t (s j)"), ('CB', c4))
                    kb.barrier()
                else:
                    CB = CBp[layer]
                ffn_phase(core, X, TB, N, nseq, L, HTv, HM, WBF[f'w_up{layer}'], cws[layer], cbs[layer], WBF[f'w_down{layer}'], CB, U, CV, SG)
                if tile == 0:
                    dump(kb, X, 'dbg1' if layer == 0 else 'dbg3', sds)
                if is_s or last_p:
                    R = nseq * 2
                    dst = (P['o_fcb_s'][layer].rearrange("s j c -> (s j) c") if is_s else P['o_fcb_p'][layer])
                    for c4 in range(22):
                        store_cols4(core, [CB[:, c4 * 4 + j, :, :].rearrange("p s j -> p (s j)") for j in range(4)], R, dst[:, c4 * 512:(c4 + 1) * 512], [('CB', ft) for ft in range(2 * FT)])
                kb.barrier()
                A.reset()
            final_norm(core, A, X, TB, P['norm_final_w'], yout, ods)
            kb.barrier()
            A.reset()
            if last_p:
                kb.dma('sp', P['o_gdn_p'].rearrange("h d e -> d h e"), m['SG'][:], ['SG'], [], sds)
                kb.dma('sp', P['o_ret_p'].rearrange("h d e -> d h e"), m['SR'][:], ['SR'], [], sds)
        if dbg:
            kb.dma('sp', P['dbgT1'], tbl_dram[48:52], [], [], dd)
        kb.barrier()
        import os
        if os.environ.get('KDEBUG'):
            print("sbuf remaining at end", nc.sbuf_bytes_remaining, "arena bytes", A.n32 * 4, "peak", A.peak * 4, "cnt", kb.cnt)
            for nm in ('s5WB0', 's5WB1', 's5WC0', 's5WC1', 's5XC', 'lc4tmp0', 'lc4tmp1', 'ssq', 'rstd', 'arena', 's5R', 'X', 'HT'):
                try:
                    print(nm, nc.lookup_mloc(nm))
                except Exception as ex:
                    print(nm, 'ERR', repr(ex)[:100])
    return nc


def kernel(**inputs):
    consts = make_consts_s5(make_consts())
    nc = build_program(consts)
    f32 = np.float32
    w = {}
    for k in W_NAMES:
        a = np.asarray(inputs[k], dtype=f32)
        if list(a.shape) != W_SHAPES[k]:
            a = a[0]
        assert list(a.shape) == W_SHAPES[k], (k, a.shape)
        w[k] = np.ascontiguousarray(a)
    in_maps = []
    for c in range(8):
        b = c % 4
        sl = slice(16 * c, 16 * c + 16)
        d = dict(w)
        d.update(consts)
        d['x_p'] = np.ascontiguousarray(np.asarray(inputs['x_prompt'], f32)[b])
        d['x_s'] = np.ascontiguousarray(np.asarray(inputs['x_sample'], f32)[sl].reshape(128, D))
        d['state_gdn'] = np.ascontiguousarray(np.asarray(inputs['state_gdn'], f32)[0, sl])
        d['state_gcb'] = np.ascontiguousarray(np.asarray(inputs['state_gdn_conv'], f32)[0, sl])
        d['state_ret'] = np.ascontiguousarray(np.asarray(inputs['state_ret'], f32)[0, sl])
        d['st_re'] = np.ascontiguousarray(np.asarray(inputs['state_s5_re'], f32)[0, sl])
        d['st_im'] = np.ascontiguousarray(np.asarray(inputs['state_s5_im'], f32)[0, sl])
        d['state_fcb'] = np.ascontiguousarray(np.asarray(inputs['state_ffn_conv'], f32)[:, sl])
        in_maps.append(d)
    res = run_bass_kernel_spmd(nc, in_maps, core_ids=list(range(8)))
    r = res.results
    cat = lambda k, axis=0: np.concatenate([np.asarray(r[c][k], f32) for c in range(8)], axis=axis)
    stk = lambda k: np.stack([np.asarray(r[c][k], f32) for c in range(4)], axis=0)
    y_prompt = stk('y_p')
    y_sample = cat('y_s').reshape(128, 8, D)
    gdn_p = stk('o_gdn_p')[None]
    gdn_s = cat('o_gdn_s')[None]
    gcb_p = stk('o_gcb_p')[None]
    gcb_s = cat('o_gcb_s')[None]
    ret_p = stk('o_ret_p')[None]
    ret_s = cat('o_ret_s')[None]
    s5r_p = stk('o_s5r_p')[None]
    s5r_s = cat('o_s5r_s')[None]
    s5i_p = stk('o_s5i_p')[None]
    s5i_s = cat('o_s5i_s')[None]
    fcb_p = np.stack([np.asarray(r[c]['o_fcb_p'], f32) for c in range(4)], axis=1)
    fcb_s = cat('o_fcb_s', axis=1)
    return (y_prompt, y_sample, gdn_p, gdn_s, gcb_p, gcb_s, ret_p, ret_s, s5r_p, s5r_s, s5i_p, s5i_s, fcb_p, fcb_s)
```

```python
import numpy as np
import ml_dtypes
from concourse.bass_utils import run_bass_kernel_spmd
import contextlib
import concourse.bass as bass
import concourse.mybir as mybir

F32 = mybir.dt.float32
BF16 = mybir.dt.bfloat16
AF = mybir.ActivationFunctionType
ALU = mybir.AluOpType
AX = mybir.AxisListType


class DSem:
    def __init__(self, sem):
        self.sem = sem
        self.count = 0


class KB:
    def __init__(self, nc, es):
        self.nc = nc
        self.es = es
        self.E = {'pe': nc.tensor, 'act': nc.scalar, 'dve': nc.vector, 'pool': nc.gpsimd, 'sp': nc.sync}
        self.sem = {e: es.enter_context(nc.semaphore('s_' + e)) for e in self.E}
        self.cnt = {e: 0 for e in self.E}
        self.waited = {e: {} for e in self.E}
        self.res = {}
        self.nds = 0
        self.nt = 0

    def sb(self, shape, dt, name=None):
        self.nt += 1
        return self.es.enter_context(self.nc.sbuf_tensor(name or f"t{self.nt}", list(shape), dt))

    def ps(self, shape, dt, name=None):
        self.nt += 1
        return self.es.enter_context(self.nc.psum_tensor(name or f"p{self.nt}", list(shape), dt))

    def dsem(self):
        self.nds += 1
        return DSem(self.es.enter_context(self.nc.semaphore(f"d{self.nds}")))

    @staticmethod
    def _ex(keys):
        out = []
        for k in keys:
            if isinstance(k, tuple) and k[0] == 'ps' and len(k) == 2:
                out.append(('ps', k[1], 0))
                out.append(('ps', k[1], 64))
            else:
                out.append(k)
        return out

    def _deps(self, reads, writes):
        reads = self._ex(reads)
        writes = self._ex(writes)
        deps = []
        for k in reads:
            st = self.res.get(k)
            if st and st['w']:
                deps.append(st['w'])
        for k in writes:
            st = self.res.get(k)
            if st:
                if st['w']:
                    deps.append(st['w'])
                deps.extend(st['r'].values())
        return deps

    def _wait(self, eng, deps):
        for (sem, val, deng) in deps:
            if deng == eng and eng == 'pe':
                continue
            key = id(sem)
            if self.waited[eng].get(key, 0) >= val:
                continue
            self.E[eng].wait_ge(sem, val)
            self.waited[eng][key] = val

    def _update(self, tok, reads, writes):
        reads = self._ex(reads)
        writes = self._ex(writes)
        for k in reads:
            st = self.res.setdefault(k, {'w': None, 'r': {}})
            st['r'][id(tok[0])] = tok
        for k in writes:
            self.res[k] = {'w': tok, 'r': {}}

    def op(self, eng, fn, reads=(), writes=()):
        self._wait(eng, self._deps(reads, writes))
        inst = fn(self.E[eng])
        self.cnt[eng] += 1
        inst.then_inc(self.sem[eng], 1)
        self._update((self.sem[eng], self.cnt[eng], eng), reads, writes)

    def dma(self, eng, out, in_, reads, writes, ds, **kw):
        self._wait(eng, self._deps(reads, writes))
        inst = self.E[eng].dma_start(out=out, in_=in_, **kw)
        ds.count += 16
        inst.then_inc(ds.sem, 16)
        self._update((ds.sem, ds.count, 'dma'), reads, writes)

    def wait_all(self, eng, keys):
        deps = []
        for k in keys:
            st = self.res.get(k)
            if st:
                if st['w']:
                    deps.append(st['w'])
                deps.extend(st['r'].values())
        self._wait(eng, deps)


def _kb_barrier(self):
    toks = []
    for e in self.E:
        if self.cnt[e] > 0:
            toks.append((self.sem[e], self.cnt[e], e))
    for ds in self.all_ds:
        if ds.count > 0:
            toks.append((ds.sem, ds.count, 'dma'))
    for e in self.E:
        self._wait(e, [t for t in toks if t[2] != e])
    self.res = {}


def _kb_dsem(self):
    self.nds += 1
    d = DSem(self.es.enter_context(self.nc.semaphore(f"d{self.nds}")))
    if not hasattr(self, 'all_ds'):
        self.all_ds = []
    self.all_ds.append(d)
    return d


KB.barrier = _kb_barrier
KB.dsem = _kb_dsem


class Arena:
    def __init__(self, kb, nbytes, name="arena"):
        self.kb = kb
        self.n32 = nbytes // 4
        self.t = kb.sb([128, self.n32], F32, name=name)
        self.off = 0

    def reset(self):
        self.off = 0

    def alloc(self, shape, dt):
        n = 1
        for s in shape[1:]:
            n *= s
        nb = n * (2 if dt == BF16 else 4)
        n32 = (nb + 3) // 4
        assert self.off + n32 <= self.n32, f"arena overflow {self.off + n32} > {self.n32}"
        ap = self.t[:, self.off:self.off + n32]
        self.off += n32
        self.peak = max(getattr(self, 'peak', 0), self.off)
        if dt == BF16:
            ap = ap.bitcast(dt)[:, :n]
        elif dt != F32:
            ap = ap.bitcast(dt)
        if len(shape) > 2:
            names = "abcdefg"[:len(shape) - 1]
            kw = {names[i]: shape[1 + i] for i in range(len(shape) - 1)}
            ap = ap.rearrange("p (" + " ".join(names) + ") -> p " + " ".join(names), **kw)
        return ap


D = 2048
DFF = 5632
KT = D // 128
FT = DFF // 128
EPS = 1e-6


class Core:
    def __init__(self, kb, ident_bf_dram, ident_f_dram):
        self.kb = kb
        nc = kb.nc
        self.banks = [kb.ps([128, 512], F32, name=f"bank{i}") for i in range(8)]
        self.R = 12
        self.wbf = [kb.sb([128, 512], BF16, name=f"wbf{i}") for i in range(self.R)]
        self.wds = [kb.dsem() for _ in range(self.R)]
        self.wi = 0
        self.ident_bf = kb.sb([128, 128], BF16, name="ident_bf")
        self.ident_f = kb.sb([128, 128], F32, name="ident_f")
        self.cds = kb.dsem()
        kb.dma('pool', self.ident_bf[:], ident_bf_dram, [], ['ident_bf'], self.cds)
        kb.dma('sp', self.ident_f[:], ident_f_dram, [], ['ident_f'], kb.dsem())

    def bank(self, i):
        return self.banks[i]

    def wload(self, src, w):
        kb = self.kb
        s = self.wi % self.R
        self.wi += 1
        kb.dma('sp', self.wbf[s][:, :w], src, [], [('wbf', s)], self.wds[s])
        return ('wbf', s), self.wbf[s]


def convert_weights(core, A, nc, P):
    kb = core.kb
    NBUF = 6
    st = [A.alloc([128, 2048], F32) for _ in range(NBUF)]
    ob = [A.alloc([128, 2048], BF16) for _ in range(NBUF)]
    ids = [kb.dsem() for _ in range(NBUF)]
    ods = [kb.dsem() for _ in range(NBUF)]
    specs = [('w_in', P['w_in'], D, DIN), ('w_out', P['w_out'], D, D), ('w_glu', P['w_glu'], D, 2 * D)]
    for l in range(2):
        specs.append((f'w_up{l}', P['w_up'][l], D, 2 * DFF))
        specs.append((f'w_down{l}', P['w_down'][l], DFF, D))
    out = {}
    i = 0
    for (nm, src, rows, cols) in specs:
        dst = nc.dram_tensor(nm + "_bf16", [rows, cols], BF16, kind="Internal").ap()
        out[nm] = dst
        for r0 in range(0, rows, 128):
            for c0 in range(0, cols, 2048):
                w = min(2048, cols - c0)
                b = i % NBUF
                i += 1
                kb.dma('sp', st[b][:, :w], src[r0:r0 + 128, c0:c0 + w], [], [('cst', b)], ids[b])
                kb.op('dve', lambda e: e.tensor_copy(out=ob[b][:, :w], in_=st[b][:, :w]), [('cst', b)], [('cob', b)])
                kb.dma('act', dst[r0:r0 + 128, c0:c0 + w], ob[b][:, :w], [('cob', b)], [], ods[b])
    return out


def rmsnorm_T(core, X, TB, N, wnorm_dram, HT, tagx='X', tagh='HT', bank=6):
    kb = core.kb
    nc = kb.nc
    if not hasattr(core, 'nrm'):
        core.nrm = dict(
            wbc=[kb.sb([128, D], F32, name=f"wbc{i}") for i in range(2)],
            wds=[kb.dsem() for _ in range(2)],
            xs=[kb.sb([128, D], BF16, name=f"xs{i}") for i in range(2)],
            ssq=kb.sb([128, 8], F32, name="ssq"),
            rstd=kb.sb([128, 8], F32, name="rstd"),
            i=0, j=0)
    n = core.nrm
    wi = n['i'] % 2
    n['i'] += 1
    wbc = n['wbc'][wi]
    kb.dma('sp', wbc[:], wnorm_dram.partition_broadcast(128), [], [('wbc', wi)], n['wds'][wi])
    ssq, rstd = n['ssq'], n['rstd']
    for tb in range(TB):
        xs = n['xs'][n['j'] % 2]
        kxs = ('xs', n['j'] % 2)
        n['j'] += 1
        kb.op('act', lambda e: e.activation(out=xs[:], in_=X[:, tb, :], func=AF.Square, accum_out=ssq[:, tb:tb + 1]),
              reads=[(tagx, tb)], writes=[kxs, ('ssq', tb)])
        kb.op('act', lambda e: e.activation(out=rstd[:, tb:tb + 1], in_=ssq[:, tb:tb + 1], func=AF.Sqrt, scale=1.0 / D, bias=core.eps_t[:, 0:1]),
              reads=[('ssq', tb)], writes=[('rstd', tb)])
        kb.op('dve', lambda e: e.reciprocal(out=rstd[:, tb:tb + 1], in_=rstd[:, tb:tb + 1]),
              reads=[('rstd', tb)], writes=[('rstd', tb)])
        kb.op('dve', lambda e: e.scalar_tensor_tensor(out=xs[:], in0=X[:, tb, :], scalar=rstd[:, tb:tb + 1], in1=wbc[:], op0=ALU.mult, op1=ALU.mult),
              reads=[(tagx, tb), ('rstd', tb), ('wbc', wi)], writes=[kxs])
        for half in range(2):
            bk = bank + half
            pt = core.bank(bk)[:].bitcast(BF16)
            for kk in range(8):
                k = half * 8 + kk
                kb.op('pe', lambda e: e.transpose(out=pt[:, kk * 128:(kk + 1) * 128], in_=xs[:, k * 128:(k + 1) * 128], identity=core.ident_bf[:]),
                      reads=[kxs, 'ident_bf'], writes=[('ps', bk)])
            kb.op('act', lambda e: e.activation(out=HT[:, half * 8:half * 8 + 8, tb * 128:(tb + 1) * 128],
                                                in_=pt.rearrange("p (k t) -> p k t", k=8), func=AF.Copy),
                  reads=[('ps', bk)], writes=[(tagh, tb)])


def ffn_phase(core, X, TB, N, nseq, L, HT, HM, w_up, conv_w_sb, conv_b_sb, w_down, CB, U, CV, SG):
    kb = core.kb
    G = 3
    groups = []
    t = 0
    while t < FT:
        g = min(G, FT - t)
        groups.append((t, g))
        t += g
    par = 0
    for (t0, g) in groups:
        for half in range(2):
            b0 = par * 3
            Ub = U[par]
            kU = ('U', par)
            par ^= 1
            c0 = half * DFF + t0 * 128
            for k in range(KT):
                wk, wt = core.wload(w_up[k * 128:(k + 1) * 128, c0:c0 + g * 128], g * 128)
                for i in range(g):
                    kb.op('pe', lambda e: e.matmul(core.bank(b0 + i)[:, :N], lhsT=wt[:, i * 128:(i + 1) * 128], rhs=HT[:, k, :], start=(k == 0), stop=(k == KT - 1)),
                          reads=[wk] + [('HT', tb) for tb in range(TB)], writes=[('ps', b0 + i)])
            for i in range(g):
                ft = half * FT + t0 + i
                kb.op('pool', lambda e: e.tensor_copy(out=Ub[:, i, :, 0:2], in_=CB[:, ft, :, :]),
                      reads=[('CB', ft)], writes=[kU + (i,)])
                kb.op('act', lambda e: e.activation(out=Ub[:, i, :, 2:2 + L], in_=core.bank(b0 + i)[:, :N].rearrange("p (s l) -> p s l", s=nseq), func=AF.Copy),
                      reads=[('ps', b0 + i)], writes=[kU + (i,)])
                kb.op('pool', lambda e: e.tensor_copy(out=CB[:, ft, :, :], in_=Ub[:, i, :, L:L + 2]),
                      reads=[kU + (i,)], writes=[('CB', ft)])
                dst = CV if half == 0 else SG
                kd = ('CV', i) if half == 0 else ('SG', i)
                d3 = dst[:, i, :].rearrange("p (s l) -> p s l", s=nseq)
                kb.op('dve', lambda e: e.tensor_scalar(out=d3, in0=Ub[:, i, :, 0:L], scalar1=conv_w_sb[:, 0, ft:ft + 1], scalar2=conv_b_sb[:, ft:ft + 1], op0=ALU.mult, op1=ALU.add),
                      reads=[kU + (i,), 'convw'], writes=[kd])
                for j in (1, 2):
                    kb.op('dve', lambda e: e.scalar_tensor_tensor(out=d3, in0=Ub[:, i, :, j:j + L], scalar=conv_w_sb[:, j, ft:ft + 1], in1=d3, op0=ALU.mult, op1=ALU.add),
                          reads=[kU + (i,), kd, 'convw'], writes=[kd])
                if half == 1:
                    kb.op('act', lambda e: e.activation(out=SG[:, i, :], in_=SG[:, i, :], func=AF.Silu),
                          reads=[kd], writes=[kd])
                    kb.op('dve', lambda e: e.tensor_tensor(out=HM[:, t0 + i, :], in0=CV[:, i, :], in1=SG[:, i, :], op=ALU.mult),
                          reads=[('CV', i), ('SG', i)], writes=[('HM', t0 + i)])
    for cb in range(D // 512):
        b0 = (cb % 2) * 4
        for k in range(FT):
            wk, wt = core.wload(w_down[k * 128:(k + 1) * 128, cb * 512:(cb + 1) * 512], 512)
            for tb in range(TB):
                kb.op('pe', lambda e: e.matmul(core.bank(b0 + tb)[:, :512], lhsT=HM[:, k, tb * 128:(tb + 1) * 128], rhs=wt[:, :512], start=(k == 0), stop=(k == FT - 1)),
                      reads=[wk, ('HM', k)], writes=[('ps', b0 + tb)])
        for tb in range(TB):
            kb.op('dve', lambda e: e.tensor_tensor(out=X[:, tb, cb * 512:(cb + 1) * 512], in0=core.bank(b0 + tb)[:, :512], in1=X[:, tb, cb * 512:(cb + 1) * 512], op=ALU.add),
                  reads=[('ps', b0 + tb), ('X', tb)], writes=[('X', tb)])


def load_cols(core, src2d, R, dst, key, bank=7):
    kb = core.kb
    if not hasattr(core, 'lc'):
        core.lc = dict(tmp=[kb.sb([128, 128], F32, name=f"lctmp{i}") for i in range(2)], ds=[kb.dsem() for _ in range(2)], i=0)
    r0 = 0
    while r0 < R:
        rows = min(128, R - r0)
        i = core.lc['i'] % 2
        core.lc['i'] += 1
        tmp = core.lc['tmp'][i]
        kb.dma('sp', tmp[0:rows, :], src2d[r0:r0 + rows, :], [], [('lctmp', i)], core.lc['ds'][i])
        kb.op('pe', lambda e: e.transpose(out=core.bank(bank)[:, 0:rows], in_=tmp[0:rows, :], identity=core.ident_f[0:rows, 0:rows]),
              reads=[('lctmp', i), 'ident_f'], writes=[('ps', bank)])
        kb.op('dve', lambda e: e.tensor_copy(out=dst[:, r0:r0 + rows], in_=core.bank(bank)[:, 0:rows]),
              reads=[('ps', bank)], writes=[key])
        r0 += rows


GH = 8
RH = 4
DIN = 7184
C_Z = 3072
C_AB = 4096
C_QB = 4112
C_KB = 4624
C_VB = 5136
C_GB = 6160
MIXW = 2048


def _act(kb, out, in_, func, reads, writes, **kw):
    kb.op('act', lambda e: e.activation(out=out, in_=in_, func=func, **kw), reads, writes)


def _tt(kb, out, in0, in1, op, reads, writes, eng='dve'):
    kb.op(eng, lambda e: e.tensor_tensor(out=out, in0=in0, in1=in1, op=op), reads, writes)


def _ts(kb, out, in0, s1, s2, op0, op1, reads, writes, eng='dve'):
    if op1 is None:
        kb.op(eng, lambda e: e.tensor_scalar(out=out, in0=in0, scalar1=s1, scalar2=None, op0=op0), reads, writes)
    else:
        kb.op(eng, lambda e: e.tensor_scalar(out=out, in0=in0, scalar1=s1, scalar2=s2, op0=op0, op1=op1), reads, writes)


def _stt(kb, out, in0, scalar, in1, op0, op1, reads, writes):
    kb.op('dve', lambda e: e.scalar_tensor_tensor(out=out, in0=in0, scalar=scalar, in1=in1, op0=op0, op1=op1), reads, writes)


def _mm(kb, out, lhsT, rhs, reads, writes, start=True, stop=True, tp=None):
    if tp is None:
        kb.op('pe', lambda e: e.matmul(out, lhsT=lhsT, rhs=rhs, start=start, stop=stop), reads, writes)
    else:
        kb.op('pe', lambda e: e.matmul(out, lhsT=lhsT, rhs=rhs, start=start, stop=stop, tile_position=tp), reads, writes)


def _tr(kb, out, in_, ident, reads, writes):
    kb.op('pe', lambda e: e.transpose(out=out, in_=in_, identity=ident), reads, writes)


def _cp(kb, eng, out, in_, reads, writes):
    if eng == 'act':
        kb.op(eng, lambda e: e.activation(out=out, in_=in_, func=AF.Copy), reads, writes)
    else:
        kb.op(eng, lambda e: e.tensor_copy(out=out, in_=in_), reads, writes)


def bc(ap, axis, n):
    a = ap.unsqueeze(axis)
    shp = list(a.shape)
    shp[axis] = n
    return a.broadcast_to(shp)


def mixer0_setup(core, P):
    kb = core.kb
    m = {}
    m['gcw'] = kb.sb([128, 4, 24], F32, name="gcw")
    load_cols(core, P['gdn_conv_w'].rearrange("j (t p) -> (j t) p", p=128), 96, m['gcw'][:].rearrange("p j t -> p (j t)"), 'gcw')
    m['gnw'] = kb.sb([128, 1], F32, name="gnw")
    load_cols(core, P['gdn_norm_w'].rearrange("(o p) -> o p", o=1), 1, m['gnw'][:], 'gnw')
    m['rgw'] = kb.sb([128, 8], F32, name="rgw")
    m['rgb'] = kb.sb([128, 8], F32, name="rgb")
    load_cols(core, P['ret_gn_w'].rearrange("(t p) -> t p", p=128), 8, m['rgw'][:], 'rgw')
    load_cols(core, P['ret_gn_b'].rearrange("(t p) -> t p", p=128), 8, m['rgb'][:], 'rgb')
    m['negA'] = kb.sb([128, 8], F32, name="negA")
    m['dtb'] = kb.sb([128, 8], F32, name="dtb")
    kb.dma('sp', m['negA'][:], P['gdn_a_log'].partition_broadcast(128), [], ['negA'], kb.dsem())
    kb.dma('sp', m['dtb'][:], P['gdn_dt_bias'].partition_broadcast(128), [], ['dtb'], kb.dsem())
    _act(kb, m['negA'][:], m['negA'][:], AF.Exp, ['negA'], ['negA'])
    _ts(kb, m['negA'][:], m['negA'][:], -1.0, None, ALU.mult, None, ['negA'], ['negA'])
    wabf = kb.sb([128, 16, 16], F32, name="wabf")
    m['wab'] = kb.sb([128, 16, 16], BF16, name="wab")
    kb.dma('sp', wabf[:], P['w_in'][:, C_AB:C_AB + 16].rearrange("(k p) c -> p k c", p=128), [], ['wabf'], kb.dsem())
    _cp(kb, 'dve', m['wab'][:], wabf[:], ['wabf'], ['wab'])
    m['mask'] = kb.sb([128, 2, 3, 64], F32, name="mask")
    kb.dma('sp', m['mask'][:], P['c_mask'], [], ['mask'], kb.dsem())
    m['segmask'] = kb.sb([128, 8, 64], BF16, name="segmask")
    kb.dma('sp', m['segmask'][:], P['c_segmask'], [], ['segmask'], kb.dsem())
    m['rowmask'] = kb.sb([128, 8], BF16, name="rowmask")
    kb.dma('sp', m['rowmask'][:], P['c_rowmask'], [], ['rowmask'], kb.dsem())
    m['dect'] = kb.sb([128, 2, 4, 64], F32, name="dect")
    kb.dma('sp', m['dect'][:], P['c_dect'], [], ['dect'], kb.dsem())
    m['qdec'] = kb.sb([128, 2, 4, 64], F32, name="qdec")
    kb.dma('sp', m['qdec'][:], P['c_qdec'], [], ['qdec'], kb.dsem())
    m['kdec'] = kb.sb([128, 2, 4], F32, name="kdec")
    kb.dma('sp', m['kdec'][:], P['c_kdec'], [], ['kdec'], kb.dsem())
    m['onesb'] = kb.sb([128, 128], BF16, name="onesb")
    kb.op('dve', lambda e: e.memset(m['onesb'][:], 1.0), writes=['onesb'])
    m['onesf'] = kb.sb([128, 128], F32, name="onesf")
    kb.op('dve', lambda e: e.memset(m['onesf'][:], 1.0 / 256.0), writes=['onesf'])
    m['ones1f'] = kb.sb([128, 128], F32, name="ones1f")
    kb.op('dve', lambda e: e.memset(m['ones1f'][:], 1.0), writes=['ones1f'])
    m['one_t'] = kb.sb([128, 1], F32, name="one_t")
    kb.op('dve', lambda e: e.memset(m['one_t'][:], 1.0), writes=['one_t'])
    m['SG'] = kb.sb([128, 8, 128], F32, name="SGp")
    m['SR'] = kb.sb([128, 4, 256], F32, name="SRp")
    m['CBq'] = kb.sb([128, 24, 3], F32, name="CBqp")
    kb.op('dve', lambda e: e.memset(m['SG'][:], 0.0), writes=['SG'])
    kb.op('dve', lambda e: e.memset(m['SR'][:], 0.0), writes=['SR'])
    kb.op('dve', lambda e: e.memset(m['CBq'][:], 0.0), writes=['CBq'])
    m['rot_ds'] = kb.dsem()
    m['st_ds'] = [kb.dsem() for _ in range(4)]
    m['sto_ds'] = [kb.dsem() for _ in range(4)]
    m['sti'] = 0
    return m


def fm_groups():
    out = []
    for g in range(8):
        out.append((g * 384, 3, 'qkv', g * 3))
    for (c, kind) in ((C_Z, 'z'), (C_GB, 'g')):
        for (i0, n) in ((0, 3), (3, 3), (6, 2)):
            out.append((c + i0 * 128, n, kind, i0))
    return out


def mixer0_sub(core, m, A, P, HTs, cfg, mixedT, mcol0, CBq):
    kb = core.kb
    NS, nseq, L, nseg, v = cfg['NS'], cfg['nseq'], cfg['L'], cfg['nseg'], cfg['v']
    TBs = NS // 128
    w_in = P['w_in_b']
    HTk = [('HT', t) for t in range(4)]
    qT = A.alloc([128, 8, NS], BF16)
    kT = A.alloc([128, 8, NS], BF16)
    vT = A.alloc([128, 8, NS], BF16)
    zT = A.alloc([128, 8, NS], BF16)
    gT = A.alloc([128, 8, NS], BF16)
    mark1 = A.off
    QU = [A.alloc([128, 3, nseq, L + 3], F32) for _ in range(2)]
    CVQs = [A.alloc([128, 3, NS], F32) for _ in range(3)]
    SQs = [A.alloc([128, NS], BF16) for _ in range(9)]
    RSs = [A.alloc([128, NS], F32) for _ in range(9)]
    par = 0
    pendA = {}
    pendBC = {}

    def flush(d, key):
        for fn in d.pop(key, []):
            fn()
    for gidx, (c0, g, kind, i0) in enumerate(fm_groups()):
        b0 = par * 3
        Ub = QU[par]
        kU = ('QU', par)
        CVQ = CVQs[gidx % 3]
        cpar = gidx % 3
        par ^= 1
        for k in range(KT):
            wk, wt = core.wload(w_in[k * 128:(k + 1) * 128, c0:c0 + g * 128], g * 128)
            for i in range(g):
                _mm(kb, core.bank(b0 + i)[:, :NS], wt[:, i * 128:(i + 1) * 128], HTs[:, k, :], [wk] + HTk, [('ps', b0 + i)], start=(k == 0), stop=(k == KT - 1))
        tA, tBC = [], []
        for i in range(g):
            ps = core.bank(b0 + i)[:, :NS]
            pk = ('ps', b0 + i)
            if kind == 'z':
                _act(kb, zT[:, i0 + i, :], ps, AF.Silu, [pk], [('zT', i0 + i)])
                continue
            if kind == 'g':
                _act(kb, gT[:, i0 + i, :], ps, AF.Silu, [pk], [('gT', i0 + i)])
                continue
            ct = i0 + i
            h = ct % 8
            _cp(kb, 'pool', Ub[:, i, :, 0:3], CBq[:, ct, :, :], [('CBq', ct)], [kU + (i,)])
            _act(kb, Ub[:, i, :, 3:3 + L], ps.rearrange("p (s l) -> p s l", s=nseq), AF.Copy, [pk], [kU + (i,)])
            _cp(kb, 'pool', CBq[:, ct, :, :], Ub[:, i, :, L:L + 3], [kU + (i,)], [('CBq', ct)])
            d3 = CVQ[:, i, :].rearrange("p (s l) -> p s l", s=nseq)
            kd = ('CVQ', cpar, i)
            slot = cpar * 3 + i
            _ts(kb, d3, Ub[:, i, :, 0:L], m['gcw'][:, 0, ct:ct + 1], None, ALU.mult, None, [kU + (i,), 'gcw'], [kd])
            for j in (1, 2, 3):
                _stt(kb, d3, Ub[:, i, :, j:j + L], m['gcw'][:, j, ct:ct + 1], d3, ALU.mult, ALU.add, [kU + (i,), kd, 'gcw'], [kd])

            def mkA(CVQ=CVQ, i=i, kd=kd, slot=slot, ct=ct, h=h):
                SQ = SQs[slot]
                if ct >= 16:
                    _act(kb, vT[:, h, :], CVQ[:, i, :], AF.Silu, [kd], [('vT', h)])
                    return
                _act(kb, CVQ[:, i, :], CVQ[:, i, :], AF.Silu, [kd], [kd])
                _act(kb, SQ[:, :], CVQ[:, i, :], AF.Square, [kd], [('SQ', slot)])

            def mkB(slot=slot, ct=ct):
                if ct >= 16:
                    return
                lb = 6 + (slot % 4) // 2
                lc = ((slot % 4) % 2) * 256
                _mm(kb, core.bank(lb)[:, lc:lc + NS], m['onesb'][:], SQs[slot][:, :], [('SQ', slot), 'onesb'], [('ps', lb)])

            def mkC(CVQ=CVQ, i=i, kd=kd, slot=slot, ct=ct, h=h):
                if ct >= 16:
                    return
                RS = RSs[slot]
                kRS = ('RS', slot)
                lb = 6 + (slot % 4) // 2
                lc = ((slot % 4) % 2) * 256
                _act(kb, RS[:, :], core.bank(lb)[:, lc:lc + NS], AF.Sqrt, [('ps', lb)], [kRS], bias=core.eps_t[:, 0:1])
                kb.op('dve', lambda e: e.reciprocal(out=RS[:, :], in_=RS[:, :]), [kRS], [kRS])
                if ct < 8:
                    _stt(kb, qT[:, h, :], CVQ[:, i, :], 128.0 ** -0.5, RS[:, :], ALU.mult, ALU.mult, [kd, kRS], [('qT', h)])
                else:
                    _tt(kb, kT[:, h, :], CVQ[:, i, :], RS[:, :], ALU.mult, [kd, kRS], [('kT', h)])
            tA.append(mkA)
            tBC.append((mkB, mkC))
        pendA[gidx] = tA
        pendBC[gidx] = [fb for (fb, fc) in tBC] + [fc for (fb, fc) in tBC]
        flush(pendBC, gidx - 2)
        flush(pendA, gidx - 1)
    for gi2 in sorted(set(list(pendA.keys()) + list(pendBC.keys()))):
        flush(pendA, gi2)
        flush(pendBC, gi2)
    kb.barrier()
    A.off = mark1
    rq = A.alloc([128, TBs, 4, 128], BF16)
    rk = A.alloc([128, TBs, 4, 128], BF16)
    rv = A.alloc([128, TBs, 1024], BF16)
    LA = A.alloc([128, TBs, 8], F32)
    BETA = A.alloc([128, TBs, 8], F32)
    mark2 = A.off
    ROT = A.alloc([128, 4, TBs, 64], F32)
    for r in range(4):
        kb.dma('sp', ROT[:, r, :, :], P['c_rot'][r, cfg['rot0']:cfg['rot0'] + NS, :].rearrange("(tb p) f -> p tb f", p=128), [], ['ROT'], m['rot_ds'])
    TA = A.alloc([128, 4, 64], F32)
    TBt = A.alloc([128, 4, 64], F32)
    bpar = 0
    for bi, c0 in enumerate((C_QB, C_KB, C_VB, C_VB + 512)):
        b0 = bpar * 2
        bpar = (bpar + 1) % 3
        for k in range(KT):
            wk, wt = core.wload(w_in[k * 128:(k + 1) * 128, c0:c0 + 512], 512)
            for tb in range(TBs):
                _mm(kb, core.bank(b0 + tb)[:, :512], HTs[:, k, tb * 128:(tb + 1) * 128], wt[:, :512], [wk] + HTk, [('ps', b0 + tb)], start=(k == 0), stop=(k == KT - 1))
        for tb in range(TBs):
            ps = core.bank(b0 + tb)
            pk = ('ps', b0 + tb)
            if bi >= 2:
                _act(kb, rv[:, tb, (bi - 2) * 512:(bi - 1) * 512], ps[:, :512], AF.Copy, [pk], ['rv'])
                continue
            dst = rq if bi == 0 else rk
            kdst = 'rq' if bi == 0 else 'rk'
            p4 = ps[:, :512].rearrange("p (h d) -> p h d", h=4)
            t1, t2 = p4[:, :, 0:64], p4[:, :, 64:128]
            cos = bc(ROT[:, 2 * bi, tb, :], 1, 4)
            sin = bc(ROT[:, 2 * bi + 1, tb, :], 1, 4)
            _tt(kb, TA, t1, cos, ALU.mult, [pk, 'ROT'], ['TA'])
            _tt(kb, TBt, t2, sin, ALU.mult, [pk, 'ROT'], ['TB'])
            _tt(kb, dst[:, tb, :, 0:64], TA, TBt, ALU.subtract, ['TA', 'TB'], [kdst])
            _tt(kb, TA, t1, sin, ALU.mult, [pk, 'ROT'], ['TA'])
            _tt(kb, TBt, t2, cos, ALU.mult, [pk, 'ROT'], ['TB'])
            _tt(kb, dst[:, tb, :, 64:128], TA, TBt, ALU.add, ['TA', 'TB'], [kdst])
    AB = A.alloc([128, TBs, 16], F32)
    for tb in range(TBs):
        for k in range(KT):
            _mm(kb, core.bank(6)[:, tb * 16:(tb + 1) * 16], HTs[:, k, tb * 128:(tb + 1) * 128], m['wab'][:, k, :], ['wab'] + HTk, [('ps', 6)], start=(k == 0), stop=(k == KT - 1))
    _act(kb, AB[:, :, :], core.bank(6)[:, 0:TBs * 16].rearrange("p (t c) -> p t c", t=TBs), AF.Copy, [('ps', 6)], ['AB'])
    _tt(kb, LA[:, :, :], AB[:, :, 0:8], bc(m['dtb'][:, :], 1, TBs), ALU.add, ['AB', 'dtb'], ['LA'])
    _act(kb, LA[:, :, :], LA[:, :, :], AF.Exp, ['LA'], ['LA'])
    _act(kb, LA[:, :, :], LA[:, :, :], AF.Ln, ['LA'], ['LA'], bias=m['one_t'][:, 0:1])
    _tt(kb, LA[:, :, :], LA[:, :, :], bc(m['negA'][:, :], 1, TBs), ALU.mult, ['LA', 'negA'], ['LA'])
    _act(kb, BETA[:, :, :], AB[:, :, 8:16], AF.Sigmoid, ['AB'], ['BETA'])
    kb.barrier()
    A.off = mark2
    gdn_sub(core, m, A, P, cfg, qT, kT, vT, zT, LA, BETA, mixedT, mcol0)
    kb.barrier()
    A.off = mark2
    ret_sub(core, m, A, P, cfg, rq, rk, rv, gT, mixedT, mcol0)
    kb.barrier()
    A.off = mark2


def gdn_sub(core, m, A, P, cfg, qT, kT, vT, zT, LA, BETA, mixedT, mcol0):
    kb = core.kb
    NS, nseg, v = cfg['NS'], cfg['nseg'], cfg['v']
    TBs = NS // 128
    MK = m['mask']
    I_INCL, I_STRICT, I_SEG = 0, 1, 2
    idf = core.ident_f
    OTs = A.alloc([128, 8, 128], F32)
    GG = A.alloc([128, 16], F32)
    EG = A.alloc([128, 8], F32)
    EKD = A.alloc([128, 8], F32)
    BEG = A.alloc([128, 8], F32)
    GQh = A.alloc([128, 8, 64], F32)
    BQh = A.alloc([128, 8, 64], F32)
    EGQ = A.alloc([128, 8, 128], F32)
    qdT = A.alloc([128, 8, 128], BF16)
    kbg = A.alloc([128, 8, 128], BF16)
    kdt = A.alloc([128, 8, 128], BF16)
    vb = A.alloc([128, 8, 128], BF16)
    D1 = A.alloc([128, 8, 64], F32)
    DMs = A.alloc([128, 8, 64], F32)
    RHt, RBt = D1, DMs
    QKm = A.alloc([128, 8, 64], BF16)
    Uc = [A.alloc([128, 8, 64], F32) for _ in range(2)]
    UTc = [A.alloc([128, 8, 64], F32) for _ in range(2)]
    Pm = A.alloc([128, 8, 64], F32)
    Pb = A.alloc([128, 8, 64], BF16)
    U32 = A.alloc([128, 8, 128], F32)
    wT = A.alloc([128, 8, 128], BF16)
    VN = A.alloc([128, 8, 128], BF16)
    def _re(ap):
        return ap.rearrange("p h c -> p (h c)").rearrange("p (a b) -> p a b", a=4)
    SQo = _re(Pb)
    RSo = _re(Uc[0])
    TMP = _re(Uc[1])
    kSQ = [('Pb', 0), ('Pb', 64)]
    kRS = [('U', 0, 0), ('U', 0, 64)]
    kTM = [('U', 1, 0), ('U', 1, 64)]
    if nseg == 1:
        S = m['SG']
        Sb = A.alloc([128, 8, 128], BF16)
        _cp(kb, 'act', Sb, S[:, :, :], ['SG'], ['Sb'])
    else:
        Ss = [A.alloc([128, 8, 128], F32) for _ in range(2)]
        Ssb = [A.alloc([128, 8, 128], BF16) for _ in range(2)]
        wTm = A.alloc([128, 8, 64], BF16)
        qdTm = A.alloc([128, 8, 64], BF16)
        kdm = A.alloc([128, 8, 128], BF16)
        EGL = A.alloc([128, 8], F32)
    H = (0, 64)
    for tb in range(TBs):
        tc0 = tb * 128
        for pb in H:
            sl = slice(pb, pb + 64)
            _mm(kb, core.bank(7)[sl, 0:8], MK[sl, v, I_INCL, :], LA[sl, tb, :], ['mask', 'LA'], [('ps', 7)])
            _mm(kb, core.bank(7)[sl, 8:16], MK[sl, v, I_SEG, :], LA[sl, tb, :], ['mask', 'LA'], [('ps', 7)])
        _act(kb, GG[:, :], core.bank(7)[:, 0:16], AF.Copy, [('ps', 7)], ['GG'])
        _act(kb, EG[:, :], GG[:, 0:8], AF.Exp, ['GG'], ['EG'])
        _tt(kb, EKD[:, :], GG[:, 8:16], GG[:, 0:8], ALU.subtract, ['GG'], ['EKD'])
        _act(kb, EKD[:, :], EKD[:, :], AF.Exp, ['EKD'], ['EKD'])
        _tt(kb, BEG[:, :], EG[:, :], BETA[:, tb, :], ALU.mult, ['EG', 'BETA'], ['BEG'])
        for pb in H:
            sl = slice(pb, pb + 64)
            _tt(kb, RHt[sl], bc(MK[sl, v, I_INCL, :], 1, 8), bc(LA[sl, tb, :], 2, 64), ALU.mult, ['mask', 'LA'], [('D1', pb)])
            _tt(kb, RBt[sl], bc(idf[sl, pb:pb + 64], 1, 8), bc(BETA[sl, tb, :], 2, 64), ALU.mult, ['ident_f', 'BETA'], [('DMs', pb)])
        for hg in range(2):
            for hh in range(4):
                h = hg * 4 + hh
                for pb in H:
                    sl = slice(pb, pb + 64)
                    _mm(kb, core.bank(hg)[:, hh * 128 + pb:hh * 128 + pb + 64], m['ones1f'][sl, :], RHt[sl, h, :], [('D1', pb), 'ones1f'], [('ps', hg)])
                    _mm(kb, core.bank(2 + hg)[:, hh * 128 + pb:hh * 128 + pb + 64], m['ones1f'][sl, :], RBt[sl, h, :], [('DMs', pb), 'ones1f'], [('ps', 2 + hg)])
            g4 = core.bank(hg)[:, :].rearrange("p (h c) -> p h c", h=4)
            b4 = core.bank(2 + hg)[:, :].rearrange("p (h c) -> p h c", h=4)
            _act(kb, EGQ[:, hg * 4:hg * 4 + 4, :], g4, AF.Exp, [('ps', hg)], ['EGQ'])
            for pb in H:
                sl = slice(pb, pb + 64)
                _cp(kb, 'act', GQh[sl, hg * 4:hg * 4 + 4, :], g4[sl, :, pb:pb + 64], [('ps', hg)], [('GQh', pb)])
                _cp(kb, 'act', BQh[sl, hg * 4:hg * 4 + 4, :], b4[sl, :, pb:pb + 64], [('ps', 2 + hg)], [('BQh', pb)])
        _tt(kb, qdT, qT[:, :, tc0:tc0 + 128], EGQ, ALU.mult, ['qT', 'EGQ'], ['qdT'])
        ptk = core.bank(4)[:].bitcast(BF16)
        ptv = core.bank(5)[:].bitcast(BF16)
        for h in range(8):
            _tr(kb, ptk[:, h * 128:(h + 1) * 128], kT[:, h, tc0:tc0 + 128], core.ident_bf[:], ['kT', 'ident_bf'], [('ps', 4)])
            _tr(kb, ptv[:, h * 128:(h + 1) * 128], vT[:, h, tc0:tc0 + 128], core.ident_bf[:], ['vT', 'ident_bf'], [('ps', 5)])
        pk3 = ptk.rearrange("p (h d) -> p h d", h=8)
        pv3 = ptv.rearrange("p (h d) -> p h d", h=8)
        _tt(kb, kbg, pk3, bc(BEG[:, :], 2, 128), ALU.mult, [('ps', 4), 'BEG'], ['kbg'])
        _tt(kb, kdt, pk3, bc(EKD[:, :], 2, 128), ALU.mult, [('ps', 4), 'EKD'], ['kdt'])
        _tt(kb, vb, pv3, bc(BETA[:, tb, :], 2, 128), ALU.mult, [('ps', 5), 'BETA'], ['vb'])
        for pb in H:
            sl = slice(pb, pb + 64)
            _tt(kb, D1[sl], GQh[sl], bc(GG[sl, 0:8], 2, 64), ALU.subtract, [('GQh', pb), 'GG'], [('D1', pb)])
            _ts(kb, D1[sl], D1[sl], 0.0, None, ALU.min, None, [('D1', pb)], [('D1', pb)])
            _act(kb, D1[sl], D1[sl], AF.Exp, [('D1', pb)], [('D1', pb)])
            _tt(kb, DMs[sl], D1[sl], bc(MK[sl, v, I_STRICT, :], 1, 8), ALU.mult, [('D1', pb), 'mask'], [('DMs', pb)])
            _tt(kb, DMs[sl], DMs[sl], BQh[sl], ALU.mult, [('DMs', pb), ('BQh', pb)], [('DMs', pb)])
            _tt(kb, D1[sl], D1[sl], bc(MK[sl, v, I_INCL, :], 1, 8), ALU.mult, [('D1', pb), 'mask'], [('D1', pb)])
            for h in range(8):
                blk = slice(tc0 + pb, tc0 + pb + 64)
                _mm(kb, core.bank(6)[sl, h * 64:(h + 1) * 64], kT[:, h, blk], kT[:, h, blk], ['kT'], [('ps', 6, pb)])
                _mm(kb, core.bank(7)[sl, h * 64:(h + 1) * 64], kT[:, h, blk], qT[:, h, blk], ['kT', 'qT'], [('ps', 7, pb)])
            U0 = Uc[0]
            _tt(kb, U0[sl], core.bank(6)[sl, :].rearrange("p (h c) -> p h c", h=8), DMs[sl], ALU.mult, [('ps', 6, pb), ('DMs', pb)], [('U', 0, pb)])
            _tt(kb, QKm[sl], core.bank(7)[sl, :].rearrange("p (h c) -> p h c", h=8), D1[sl], ALU.mult, [('ps', 7, pb), ('D1', pb)], [('QKm', pb)])
        for pb in H:
            sl = slice(pb, pb + 64)
            for h in range(8):
                _mm(kb, core.bank(0)[sl, h * 64:(h + 1) * 64], Uc[0][sl, h, :], idf[sl, pb:pb + 64], [('U', 0, pb), 'ident_f'], [('ps', 0, pb)])
            _cp(kb, 'act', UTc[0][sl], core.bank(0)[sl, :].rearrange("p (h c) -> p h c", h=8), [('ps', 0, pb)], [('UT', 0, pb)])
            _tt(kb, Pm[sl], bc(idf[sl, pb:pb + 64], 1, 8), Uc[0][sl], ALU.subtract, ['ident_f', ('U', 0, pb)], [('P', pb)])
        cur = 0
        for step in range(5):
            nxt = cur ^ 1
            last = (step == 4)
            for pb in H:
                sl = slice(pb, pb + 64)
                for h in range(8):
                    if not last:
                        _mm(kb, core.bank(1)[sl, h * 64:(h + 1) * 64], UTc[cur][sl, h, :], Uc[cur][sl, h, :], [('UT', cur, pb), ('U', cur, pb)], [('ps', 1, pb)])
                    _mm(kb, core.bank(2)[sl, h * 64:(h + 1) * 64], Uc[cur][sl, h, :], UTc[cur][sl, h, :], [('UT', cur, pb), ('U', cur, pb)], [('ps', 2, pb)])
                if not last:
                    _cp(kb, 'act', Uc[nxt][sl], core.bank(1)[sl, :].rearrange("p (h c) -> p h c", h=8), [('ps', 1, pb)], [('U', nxt, pb)])
                _cp(kb, 'act', UTc[nxt][sl], core.bank(2)[sl, :].rearrange("p (h c) -> p h c", h=8), [('ps', 2, pb)], [('UT', nxt, pb)])
                for h in range(8):
                    _mm(kb, core.bank(3)[sl, h * 64:(h + 1) * 64], UTc[nxt][sl, h, :], Pm[sl, h, :], [('UT', nxt, pb), ('P', pb)], [('ps', 3, pb)])
                _tt(kb, Pm[sl], Pm[sl], core.bank(3)[sl, :].rearrange("p (h c) -> p h c", h=8), ALU.add, [('ps', 3, pb), ('P', pb)], [('P', pb)])
            cur = nxt
        for pb in H:
            sl = slice(pb, pb + 64)
            _cp(kb, 'act', Pb[sl], Pm[sl], [('P', pb)], [('Pb', pb)])
            for h in range(8):
                _mm(kb, core.bank(4 + h // 4)[sl, (h % 4) * 128:(h % 4 + 1) * 128], Pb[sl, h, :], vb[sl, h, :], [('Pb', pb), 'vb'], [('ps', 4 + h // 4, pb)])
            for hg in range(2):
                _cp(kb, 'act', U32[sl, hg * 4:hg * 4 + 4, :], core.bank(4 + hg)[sl, :].rearrange("p (h e) -> p h e", h=4), [('ps', 4 + hg, pb)], [('U32', pb)])
            for h in range(8):
                _mm(kb, core.bank(6 + h // 4)[:, (h % 4) * 128 + pb:(h % 4) * 128 + pb + 64], kbg[sl, h, :], Pb[sl, h, :], [('Pb', pb), 'kbg'], [('ps', 6 + h // 4)])
        for hg in range(2):
            _cp(kb, 'act', wT[:, hg * 4:hg * 4 + 4, :], core.bank(6 + hg)[:, :].rearrange("p (h c) -> p h c", h=4), [('ps', 6 + hg)], ['wT'])
        for pb in H:
            sl = slice(pb, pb + 64)
            hc = slice(pb, pb + 64)
            if nseg == 1:
                for h in range(8):
                    _mm(kb, core.bank(h // 4)[sl, (h % 4) * 128:(h % 4 + 1) * 128], wT[:, h, hc], Sb[:, h, :], ['wT', 'Sb'], [('ps', h // 4, pb)])
                for hg in range(2):
                    _tt(kb, VN[sl, hg * 4:hg * 4 + 4, :], U32[sl, hg * 4:hg * 4 + 4, :], core.bank(hg)[sl, :].rearrange("p (h e) -> p h e", h=4), ALU.subtract, [('U32', pb), ('ps', hg, pb)], [('VN', pb)])
                for h in range(8):
                    o = core.bank(2)[:, h * 64:(h + 1) * 64]
                    _mm(kb, o, Sb[:, h, :], qdT[:, h, hc], ['Sb', 'qdT'], [('ps', 2)], start=True, stop=False)
                    _mm(kb, o, VN[sl, h, :], QKm[sl, h, :], [('VN', pb), ('QKm', pb)], [('ps', 2)], start=False, stop=True)
                    _mm(kb, core.bank(4 + h // 4)[:, (h % 4) * 128:(h % 4 + 1) * 128], kdt[sl, h, :], VN[sl, h, :], ['kdt', ('VN', pb)], [('ps', 4 + h // 4)])
                _cp(kb, 'act', OTs[:, :, hc], core.bank(2)[:, :].rearrange("p (h c) -> p h c", h=8), [('ps', 2)], ['OTs'])
                egl = bc(EGQ[:, :, pb + 63], 2, 128)
                _tt(kb, S[:, :, :], S[:, :, :], egl, ALU.mult, ['SG', 'EGQ'], ['SG'])
                for hg in range(2):
                    _tt(kb, S[:, hg * 4:hg * 4 + 4, :], S[:, hg * 4:hg * 4 + 4, :], core.bank(4 + hg)[:, :].rearrange("p (h e) -> p h e", h=4), ALU.add, ['SG', ('ps', 4 + hg)], ['SG'])
                _cp(kb, 'act', Sb, S[:, :, :], ['SG'], ['Sb'])
            else:
                half = pb // 64
                sq0 = cfg['seq0'] + (tb * 2 + half) * 8
                for h in range(8):
                    si = m['sti'] % 2
                    m['sti'] += 1
                    St, Stb = Ss[si], Ssb[si]
                    kS, kSb = ('Ss', si), ('Ssb', si)
                    kb.dma('sp', St, P['state_gdn'][sq0:sq0 + 8, h, :, :].rearrange("s d e -> d s e"), [], [kS], m['st_ds'][si])
                    _cp(kb, 'act', Stb, St, [kS], [kSb])
                    _tt(kb, wTm, bc(wT[:, h, hc], 1, 8), m['segmask'][:, :, :], ALU.mult, ['wT', 'segmask'], ['wTm'])
                    _tt(kb, qdTm, bc(qdT[:, h, hc], 1, 8), m['segmask'][:, :, :], ALU.mult, ['qdT', 'segmask'], ['qdTm'])
                    _tt(kb, kdm[sl], bc(kdt[sl, h, :], 1, 8), bc(m['rowmask'][sl, :], 2, 128), ALU.mult, ['kdt', 'rowmask'], [('kdm', pb)])
                    for s in range(8):
                        _mm(kb, core.bank(0)[sl, 0:128], wTm[:, s, :], Stb[:, s, :], ['wTm', kSb], [('ps', 0, pb)], start=(s == 0), stop=(s == 7))
                    _tt(kb, VN[sl, h, :], U32[sl, h, :], core.bank(0)[sl, 0:128], ALU.subtract, [('U32', pb), ('ps', 0, pb)], [('VN', pb)])
                    o = core.bank(2)[:, h * 64:(h + 1) * 64]
                    for s in range(8):
                        _mm(kb, o, Stb[:, s, :], qdTm[:, s, :], [kSb, 'qdTm'], [('ps', 2)], start=(s == 0), stop=False)
                    _mm(kb, o, VN[sl, h, :], QKm[sl, h, :], [('VN', pb), ('QKm', pb)], [('ps', 2)], start=False, stop=True)
                    for s in range(8):
                        _mm(kb, core.bank(4 + s // 4)[:, (s % 4) * 128:(s % 4 + 1) * 128], kdm[sl, s, :], VN[sl, h, :], [('kdm', pb), ('VN', pb)], [('ps', 4 + s // 4)])
                    _cp(kb, 'dve', EGL[:, :], EGQ[:, h, pb + 7:pb + 64:8], ['EGQ'], ['EGL'])
                    _tt(kb, St, St, bc(EGL[:, :], 2, 128), ALU.mult, [kS, 'EGL'], [kS])
                    for sg in range(2):
                        _tt(kb, St[:, sg * 4:sg * 4 + 4, :], St[:, sg * 4:sg * 4 + 4, :], core.bank(4 + sg)[:, :].rearrange("p (s e) -> p s e", s=4), ALU.add, [kS, ('ps', 4 + sg)], [kS])
                    kb.dma('sp', P['o_gdn_s'][sq0:sq0 + 8, h, :, :].rearrange("s d e -> d s e"), St, [kS], [], m['sto_ds'][si])
                _cp(kb, 'act', OTs[:, :, hc], core.bank(2)[:, :].rearrange("p (h c) -> p h c", h=8), [('ps', 2)], ['OTs'])
        for hg in range(2):
            hs = slice(hg * 4, hg * 4 + 4)
            _act(kb, SQo, OTs[:, hs, :], AF.Square, ['OTs'], kSQ)
            for hh in range(4):
                _mm(kb, core.bank(3)[:, hh * 128:(hh + 1) * 128], m['onesb'][:], SQo[:, hh, :], kSQ + ['onesb'], [('ps', 3)])
            _act(kb, RSo, core.bank(3)[:, :].rearrange("p (h c) -> p h c", h=4), AF.Sqrt, [('ps', 3)], kRS, scale=1.0 / 128, bias=core.eps_t[:, 0:1])
            kb.op('dve', lambda e: e.reciprocal(out=RSo, in_=RSo), kRS, kRS)
            _stt(kb, TMP, OTs[:, hs, :], m['gnw'][:, 0:1], RSo, ALU.mult, ALU.mult, ['OTs', 'gnw'] + kRS, kTM)
            _tt(kb, mixedT[:, hs, mcol0 + tc0:mcol0 + tc0 + 128], TMP, zT[:, hs, tc0:tc0 + 128], ALU.mult, kTM + ['zT'], ['mixedT'])


def ret_sub(core, m, A, P, cfg, rq, rk, rv, gT, mixedT, mcol0):
    kb = core.kb
    NS, nseg, v = cfg['NS'], cfg['nseg'], cfg['v']
    TBs = NS // 128
    NBs = NS // 64
    lg = [np.log1p(-2.0 ** (-5.0 - h)) for h in range(4)]
    cch = 64 if nseg == 1 else 8
    gch = [float(np.exp(lg[h] * cch)) for h in range(4)]
    rqT = A.alloc([128, 4, NS], BF16)
    rkT = A.alloc([128, 4, NS], BF16)
    rqd = A.alloc([128, 4, NS], BF16)
    rkd = A.alloc([128, TBs, 4, 128], BF16)
    RQK = A.alloc([128, 4, 64], BF16)
    ORs = A.alloc([128, 8, NS], F32)
    for tb in range(TBs):
        ptq = core.bank(0)[:].bitcast(BF16)
        ptk = core.bank(1)[:].bitcast(BF16)
        for h in range(4):
            _tr(kb, ptq[:, h * 128:(h + 1) * 128], rq[:, tb, h, :], core.ident_bf[:], ['rq', 'ident_bf'], [('ps', 0)])
            _tr(kb, ptk[:, h * 128:(h + 1) * 128], rk[:, tb, h, :], core.ident_bf[:], ['rk', 'ident_bf'], [('ps', 1)])
        _cp(kb, 'act', rqT[:, :, tb * 128:(tb + 1) * 128], ptq[:, 0:512].rearrange("p (h c) -> p h c", h=4), [('ps', 0)], ['rqT'])
        _cp(kb, 'act', rkT[:, :, tb * 128:(tb + 1) * 128], ptk[:, 0:512].rearrange("p (h c) -> p h c", h=4), [('ps', 1)], ['rkT'])
    _tt(kb, rqd.rearrange("p h (b c) -> p h b c", c=64), rqT.rearrange("p h (b c) -> p h b c", c=64), bc(m['qdec'][:, v, :, :], 2, NBs), ALU.mult, ['rqT', 'qdec'], ['rqd'])
    _tt(kb, rkd, rk, bc(bc(m['kdec'][:, v, :], 1, TBs), 3, 128), ALU.mult, ['rk', 'kdec'], ['rkd'])
    if nseg == 1:
        S = m['SR']
        Sb = A.alloc([128, 4, 256], BF16)
        _cp(kb, 'act', Sb, S[:, :, :], ['SR'], ['SRb'])
    else:
        Ss = [A.alloc([128, 8, 256], F32) for _ in range(2)]
        Ssb = [A.alloc([128, 8, 256], BF16) for _ in range(2)]
        rqdm = A.alloc([128, 8, 64], BF16)
        rkdm = A.alloc([128, 8, 128], BF16)
    for tb in range(TBs):
        for pb in (0, 64):
            sl = slice(pb, pb + 64)
            blk = slice(tb * 128 + pb, tb * 128 + pb + 64)
            for h in range(4):
                _mm(kb, core.bank(0)[sl, h * 64:(h + 1) * 64], rkT[:, h, blk], rqT[:, h, blk], ['rkT', 'rqT'], [('ps', 0, pb)])
            _tt(kb, RQK[sl], core.bank(0)[sl, 0:256].rearrange("p (h c) -> p h c", h=4), m['dect'][sl, v, :, :], ALU.mult, [('ps', 0, pb), 'dect'], [('RQK', pb)])
            if nseg == 1:
                for h in range(4):
                    for et in range(2):
                        o = core.bank(1)[:, (h * 2 + et) * 64:(h * 2 + et + 1) * 64]
                        _mm(kb, o, rv[sl, tb, h * 256 + et * 128:h * 256 + (et + 1) * 128], RQK[sl, h, :], ['rv', ('RQK', pb)], [('ps', 1)], start=True, stop=False)
                        _mm(kb, o, Sb[:, h, et * 128:(et + 1) * 128], rqd[:, h, blk], ['SRb', 'rqd'], [('ps', 1)], start=False, stop=True)
                    _mm(kb, core.bank(2 + h // 2)[:, (h % 2) * 256:(h % 2 + 1) * 256], rkd[sl, tb, h, :], rv[sl, tb, h * 256:(h + 1) * 256], ['rkd', 'rv'], [('ps', 2 + h // 2)])
                _cp(kb, 'act', ORs[:, :, blk], core.bank(1)[:, :].rearrange("p (h c) -> p h c", h=8), [('ps', 1)], ['ORs'])
                for h in range(4):
                    _stt(kb, S[:, h, :], S[:, h, :], gch[h], core.bank(2 + h // 2)[:, (h % 2) * 256:(h % 2 + 1) * 256], ALU.mult, ALU.add, ['SR', ('ps', 2 + h // 2)], ['SR'])
                _cp(kb, 'act', Sb, S[:, :, :], ['SR'], ['SRb'])
            else:
                half = pb // 64
                sq0 = cfg['seq0'] + (tb * 2 + half) * 8
                for h in range(4):
                    si = m['sti'] % 2
                    m['sti'] += 1
                    St, Stb = Ss[si], Ssb[si]
                    kS, kSb = ('Rs', si), ('Rsb', si)
                    kb.dma('sp', St, P['state_ret'][sq0:sq0 + 8, h, :, :].rearrange("s d e -> d s e"), [], [kS], m['st_ds'][2 + si])
                    _cp(kb, 'act', Stb, St, [kS], [kSb])
                    _tt(kb, rqdm, bc(rqd[:, h, blk], 1, 8), m['segmask'][:, :, :], ALU.mult, ['rqd', 'segmask'], ['rqdm'])
                    _tt(kb, rkdm[sl], bc(rkd[sl, tb, h, :], 1, 8), bc(m['rowmask'][sl, :], 2, 128), ALU.mult, ['rkd', 'rowmask'], [('rkdm', pb)])
                    for et in range(2):
                        o = core.bank(1)[:, (h * 2 + et) * 64:(h * 2 + et + 1) * 64]
                        _mm(kb, o, rv[sl, tb, h * 256 + et * 128:h * 256 + (et + 1) * 128], RQK[sl, h, :], ['rv', ('RQK', pb)], [('ps', 1)], start=True, stop=False)
                        for s in range(8):
                            _mm(kb, o, Stb[:, s, et * 128:(et + 1) * 128], rqdm[:, s, :], [kSb, 'rqdm'], [('ps', 1)], start=False, stop=(s == 7))
                    for sg in range(2):
                        for s4 in range(4):
                            s = sg * 4 + s4
                            _mm(kb, core.bank(2 + s4 // 2)[:, (s4 % 2) * 256:(s4 % 2 + 1) * 256], rkdm[sl, s, :], rv[sl, tb, h * 256:(h + 1) * 256], [('rkdm', pb), 'rv'], [('ps', 2 + s4 // 2)])
                        for b2 in range(2):
                            s0 = sg * 4 + b2 * 2
                            _stt(kb, St[:, s0:s0 + 2, :], St[:, s0:s0 + 2, :], gch[h], core.bank(2 + b2)[:, :].rearrange("p (s e) -> p s e", s=2), ALU.mult, ALU.add, [kS, ('ps', 2 + b2)], [kS])
                    kb.dma('sp', P['o_ret_s'][sq0:sq0 + 8, h, :, :].rearrange("s d e -> d s e"), St, [kS], [], m['sto_ds'][2 + si])
                _cp(kb, 'act', ORs[:, :, blk], core.bank(1)[:, :].rearrange("p (h c) -> p h c", h=8), [('ps', 1)], ['ORs'])
    SQr = A.alloc([128, 2, NS], F32)
    MEAN = A.alloc([128, NS], F32)
    VAR = A.alloc([128, NS], F32)
    T1 = A.alloc([128, NS], F32)
    for h in range(4):
        _act(kb, SQr, ORs[:, 2 * h:2 * h + 2, :], AF.Square, ['ORs'], ['SQr'])
        for et in range(2):
            _mm(kb, core.bank(4)[:, 0:NS], m['onesf'][:], ORs[:, 2 * h + et, :], ['ORs', 'onesf'], [('ps', 4)], start=(et == 0), stop=(et == 1))
        for et in range(2):
            _mm(kb, core.bank(5)[:, 0:NS], m['onesf'][:], SQr[:, et, :], ['SQr', 'onesf'], [('ps', 5)], start=(et == 0), stop=(et == 1))
        _cp(kb, 'act', MEAN, core.bank(4)[:, 0:NS], [('ps', 4)], ['MEAN'])
        _act(kb, VAR, core.bank(4)[:, 0:NS], AF.Square, [('ps', 4)], ['VAR'])
        _tt(kb, VAR, core.bank(5)[:, 0:NS], VAR, ALU.subtract, [('ps', 5), 'VAR'], ['VAR'])
        _act(kb, VAR, VAR, AF.Sqrt, ['VAR'], ['VAR'], bias=core.eps_t[:, 0:1])
        kb.op('dve', lambda e: e.reciprocal(out=VAR, in_=VAR), ['VAR'], ['VAR'])
        for et in range(2):
            c = 2 * h + et
            _tt(kb, T1, ORs[:, c, :], MEAN, ALU.subtract, ['ORs', 'MEAN'], ['T1'])
            _tt(kb, T1, T1, VAR, ALU.mult, ['T1', 'VAR'], ['T1'])
            _ts(kb, T1, T1, m['rgw'][:, c:c + 1], m['rgb'][:, c:c + 1], ALU.mult, ALU.add, ['T1', 'rgw', 'rgb'], ['T1'])
            _tt(kb, mixedT[:, 8 + c, mcol0:mcol0 + NS], T1, gT[:, c, :], ALU.mult, ['T1', 'gT'], ['mixedT'])


def wout_phase(core, X, TB, mixedT, w_out):
    kb = core.kb
    for cb in range(D // 512):
        b0 = (cb % 2) * 4
        for k in range(KT):
            wk, wt = core.wload(w_out[k * 128:(k + 1) * 128, cb * 512:(cb + 1) * 512], 512)
            for tb in range(TB):
                _mm(kb, core.bank(b0 + tb)[:, :512], mixedT[:, k, tb * 128:(tb + 1) * 128], wt[:, :512], [wk, 'mixedT'], [('ps', b0 + tb)], start=(k == 0), stop=(k == KT - 1))
        for tb in range(TB):
            _tt(kb, X[:, tb, cb * 512:(cb + 1) * 512], core.bank(b0 + tb)[:, :512], X[:, tb, cb * 512:(cb + 1) * 512], ALU.add, [('ps', b0 + tb), ('X', tb)], [('X', tb)])


I32 = mybir.dt.int32
TWO_PI = 6.283179


def s5_setup(core, P, A, tbl_dram):
    kb = core.kb
    idf = core.ident_f
    s = {}
    s['sel'] = kb.sb([128, 2], F32, name="s5sel")
    kb.dma('sp', s['sel'][:], P['c_sel'], [], ['sel'], kb.dsem())
    s['iota1'] = A.alloc([128, 512], F32)
    kb.dma('sp', s['iota1'], P['c_iota1'], [], ['iota1'], kb.dsem())
    s['segst'] = kb.sb([128, 128], F32, name="s5segst")
    kb.dma('sp', s['segst'][:], P['c_segst'], [], ['segst'], kb.dsem())
    s['halfpi'] = kb.sb([128, 1], F32, name="s5halfpi")
    kb.op('dve', lambda e: e.memset(s['halfpi'][:], float(np.pi / 2)), writes=['halfpi'])
    s['dcol'] = kb.sb([128, 16], F32, name="s5d")
    load_cols(core, P['s5_d'].rearrange("(t p) -> t p", p=128), 16, s['dcol'][:], 'dcol')
    LR = A.alloc([128, 64], F32)
    LI = A.alloc([128, 64], F32)
    DT = A.alloc([128, 64], F32)
    load_cols(core, P['s5_lam_re'].rearrange("(pr gi) p -> pr (gi p)", gi=2), 64, LR, 'LR')
    load_cols(core, P['s5_lam_im'].rearrange("(pr gi) p -> pr (gi p)", gi=2), 64, LI, 'LI')
    LD2 = A.alloc([128, 2], F32)
    LDrep = A.alloc([128, 2, 64], F32)
    kb.dma('sp', LD2[0:64, :], P['s5_log_dt'].rearrange("(pr gi) -> pr gi", gi=2), [], ['LD2'], kb.dsem())
    _cp(kb, 'dve', LDrep[0:64], bc(LD2[0:64, :], 2, 64), ['LD2'], ['LDrep'])
    _tr(kb, core.bank(6)[:, 0:64], LDrep[0:64].rearrange("r g p -> r (g p)"), idf[0:64, 0:64], ['LDrep', 'ident_f'], [('ps', 6)])
    _act(kb, DT, core.bank(6)[:, 0:64], AF.Exp, [('ps', 6)], ['DT'])
    s['R'] = kb.sb([128, 64], F32, name="s5R")
    s['THK'] = kb.sb([128, 64], F32, name="s5THK")
    R, THK = s['R'], s['THK']
    TH = A.alloc([128, 64], F32)
    _tt(kb, R[:], LR, DT, ALU.mult, ['LR', 'DT'], ['R'])
    _act(kb, R[:], R[:], AF.Exp, ['R'], ['R'])
    _tt(kb, TH, LI, DT, ALU.mult, ['LI', 'DT'], ['TH'])
    _ts(kb, THK[:], TH, float(1.0 / (2 * np.pi)), None, ALU.mult, None, ['TH'], ['THK'])
    Ki = A.alloc([128, 64], I32)
    FR = A.alloc([128, 64], F32)
    AB = A.alloc([128, 64], F32)
    CS = A.alloc([128, 64], F32)
    SN = A.alloc([128, 64], F32)
    _cp(kb, 'dve', Ki, THK[:], ['THK'], ['Ki'])
    _tt(kb, FR, THK[:], Ki, ALU.subtract, ['THK', 'Ki'], ['FR'])
    _stt(kb, AB, FR, -1.0, FR, ALU.mult, ALU.max, ['FR'], ['AB'])
    _act(kb, SN, FR, AF.Sin, ['FR'], ['SN'], scale=TWO_PI)
    _act(kb, CS, AB, AF.Sin, ['AB', 'halfpi'], ['CS'], scale=-TWO_PI, bias=s['halfpi'][:, 0:1])
    ABr = A.alloc([128, 64], F32)
    ABi = A.alloc([128, 64], F32)
    _tt(kb, ABr, R[:], CS, ALU.mult, ['R', 'CS'], ['ABr'])
    _ts(kb, ABr, ABr, -1.0, None, ALU.add, None, ['ABr'], ['ABr'])
    _tt(kb, ABi, R[:], SN, ALU.mult, ['R', 'SN'], ['ABi'])
    DEN = A.alloc([128, 64], F32)
    T1 = A.alloc([128, 64], F32)
    T2 = A.alloc([128, 64], F32)
    CFr = A.alloc([128, 64], F32)
    CFi = A.alloc([128, 64], F32)
    _tt(kb, DEN, LR, LR, ALU.mult, ['LR'], ['DEN'])
    _tt(kb, T1, LI, LI, ALU.mult, ['LI'], ['T1'])
    _tt(kb, DEN, DEN, T1, ALU.add, ['DEN', 'T1'], ['DEN'])
    kb.op('dve', lambda e: e.reciprocal(out=DEN, in_=DEN), ['DEN'], ['DEN'])
    _tt(kb, T1, ABr, LR, ALU.mult, ['ABr', 'LR'], ['T1'])
    _tt(kb, T2, ABi, LI, ALU.mult, ['ABi', 'LI'], ['T2'])
    _tt(kb, T1, T1, T2, ALU.add, ['T1', 'T2'], ['T1'])
    _tt(kb, CFr, T1, DEN, ALU.mult, ['T1', 'DEN'], ['CFr'])
    _tt(kb, T1, ABi, LR, ALU.mult, ['ABi', 'LR'], ['T1'])
    _tt(kb, T2, ABr, LI, ALU.mult, ['ABr', 'LI'], ['T2'])
    _tt(kb, T1, T1, T2, ALU.subtract, ['T1', 'T2'], ['T1'])
    _tt(kb, CFi, T1, DEN, ALU.mult, ['T1', 'DEN'], ['CFi'])
    BN = A.alloc([128, 2048], F32)
    SLs = {}
    for nm, kind in (('s5_b_re', 'b'), ('s5_b_im', 'b'), ('s5_c_re', 'c'), ('s5_c_im', 'c')):
        dst = A.alloc([128, 64, 16], F32)
        SLs[nm] = dst
        if kind == 'b':
            src = P[nm].rearrange("(pr gi) p c -> pr (gi p c)", gi=2)
            v4 = BN[0:64, :].rearrange("r (gi p c) -> r gi p c", gi=2, p=64)
            kb.dma('sp', BN[0:64, :], src, [], ['BN'], kb.dsem())
        else:
            src4 = P[nm].rearrange("(pr gi) c p -> pr gi c p", gi=2)
            d4 = BN[0:64, :].rearrange("r (c gi p) -> r c gi p", c=16, gi=2)
            for gi in range(2):
                kb.dma('sp', d4[:, :, gi, :], src4[:, gi, :, :], [], ['BN'], kb.dsem())
        for c0 in (0, 8):
            for cc in range(8):
                c = c0 + cc
                in_ = v4[:, :, :, c] if kind == 'b' else BN[0:64, c * 128:(c + 1) * 128]
                kb.op('pe', lambda e: e.transpose(out=core.bank(c0 // 8)[:, cc * 64:(cc + 1) * 64], in_=in_, identity=idf[0:64, 0:64]),
                      ['BN', 'ident_f'], [('ps', c0 // 8)])
            _cp(kb, 'act', dst[:, :, c0:c0 + 8].rearrange("p pr c -> p c pr"), core.bank(c0 // 8)[:, :].rearrange("p (c pr) -> p c pr", c=8), [('ps', c0 // 8)], [nm])
    Bre, Bim, Cre, Cim = SLs['s5_b_re'], SLs['s5_b_im'], SLs['s5_c_re'], SLs['s5_c_im']
    BBr = A.alloc([128, 64, 16], F32)
    BBi = A.alloc([128, 64, 16], F32)
    TT = A.alloc([128, 64, 16], F32)
    _tt(kb, BBr, Bre, bc(CFr, 2, 16), ALU.mult, ['s5_b_re', 'CFr'], ['BBr'])
    _tt(kb, TT, Bim, bc(CFi, 2, 16), ALU.mult, ['s5_b_im', 'CFi'], ['TT'])
    _tt(kb, BBr, BBr, TT, ALU.subtract, ['BBr', 'TT'], ['BBr'])
    _tt(kb, BBi, Bim, bc(CFr, 2, 16), ALU.mult, ['s5_b_im', 'CFr'], ['BBi'])
    _tt(kb, TT, Bre, bc(CFi, 2, 16), ALU.mult, ['s5_b_re', 'CFi'], ['TT'])
    _tt(kb, BBi, BBi, TT, ALU.add, ['BBi', 'TT'], ['BBi'])
    s['WB'] = [kb.sb([128, 16, 128], BF16, name=f"s5WB{i}") for i in range(2)]
    s['WC'] = [kb.sb([128, 64, 2, 16], BF16, name=f"s5WC{i}") for i in range(3)]
    LT = A.alloc([128, 64, 2, 16], F32)
    for ri, BB in enumerate((BBr, BBi)):
        kBB = 'BBr' if ri == 0 else 'BBi'
        for gi in range(2):
            _ts(kb, LT[:, :, gi, :], BB, s['sel'][:, gi:gi + 1], None, ALU.mult, None, [kBB, 'sel'], ['LT'])
        for k4 in range(4):
            for kk in range(4):
                k = k4 * 4 + kk
                kb.op('pe', lambda e: e.transpose(out=core.bank(2 + k4 % 2)[:, kk * 128:(kk + 1) * 128], in_=LT[:, 4 * k:4 * k + 4, :, :].rearrange("p q g c -> p (q g c)"), identity=idf[:]),
                      ['LT', 'ident_f'], [('ps', 2 + k4 % 2)])
            _cp(kb, 'act', s['WB'][ri][:, k4 * 4:k4 * 4 + 4, :], core.bank(2 + k4 % 2)[:, :].rearrange("p (k n) -> p k n", k=4), [('ps', 2 + k4 % 2)], ['WB'])
    for gi in range(2):
        _ts(kb, s['WC'][0][:, :, gi, :], Cre, s['sel'][:, gi:gi + 1], None, ALU.mult, None, ['s5_c_re', 'sel'], ['WC'])
        _ts(kb, s['WC'][1][:, :, gi, :], Cim, s['sel'][:, gi:gi + 1], -1.0, ALU.mult, ALU.mult, ['s5_c_im', 'sel'], ['WC'])
        _ts(kb, s['WC'][2][:, :, gi, :], Cre, s['sel'][:, gi:gi + 1], -1.0, ALU.mult, ALU.mult, ['s5_c_re', 'sel'], ['WC'])
    KK = [A.alloc([128, 512], F32) for _ in range(2)]
    KI = [A.alloc([128, 512], I32) for _ in range(2)]
    TB2 = [A.alloc([128, 2, 512], F32) for _ in range(2)]
    tds = [kb.dsem() for _ in range(2)]
    for pr in range(64):
        i = pr % 2
        _ts(kb, KK[i], s['iota1'], THK[:, pr:pr + 1], None, ALU.mult, None, ['iota1', 'THK'], [('KK', i)])
        _cp(kb, 'dve', KI[i], KK[i], [('KK', i)], [('KI', i)])
        _tt(kb, KK[i], KK[i], KI[i], ALU.subtract, [('KK', i), ('KI', i)], [('KK', i)])
        _act(kb, TB2[i][:, 1, :], KK[i], AF.Sin, [('KK', i)], [('TB2', i)], scale=TWO_PI)
        _stt(kb, KK[i], KK[i], -1.0, KK[i], ALU.mult, ALU.max, [('KK', i)], [('KK', i)])
        _act(kb, TB2[i][:, 0, :], KK[i], AF.Sin, [('KK', i), 'halfpi'], [('TB2', i)], scale=-TWO_PI, bias=s['halfpi'][:, 0:1])
        kb.dma('sp', tbl_dram[pr], TB2[i], [('TB2', i)], ['tbl_dram'], tds[i])
    s['XC'] = kb.sb([128, 64, 2], F32, name="s5XC")
    kb.op('dve', lambda e: e.memset(s['XC'][:], 0.0), writes=['XC'])
    s['tds'] = [kb.dsem() for _ in range(2)]
    s['ti'] = 0
    return s


def gelu_tanh(kb, dst, Y, X2, keys_in, key_out):
    _act(kb, X2, Y, AF.Square, keys_in, ['gX2'])
    _ts(kb, X2, X2, 0.044715, 1.0, ALU.mult, ALU.add, ['gX2'], ['gX2'])
    _tt(kb, X2, X2, Y, ALU.mult, ['gX2'] + keys_in, ['gX2'])
    _act(kb, X2, X2, AF.Sigmoid, ['gX2'], ['gX2'], scale=1.5957691216057308)
    _tt(kb, dst, Y, X2, ALU.mult, ['gX2'] + keys_in, [key_out])


def s5_phase(core, s, A, P, X, HT, cfg, tbl_dram, XCs=None):
    kb = core.kb
    N, nseq, L = cfg['N'], cfg['nseq'], cfg['L']
    TB = N // 128
    PB = 2 if nseq == 1 else 1
    HTk = [('HT', t) for t in range(4)]
    G = A.alloc([128, 16, N], BF16)
    TBLs = [A.alloc([128, PB, 2, L], F32) for _ in range(2)]
    nm10 = ('Cr', 'Ci', 'Zr', 'Zi', 'T1', 'T2', 'S3', 'S4')
    B = {n: A.alloc([128, PB, N], F32) for n in nm10}
    P1, P2, P3, P4 = (A.alloc([128, PB, N], BF16) for _ in range(4))
    Y = A.alloc([128, N], F32)
    X2 = A.alloc([128, N], F32)
    RSEG = A.alloc([128, PB, N], F32)
    Cr, Ci, Zr, Zi, T1, T2, S3, S4 = (B[n] for n in nm10)
    v4 = lambda ap: ap.rearrange("p a (s l) -> p a s l", s=nseq)
    v3 = lambda ap: ap.rearrange("p (s l) -> p s l", s=nseq)
    ngrp = 64 // PB
    for gidx in range(ngrp):
        pr0 = gidx * PB
        ti = gidx % 2
        TBL = TBLs[ti]
        kT = ('TBL', ti)
        if L == 512:
            kb.dma('sp', TBL.rearrange("p a r l -> p a (r l)"), tbl_dram[pr0:pr0 + PB].rearrange("a p r l -> p a (r l)"), ['tbl_dram'], [kT], s['tds'][ti])
        else:
            for a in range(PB):
                kb.dma('sp', TBL[:, a, :, :], tbl_dram[pr0 + a][:, :, 0:L], ['tbl_dram'], [kT], s['tds'][ti])
        COS = bc(TBL[:, :, 0, :], 2, nseq)
        SIN = bc(TBL[:, :, 1, :], 2, nseq)
        nbk = (PB * N + 511) // 512
        for a in range(PB):
            pr = pr0 + a
            k, q = pr // 4, pr % 4
            qs = slice(32 * q, 32 * q + 32)
            col = a * N
            bre = col // 512
            bim = nbk + col // 512
            _mm(kb, core.bank(bre)[:, col % 512:col % 512 + N], s['WB'][0][qs, k, :], HT[qs, k, :], ['WB'] + HTk, [('ps', bre)], tp=(32 * q, 0))
            _mm(kb, core.bank(bim)[:, col % 512:col % 512 + N], s['WB'][1][qs, k, :], HT[qs, k, :], ['WB'] + HTk, [('ps', bim)], tp=(32 * q, 0))
        for bk in range(nbk):
            w = min(512, PB * N - bk * 512)
            _cp(kb, 'act', S3.rearrange("p a n -> p (a n)")[:, bk * 512:bk * 512 + w], core.bank(bk)[:, :w], [('ps', bk)], ['S3'])
            _cp(kb, 'act', S4.rearrange("p a n -> p (a n)")[:, bk * 512:bk * 512 + w], core.bank(nbk + bk)[:, :w], [('ps', nbk + bk)], ['S4'])
        pre, pim = v4(S3), v4(S4)
        _tt(kb, v4(T1), pre, COS, ALU.mult, ['S3', kT], ['T1'])
        _tt(kb, v4(T2), pim, SIN, ALU.mult, ['S4', kT], ['T2'])
        _tt(kb, Cr, T1, T2, ALU.add, ['T1', 'T2'], ['Cr'])
        _tt(kb, v4(T1), pim, COS, ALU.mult, ['S4', kT], ['T1'])
        _tt(kb, v4(T2), pre, SIN, ALU.mult, ['S3', kT], ['T2'])
        _tt(kb, Ci, T1, T2, ALU.subtract, ['T1', 'T2'], ['Ci'])
        for a in range(PB):
            pr = pr0 + a
            rcol = s['R'][:, pr:pr + 1]
            if nseq == 1:
                XC = s['XC']
                kb.op('dve', lambda e: e.tensor_tensor_scan(out=Zr[:, a, :], data0=rcol.broadcast_to([128, N]), data1=Cr[:, a, :], initial=XC[:, pr, 0:1], op0=ALU.mult, op1=ALU.add), ['Cr', 'R', ('XC', pr)], ['Zr'])
                kb.op('dve', lambda e: e.tensor_tensor_scan(out=Zi[:, a, :], data0=rcol.broadcast_to([128, N]), data1=Ci[:, a, :], initial=XC[:, pr, 1:2], op0=ALU.mult, op1=ALU.add), ['Ci', 'R', ('XC', pr)], ['Zi'])
            else:
                c3r, c3i = v3(Cr[:, a, :]), v3(Ci[:, a, :])
                _stt(kb, c3r[:, :, 0], XCs[:, pr, :, 0], rcol, c3r[:, :, 0], ALU.mult, ALU.add, [('XCs', pr), 'R', 'Cr'], ['Cr'])
                _stt(kb, c3i[:, :, 0], XCs[:, pr, :, 1], rcol, c3i[:, :, 0], ALU.mult, ALU.add, [('XCs', pr), 'R', 'Ci'], ['Ci'])
                _ts(kb, RSEG[:, a, :], s['segst'][:, :N], rcol, None, ALU.mult, None, ['segst', 'R'], ['RSEG'])
                kb.op('dve', lambda e: e.tensor_tensor_scan(out=Zr[:, a, :], data0=RSEG[:, a, :], data1=Cr[:, a, :], initial=0.0, op0=ALU.mult, op1=ALU.add), ['Cr', 'RSEG'], ['Zr'])
                kb.op('dve', lambda e: e.tensor_tensor_scan(out=Zi[:, a, :], data0=RSEG[:, a, :], data1=Ci[:, a, :], initial=0.0, op0=ALU.mult, op1=ALU.add), ['Ci', 'RSEG'], ['Zi'])
        _tt(kb, v4(P1), v4(Zr), COS, ALU.mult, ['Zr', kT], ['P1'])
        _tt(kb, v4(P2), v4(Zi), SIN, ALU.mult, ['Zi', kT], ['P2'])
        _tt(kb, v4(P3), v4(Zr), SIN, ALU.mult, ['Zr', kT], ['P3'])
        _tt(kb, v4(P4), v4(Zi), COS, ALU.mult, ['Zi', kT], ['P4'])
        for a in range(PB):
            pr = pr0 + a
            k, q = pr // 4, pr % 4
            qs = slice(32 * q, 32 * q + 32)
            if nseq == 1:
                xcr, xci = s['XC'][:, pr, 0:1], s['XC'][:, pr, 1:2]
                kXC = ('XC', pr)
            else:
                xcr, xci = XCs[:, pr, :, 0], XCs[:, pr, :, 1]
                kXC = ('XCs', pr)
            cl, sn = TBL[:, a, 0, L - 1:L], TBL[:, a, 1, L - 1:L]
            zrl, zil = v3(Zr[:, a, :])[:, :, L - 1], v3(Zi[:, a, :])[:, :, L - 1]
            t1 = T1[:, a, 0:nseq]
            t2 = T2[:, a, 0:nseq]
            _ts(kb, t1, zil, sn, None, ALU.mult, None, ['Zi', kT], ['T1'])
            _ts(kb, t2, zrl, sn, None, ALU.mult, None, ['Zr', kT], ['T2'])
            _stt(kb, xcr, zrl, cl, t1, ALU.mult, ALU.subtract, ['Zr', kT, 'T1'], [kXC])
            _stt(kb, xci, zil, cl, t2, ALU.mult, ALU.add, ['Zi', kT, 'T2'], [kXC])
            yb = 4 + (k % 2)
            yo = core.bank(yb)[qs, :N]
            wcr = s['WC'][0][:, pr, :, :].rearrange("p g c -> p (g c)")
            wci = s['WC'][1][:, pr, :, :].rearrange("p g c -> p (g c)")
            _mm(kb, yo, wcr, P1[:, a, :], ['WC', 'P1'], [('ps', yb)], start=True, stop=False, tp=(0, 32 * q))
            _mm(kb, yo, s['WC'][2][:, pr, :, :].rearrange("p g c -> p (g c)"), P2[:, a, :], ['WC', 'P2'], [('ps', yb)], start=False, stop=False, tp=(0, 32 * q))
            _mm(kb, yo, wci, P3[:, a, :], ['WC', 'P3'], [('ps', yb)], start=False, stop=False, tp=(0, 32 * q))
            _mm(kb, yo, wci, P4[:, a, :], ['WC', 'P4'], [('ps', yb)], start=False, stop=True, tp=(0, 32 * q))
            if q == 3:
                _stt(kb, Y, HT[:, k, :], s['dcol'][:, k:k + 1], core.bank(yb)[:, :N], ALU.mult, ALU.add, HTk + ['dcol', ('ps', yb)], ['Y'])
                gelu_tanh(kb, G[:, k, :], Y, X2, ['Y'], 'G')
    w_glu = P['w_glu']
    SGt = A.alloc([128, 512], F32)
    for cb in range(4):
        for half in range(2):
            b0 = half * 4
            c0 = half * 2048 + cb * 512
            for k in range(KT):
                wk, wt = core.wload(w_glu[k * 128:(k + 1) * 128, c0:c0 + 512], 512)
                for tb in range(TB):
                    _mm(kb, core.bank(b0 + tb)[:, :512], G[:, k, tb * 128:(tb + 1) * 128], wt[:, :512], [wk, 'G'], [('ps', b0 + tb)], start=(k == 0), stop=(k == KT - 1))
        for tb in range(TB):
            _act(kb, SGt, core.bank(4 + tb)[:, :512], AF.Sigmoid, [('ps', 4 + tb)], ['SGt'])
            _tt(kb, SGt, core.bank(tb)[:, :512], SGt, ALU.mult, [('ps', tb), 'SGt'], ['SGt'])
            _tt(kb, X[:, tb, cb * 512:(cb + 1) * 512], X[:, tb, cb * 512:(cb + 1) * 512], SGt, ALU.add, ['SGt', ('X', tb)], [('X', tb)])


BF = ml_dtypes.bfloat16
BF = ml_dtypes.bfloat16
PAST_LEN = 16384
SEQ = 2048
DEC_SEQ = 8


def make_consts():
    c = {}
    c['idb'] = np.eye(128).astype(BF)
    c['idf'] = np.eye(128, dtype=np.float32)
    mask = np.zeros((128, 2, 3, 64), np.float32)
    j = np.arange(64)[:, None]
    i = np.arange(64)[None, :]
    for v, seg in enumerate((64, 8)):
        same = (j // seg) == (i // seg)
        incl = same & (i >= j)
        strict = same & (i > j)
        for half in range(2):
            mask[half * 64:(half + 1) * 64, v, 0] = incl
            mask[half * 64:(half + 1) * 64, v, 1] = strict
            mask[half * 64:(half + 1) * 64, v, 2] = same
    c['c_mask'] = mask
    segmask = np.zeros((128, 8, 64), np.float32)
    for s in range(8):
        segmask[:, s, s * 8:(s + 1) * 8] = 1
    c['c_segmask'] = segmask.astype(BF)
    rowmask = np.zeros((128, 8), np.float32)
    for p in range(128):
        rowmask[p, (p % 64) // 8] = 1
    c['c_rowmask'] = rowmask.astype(BF)
    lg = np.log1p(-np.exp2(-5.0 - np.arange(4, dtype=np.float64)))
    dect = np.zeros((128, 2, 4, 64), np.float64)
    qdec = np.zeros((128, 2, 4, 64), np.float64)
    kdec = np.zeros((128, 2, 4), np.float64)
    for v, seg in enumerate((64, 8)):
        same = (j // seg) == (i // seg)
        incl = same & (i >= j)
        for h in range(4):
            d = np.where(incl, np.exp(lg[h] * np.where(incl, (i - j), 0)), 0.0)
            dect[0:64, v, h] = d
            dect[64:128, v, h] = d
            qdec[:, v, h, :] = np.exp(lg[h] * ((np.arange(64) % seg) + 1.0))[None, :]
            kdec[:, v, h] = np.exp(lg[h] * (seg - 1.0 - (np.arange(128) % seg)))
    c['c_dect'] = dect.astype(np.float32)
    c['c_qdec'] = qdec.astype(np.float32)
    c['c_kdec'] = kdec.astype(np.float32)
    half = 64
    inv = (np.float32(10000.0) ** (-(np.arange(half, dtype=np.float32)) / np.float32(half))).astype(np.float32)
    pos = np.concatenate([np.arange(SEQ, dtype=np.float32), np.tile(PAST_LEN + np.arange(DEC_SEQ, dtype=np.float32), 16)])
    ang = (pos[:, None] * inv[None, :]).astype(np.float32).astype(np.float64)
    rot = np.stack([np.cos(ang), np.sin(ang), np.cos(ang) * 128.0 ** -0.5, np.sin(ang) * 128.0 ** -0.5]).astype(np.float32)
    c['c_rot'] = rot
    return c


def make_consts_s5(c):
    sel = np.zeros((128, 2), np.float32)
    sel[:64, 0] = 1
    sel[64:, 1] = 1
    c['c_sel'] = sel
    c['c_iota1'] = np.tile(np.arange(1, 513, dtype=np.float32)[None, :], (128, 1))
    segst = np.ones((128, 128), np.float32)
    segst[:, ::8] = 0
    c['c_segst'] = segst
    return c


def rmsnorm_TA(core, A, X, TB, N, wnorm_dram, HT):
    kb = core.kb
    if not hasattr(core, 'nrm2'):
        core.nrm2 = dict(ssq=kb.sb([128, 8], F32, name="ssq"), rstd=kb.sb([128, 8], F32, name="rstd"), wds=kb.dsem())
    n = core.nrm2
    wbc = A.alloc([128, D], F32)
    xsb = [A.alloc([128, D], BF16) for _ in range(2)]
    kb.dma('sp', wbc, wnorm_dram.partition_broadcast(128), [], ['wbc'], n['wds'])
    ssq, rstd = n['ssq'], n['rstd']
    for tb in range(TB):
        xs = xsb[tb % 2]
        kxs = ('xs', tb % 2)
        _act(kb, xs, X[:, tb, :], AF.Square, [('X', tb)], [kxs, ('ssq', tb)], accum_out=ssq[:, tb:tb + 1])
        _act(kb, rstd[:, tb:tb + 1], ssq[:, tb:tb + 1], AF.Sqrt, [('ssq', tb)], [('rstd', tb)], scale=1.0 / D, bias=core.eps_t[:, 0:1])
        kb.op('dve', lambda e: e.reciprocal(out=rstd[:, tb:tb + 1], in_=rstd[:, tb:tb + 1]), [('rstd', tb)], [('rstd', tb)])
        _stt(kb, xs, X[:, tb, :], rstd[:, tb:tb + 1], wbc, ALU.mult, ALU.mult, [('X', tb), ('rstd', tb), 'wbc'], [kxs])
        for half in range(2):
            bk = 6 + half
            pt = core.bank(bk)[:].bitcast(BF16)
            for kk in range(8):
                k = half * 8 + kk
                _tr(kb, pt[:, kk * 128:(kk + 1) * 128], xs[:, k * 128:(k + 1) * 128], core.ident_bf[:], [kxs, 'ident_bf'], [('ps', bk)])
            _act(kb, HT[:, half * 8:half * 8 + 8, tb * 128:(tb + 1) * 128], pt.rearrange("p (k t) -> p k t", k=8), AF.Copy, [('ps', bk)], [('HT', tb)])


def final_norm(core, A, X, TB, w_dram, out_view, ods):
    kb = core.kb
    n = core.nrm2
    wbc = A.alloc([128, D], F32)
    junk = A.alloc([128, D], BF16)
    kb.dma('sp', wbc, w_dram.partition_broadcast(128), [], ['wbc'], n['wds'])
    ssq, rstd = n['ssq'], n['rstd']
    for tb in range(TB):
        _act(kb, junk, X[:, tb, :], AF.Square, [('X', tb)], ['junk', ('ssq', tb)], accum_out=ssq[:, tb:tb + 1])
        _act(kb, rstd[:, tb:tb + 1], ssq[:, tb:tb + 1], AF.Sqrt, [('ssq', tb)], [('rstd', tb)], scale=1.0 / D, bias=core.eps_t[:, 0:1])
        kb.op('dve', lambda e: e.reciprocal(out=rstd[:, tb:tb + 1], in_=rstd[:, tb:tb + 1]), [('rstd', tb)], [('rstd', tb)])
        _stt(kb, X[:, tb, :], X[:, tb, :], rstd[:, tb:tb + 1], wbc, ALU.mult, ALU.mult, [('X', tb), ('rstd', tb), 'wbc'], [('X', tb)])
    kb.dma('sp', out_view, X[:, 0:TB, :], [('X', tb) for tb in range(TB)], [], ods)


def load_cols4(core, src2d, R, ntile, dst3, key):
    kb = core.kb
    if not hasattr(core, 'lc4'):
        core.lc4 = dict(tmp=[kb.sb([128, 512], F32, name=f"lc4tmp{i}") for i in range(2)], ds=[kb.dsem() for _ in range(2)], ods=[kb.dsem() for _ in range(2)], i=0)
    i = core.lc4['i'] % 2
    core.lc4['i'] += 1
    tmp = core.lc4['tmp'][i]
    kb.dma('sp', tmp[0:R, 0:ntile * 128], src2d, [], [('lc4', i)], core.lc4['ds'][i])
    for j in range(ntile):
        _tr(kb, core.bank(7)[:, j * R:(j + 1) * R], tmp[0:R, j * 128:(j + 1) * 128], core.ident_f[0:R, 0:R], [('lc4', i), 'ident_f'], [('ps', 7)])
    _cp(kb, 'dve', dst3, core.bank(7)[:, 0:ntile * R].rearrange("p (t r) -> p t r", t=ntile), [('ps', 7)], [key])


def store_cols4(core, srcs, R, dst2d, keys):
    kb = core.kb
    if not hasattr(core, 'lc4'):
        core.lc4 = dict(tmp=[kb.sb([128, 512], F32, name=f"lc4tmp{i}") for i in range(2)], ds=[kb.dsem() for _ in range(2)], ods=[kb.dsem() for _ in range(2)], i=0)
    i = core.lc4['i'] % 2
    core.lc4['i'] += 1
    tmp = core.lc4['tmp'][i]
    n = len(srcs)
    for j, sap in enumerate(srcs):
        _tr(kb, core.bank(7)[0:R, j * 128:(j + 1) * 128], sap, core.ident_f[:], list(keys) + ['ident_f'], [('ps', 7)])
    _cp(kb, 'dve', tmp[0:R, 0:n * 128], core.bank(7)[0:R, 0:n * 128], [('ps', 7)], [('lc4', i)])
    kb.dma('sp', dst2d, tmp[0:R, 0:n * 128], [('lc4', i)], [], core.lc4['ods'][i])


W_NAMES = ['norm_mix_w', 'norm_ffn_w', 'norm_final_w', 'w_in', 'gdn_conv_w', 'gdn_a_log', 'gdn_dt_bias', 'gdn_norm_w', 'ret_gn_w', 'ret_gn_b', 'w_out',
           's5_lam_re', 's5_lam_im', 's5_log_dt', 's5_b_re', 's5_b_im', 's5_c_re', 's5_c_im', 's5_d', 'w_glu', 'w_up', 'ffn_conv_w', 'ffn_conv_b', 'w_down']
W_SHAPES = dict(norm_mix_w=[2, D], norm_ffn_w=[2, D], norm_final_w=[D], w_in=[D, DIN], gdn_conv_w=[4, 3072], gdn_a_log=[8], gdn_dt_bias=[8], gdn_norm_w=[128],
                ret_gn_w=[1024], ret_gn_b=[1024], w_out=[D, D], s5_lam_re=[128, 64], s5_lam_im=[128, 64], s5_log_dt=[128], s5_b_re=[128, 64, 16], s5_b_im=[128, 64, 16],
                s5_c_re=[128, 16, 64], s5_c_im=[128, 16, 64], s5_d=[D], w_glu=[D, 2 * D], w_up=[2, D, 2 * DFF], ffn_conv_w=[2, 3, 2 * DFF], ffn_conv_b=[2, 2 * DFF], w_down=[2, DFF, D])
IN_SHAPES = dict(x_p=[2048, D], x_s=[128, D], state_gdn=[16, 8, 128, 128], state_gcb=[16, 3, 3072], state_ret=[16, 4, 128, 256], st_re=[16, 128, 64], st_im=[16, 128, 64],
                 state_fcb=[2, 16, 2, 2 * DFF])
OUT_SHAPES = dict(y_p=[2048, D], y_s=[128, D], o_gdn_p=[8, 128, 128], o_gdn_s=[16, 8, 128, 128], o_gcb_p=[3, 3072], o_gcb_s=[16, 3, 3072], o_ret_p=[4, 128, 256],
                  o_ret_s=[16, 4, 128, 256], o_s5r_p=[128, 64], o_s5r_s=[16, 128, 64], o_s5i_p=[128, 64], o_s5i_s=[16, 128, 64], o_fcb_p=[2, 2, 2 * DFF], o_fcb_s=[2, 16, 2, 2 * DFF])


def build_program(consts, n_tiles=5, dbg=False):
    nc = bass.Bass("TRN2", target_bir_lowering=False)

    def din(name, shape, dt=F32):
        return nc.dram_tensor(name, list(shape), dt, kind="ExternalInput").ap()

    def dout(name, shape, dt=F32):
        return nc.dram_tensor(name, list(shape), dt, kind="ExternalOutput").ap()
    P = {}
    for k in W_NAMES:
        P[k] = din(k, W_SHAPES[k])
    for k, shp in IN_SHAPES.items():
        P[k] = din(k, shp)
    for k, v in consts.items():
        P[k] = din(k, v.shape, BF16 if v.dtype == BF else F32)
    for k, shp in OUT_SHAPES.items():
        P[k] = dout(k, shp)
    tbl_dram = nc.dram_tensor("tbl_scratch", [64, 128, 2, 512], F32, kind="Internal").ap()
    if dbg:
        for i in range(4):
            P[f'dbg{i}'] = dout(f'dbg{i}', [512, D])
        P['dbgG'] = dout('dbgG', [128, 16 * 512], BF16)
        P['dbgT0'] = dout('dbgT0', [4, 128, 2, 512])
        P['dbgR'] = dout('dbgR', [128, 128])
        P['dbgT1'] = dout('dbgT1', [4, 128, 2, 512])

    def dump(kb, X, name, dds):
        if dbg:
            kb.barrier()
            kb.dma('sp', P[name].rearrange("(tb p) d -> p tb d", p=128), X[:, 0:4, :], [], [], dds)
            kb.barrier()
    P0 = dict(P)
    for k in ('w_in', 'gdn_conv_w', 'gdn_a_log', 'gdn_dt_bias', 'gdn_norm_w', 'ret_gn_w', 'ret_gn_b', 'w_out', 's5_lam_re', 's5_lam_im', 's5_log_dt',
              's5_b_re', 's5_b_im', 's5_c_re', 's5_c_im', 's5_d', 'w_glu'):
        P0[k] = P[k]
    with contextlib.ExitStack() as es:
        kb = KB(nc, es)
        core = Core(kb, P['idb'], P['idf'])
        core.eps_t = kb.sb([128, 1], F32, name="eps")
        kb.op('dve', lambda e: e.memset(core.eps_t[:], EPS), writes=['eps'])
        m = mixer0_setup(core, P0)
        X = kb.sb([128, 4, D], F32, name="X")
        HT = kb.sb([128, KT, 512], BF16, name="HT")
        cws = [kb.sb([128, 3, 2 * FT], F32, name=f"cws{l}") for l in range(2)]
        cbs = [kb.sb([128, 2 * FT], F32, name=f"cbs{l}") for l in range(2)]
        CBp = [kb.sb([128, 2 * FT, 1, 2], F32, name=f"CBp{l}") for l in range(2)]
        for l in range(2):
            load_cols(core, P['ffn_conv_w'][l].rearrange("j (t p) -> (j t) p", p=128), 3 * 2 * FT, cws[l][:].rearrange("p j t -> p (j t)"), 'convw')
            load_cols(core, P['ffn_conv_b'][l].rearrange("(t p) -> t p", p=128), 2 * FT, cbs[l][:], 'convw')
            kb.op('dve', lambda e: e.memset(CBp[l][:], 0.0), writes=[('CB', ft) for ft in range(2 * FT)])
        rem = nc.sbuf_bytes_remaining
        A = Arena(kb, rem - 28000)
        WBF = convert_weights(core, A, nc, P)
        kb.barrier()
        A.reset()
        P0['w_in_b'] = WBF['w_in']
        P0['w_glu'] = WBF['w_glu']
        s = s5_setup(core, P0, A, tbl_dram)
        kb.barrier()
        A.reset()
        if dbg:
            dd = kb.dsem()
            kb.dma('sp', P['dbgT0'], tbl_dram[48:52], [], [], dd)
            kb.dma('sp', P['dbgR'][:, 0:64], s['R'][:], [], [], dd)
            kb.dma('sp', P['dbgR'][:, 64:128], s['THK'][:], [], [], dd)
            kb.barrier()
        xds = kb.dsem()
        ods = kb.dsem()
        sds = kb.dsem()
        for tile in range(n_tiles):
            is_s = (tile == 4)
            N = 128 if is_s else 512
            TB = N // 128
            nseq, L = (16, 8) if is_s else (1, 512)
            last_p = (tile == 3)
            if is_s:
                xin = P['x_s'].rearrange("(tb p) d -> p tb d", p=128)
                yout = P['y_s'].rearrange("(tb p) d -> p tb d", p=128)
            else:
                xin = P['x_p'][tile * 512:(tile + 1) * 512, :].rearrange("(tb p) d -> p tb d", p=128)
                yout = P['y_p'][tile * 512:(tile + 1) * 512, :].rearrange("(tb p) d -> p tb d", p=128)
            kb.dma('sp', X[:, 0:TB, :], xin, [], [('X', tb) for tb in range(TB)], xds)
            HTv = HT[:, :, 0:N]
            rmsnorm_TA(core, A, X, TB, N, P['norm_mix_w'][0], HTv)
            kb.barrier()
            A.reset()
            mixedT = A.alloc([128, 16, N], BF16)
            if is_s:
                CBq = A.alloc([128, 24, 16, 3], F32)
                for c4 in range(6):
                    load_cols4(core, P['state_gcb'].rearrange("s j c -> (s j) c")[:, c4 * 512:(c4 + 1) * 512], 48, 4, CBq[:, c4 * 4:c4 * 4 + 4, :, :].rearrange("p t s j -> p t (s j)"), ('CBq', 0))
                kb.barrier()
            else:
                CBq = m['CBq'][:].rearrange("p c (s j) -> p c s j", s=1)
            base = A.off
            NS = 128 if is_s else 256
            for sub in range(N // NS):
                A.off = base
                cfg = dict(NS=NS, nseq=(16 if is_s else 1), L=(8 if is_s else NS), nseg=(8 if is_s else 1), v=(1 if is_s else 0),
                           rot0=(2048 if is_s else tile * 512 + sub * NS), seq0=0)
                mixer0_sub(core, m, A, P0, HTv[:, :, sub * NS:(sub + 1) * NS], cfg, mixedT, sub * NS, CBq)
                kb.barrier()
            wout_phase(core, X, TB, mixedT, WBF['w_out'])
            if tile == 0:
                dump(kb, X, 'dbg0', sds)
            if is_s or last_p:
                R = nseq * 3
                dst = (P['o_gcb_s'].rearrange("s j c -> (s j) c") if is_s else P['o_gcb_p'])
                for c4 in range(6):
                    store_cols4(core, [CBq[:, c4 * 4 + j, :, :].rearrange("p s j -> p (s j)") for j in range(4)], R, dst[:, c4 * 512:(c4 + 1) * 512], [('CBq', ct) for ct in range(24)])
            kb.barrier()
            A.reset()
            for layer in range(2):
                if layer == 1:
                    rmsnorm_TA(core, A, X, TB, N, P['norm_mix_w'][1], HTv)
                    kb.barrier()
                    A.reset()
                    XCs = None
                    if is_s:
                        XCs = A.alloc([128, 64, 16, 2], F32)
                        for sq in range(16):
                            for ri, nm in enumerate(('st_re', 'st_im')):
                                load_cols(core, P[nm][sq].rearrange("(pr gi) p -> pr (gi p)", gi=2), 64, XCs[:, :, sq, ri], 'XCs')
                    if is_s:
                        kb.barrier()
                    s5_phase(core, s, A, P0, X, HTv, dict(N=N, nseq=nseq, L=L), tbl_dram, XCs)
                    if is_s or last_p:
                        kb.barrier()
                    if tile == 0:
                        dump(kb, X, 'dbg2', sds)
                        if dbg:
                            kb.dma('sp', P['dbgG'], A.t[:, 0:4096].bitcast(BF16), [], [], sds)
                            kb.barrier()
                    if is_s:
                        for sq in range(16):
                            for ri, nm in enumerate(('o_s5r_s', 'o_s5i_s')):
                                store_cols4(core, [XCs[:, :, sq, ri]], 64, P[nm][sq].rearrange("(pr gi) p -> pr (gi p)", gi=2), ['XCs'])
                    elif last_p:
                        for ri, nm in enumerate(('o_s5r_p', 'o_s5i_p')):
                            store_cols4(core, [s['XC'][:, :, ri]], 64, P[nm].rearrange("(pr gi) p -> pr (gi p)", gi=2), ['XC'])
                    kb.barrier()
                    A.reset()
                rmsnorm_TA(core, A, X, TB, N, P['norm_ffn_w'][layer], HTv)
                kb.barrier()
                A.reset()
                HM = A.alloc([128, FT, N], BF16)
                U = [A.alloc([128, 3, nseq, L + 2], F32) for _ in range(2)]
                CV = A.alloc([128, 3, N], F32)
                SG = A.alloc([128, 3, N], F32)
                if is_s:
                    CB = A.alloc([128, 2 * FT, 16, 2], F32)
                    src = P['state_fcb'][layer].rearrange("s j c -> (s j) c")
                    for c4 in range(22):
                        load_cols4(core, src[:, c4 * 512:(c4 + 1) * 512], 32, 4, CB[:, c4 * 4:c4 * 4 + 4, :, :].rearrange("p t s j -> p t (s j)"), ('CB', c4))
                    kb.barrier()
                else:
                    CB = CBp[layer]
                ffn_phase(core, X, TB, N, nseq, L, HTv, HM, WBF[f'w_up{layer}'], cws[layer], cbs[layer], WBF[f'w_down{layer}'], CB, U, CV, SG)
                if tile == 0:
                    dump(kb, X, 'dbg1' if layer == 0 else 'dbg3', sds)
                if is_s or last_p:
                    R = nseq * 2
                    dst = (P['o_fcb_s'][layer].rearrange("s j c -> (s j) c") if is_s else P['o_fcb_p'][layer])
                    for c4 in range(22):
                        store_cols4(core, [CB[:, c4 * 4 + j, :, :].rearrange("p s j -> p (s j)") for j in range(4)], R, dst[:, c4 * 512:(c4 + 1) * 512], [('CB', ft) for ft in range(2 * FT)])
                kb.barrier()
                A.reset()
            final_norm(core, A, X, TB, P['norm_final_w'], yout, ods)
            kb.barrier()
            A.reset()
            if last_p:
                kb.dma('sp', P['o_gdn_p'].rearrange("h d e -> d h e"), m['SG'][:], ['SG'], [], sds)
                kb.dma('sp', P['o_ret_p'].rearrange("h d e -> d h e"), m['SR'][:], ['SR'], [], sds)
        if dbg:
            kb.dma('sp', P['dbgT1'], tbl_dram[48:52], [], [], dd)
        kb.barrier()
        import os
        if os.environ.get('KDEBUG'):
            print("sbuf remaining at end", nc.sbuf_bytes_remaining, "arena bytes", A.n32 * 4, "peak", A.peak * 4, "cnt", kb.cnt)
            for nm in ('s5WB0', 's5WB1', 's5WC0', 's5WC1', 's5XC', 'lc4tmp0', 'lc4tmp1', 'ssq', 'rstd', 'arena', 's5R', 'X', 'HT'):
                try:
                    print(nm, nc.lookup_mloc(nm))
                except Exception as ex:
                    print(nm, 'ERR', repr(ex)[:100])
    return nc


def kernel(**inputs):
    consts = make_consts_s5(make_consts())
    nc = build_program(consts)
    f32 = np.float32
    w = {}
    for k in W_NAMES:
        a = np.asarray(inputs[k], dtype=f32)
        if list(a.shape) != W_SHAPES[k]:
            a = a[0]
        assert list(a.shape) == W_SHAPES[k], (k, a.shape)
        w[k] = np.ascontiguousarray(a)
    in_maps = []
    for c in range(8):
        b = c % 4
        sl = slice(16 * c, 16 * c + 16)
        d = dict(w)
        d.update(consts)
        d['x_p'] = np.ascontiguousarray(np.asarray(inputs['x_prompt'], f32)[b])
        d['x_s'] = np.ascontiguousarray(np.asarray(inputs['x_sample'], f32)[sl].reshape(128, D))
        d['state_gdn'] = np.ascontiguousarray(np.asarray(inputs['state_gdn'], f32)[0, sl])
        d['state_gcb'] = np.ascontiguousarray(np.asarray(inputs['state_gdn_conv'], f32)[0, sl])
        d['state_ret'] = np.ascontiguousarray(np.asarray(inputs['state_ret'], f32)[0, sl])
        d['st_re'] = np.ascontiguousarray(np.asarray(inputs['state_s5_re'], f32)[0, sl])
        d['st_im'] = np.ascontiguousarray(np.asarray(inputs['state_s5_im'], f32)[0, sl])
        d['state_fcb'] = np.ascontiguousarray(np.asarray(inputs['state_ffn_conv'], f32)[:, sl])
        in_maps.append(d)
    res = run_bass_kernel_spmd(nc, in_maps, core_ids=list(range(8)))
    r = res.results
    cat = lambda k, axis=0: np.concatenate([np.asarray(r[c][k], f32) for c in range(8)], axis=axis)
    stk = lambda k: np.stack([np.asarray(r[c][k], f32) for c in range(4)], axis=0)
    y_prompt = stk('y_p')
    y_sample = cat('y_s').reshape(128, 8, D)
    gdn_p = stk('o_gdn_p')[None]
    gdn_s = cat('o_gdn_s')[None]
    gcb_p = stk('o_gcb_p')[None]
    gcb_s = cat('o_gcb_s')[None]
    ret_p = stk('o_ret_p')[None]
    ret_s = cat('o_ret_s')[None]
    s5r_p = stk('o_s5r_p')[None]
    s5r_s = cat('o_s5r_s')[None]
    s5i_p = stk('o_s5i_p')[None]
    s5i_s = cat('o_s5i_s')[None]
    fcb_p = np.stack([np.asarray(r[c]['o_fcb_p'], f32) for c in range(4)], axis=1)
    fcb_s = cat('o_fcb_s', axis=1)
    return (y_prompt, y_sample, gdn_p, gdn_s, gcb_p, gcb_s, ret_p, ret_s, s5r_p, s5r_s, s5i_p, s5i_s, fcb_p, fcb_s)
```
